# Optimizing a Trainium2 kernel written in Bass

```python
import math
import jax, jax.numpy as jnp
from jax import lax
import numpy as np

D_MODEL = 1024
BATCH = 4
SEQ = 8192
DEPTH = 2

HEAD_DIM = 64
Q_BLOCK = 128
MIX_W = 512
N_BRANCH = 4
SWA_HEADS = 8
SWA_KV_HEADS = 2
SWA_WINDOW = 128
CONV_K = 3
NSA_HEADS = 8
NSA_KV_HEADS = 2
CMP_BLOCK = 32
CMP_STRIDE = 16
CMP_HIDDEN = 256
SEL_BLOCK = 64
SEL_TOPK = 16
NSA_WINDOW = 512
RET_HEADS = 4
RET_QK_DIM = 64
RET_V_DIM = 128
RET_CHUNK = 128
ROPE_BASE = 10000.0
D_FF = 2816
EPS = 1e-6

SWA_Q = SWA_HEADS * HEAD_DIM
SWA_KV = SWA_KV_HEADS * HEAD_DIM
NSA_Q = NSA_HEADS * HEAD_DIM
NSA_KV = NSA_KV_HEADS * HEAD_DIM
RET_QK = RET_HEADS * RET_QK_DIM
RET_V = RET_HEADS * RET_V_DIM
IN_WIDTHS = (
    SWA_Q, SWA_KV, SWA_KV,
    MIX_W, MIX_W, MIX_W,
    NSA_Q, NSA_KV, NSA_KV, NSA_KV, NSA_KV, NSA_KV, NSA_KV,
    NSA_HEADS * 3,
    RET_QK, RET_QK, RET_V, RET_V,
    N_BRANCH * D_MODEL,
)
IN_TOTAL = sum(IN_WIDTHS)
SPLIT_POINTS = tuple(int(v) for v in np.cumsum(IN_WIDTHS)[:-1])

kernel_name = "hybrid_parallel_gated_swa_conv_nsa_retention"


def rms_norm(x, g):
    xf = x.astype(jnp.float32)
    y = xf * lax.rsqrt(jnp.mean(xf * xf, axis=-1, keepdims=True) + EPS)
    return (y * g.astype(jnp.float32)).astype(x.dtype)


def heads(t, n):
    b, s, _ = t.shape
    return t.reshape(b, s, n, -1).transpose(0, 2, 1, 3)


def merge_heads(t):
    b, n, s, d = t.shape
    return t.transpose(0, 2, 1, 3).reshape(b, s, n * d)


def swiglu(x, w_gate, w_up, w_down):
    return (jax.nn.silu(x @ w_gate) * (x @ w_up)) @ w_down


def masked_softmax(s, mask, sink=None):
    s = jnp.where(mask, s.astype(jnp.float32), -jnp.inf)
    m = jnp.max(s, axis=-1, keepdims=True)
    if sink is not None:
        m = jnp.maximum(m, sink)
    m = jnp.where(jnp.isfinite(m), m, 0.0)
    p = jnp.exp(s - m)
    denom = jnp.sum(p, axis=-1, keepdims=True)
    if sink is not None:
        denom = denom + jnp.exp(sink - m)
    return p / jnp.maximum(denom, 1e-30)


def banded_attention(q, k, v, window, sink=None):
    b, h, t, d = q.shape
    g = k.shape[1]
    r = h // g
    n_blk = t // Q_BLOCK
    pad = window
    span = pad + Q_BLOCK
    kp = jnp.pad(k, ((0, 0), (0, 0), (pad, 0), (0, 0)))
    vp = jnp.pad(v, ((0, 0), (0, 0), (pad, 0), (0, 0)))
    qb = q.reshape(b, g, r, n_blk, Q_BLOCK, d).transpose(3, 0, 1, 2, 4, 5)
    scale = d ** -0.5

    def one_block(args):
        qi, i = args
        start = i * Q_BLOCK
        ki = lax.dynamic_slice_in_dim(kp, start, span, axis=2)
        vi = lax.dynamic_slice_in_dim(vp, start, span, axis=2)
        qpos = start + jnp.arange(Q_BLOCK)
        kpos = start - pad + jnp.arange(span)
        diff = qpos[:, None] - kpos[None, :]
        mask = (kpos[None, :] >= 0) & (diff >= 0) & (diff < window)
        s = jnp.einsum('bgrqd,bgkd->bgrqk', qi, ki) * scale
        p = masked_softmax(s, mask, sink)
        return jnp.einsum('bgrqk,bgkd->bgrqd', p.astype(vi.dtype), vi)

    out = lax.map(one_block, (qb, jnp.arange(n_blk)))
    return out.transpose(1, 2, 3, 0, 4, 5).reshape(b, h, t, d)


def swa_sink_mixer(q, k, v, q_gain, k_gain, sinks):
    q = rms_norm(heads(q, SWA_HEADS), q_gain)
    k = rms_norm(heads(k, SWA_KV_HEADS), k_gain)
    v = heads(v, SWA_KV_HEADS)
    sink = sinks.astype(jnp.float32).reshape(1, SWA_KV_HEADS, SWA_HEADS // SWA_KV_HEADS, 1, 1)
    return merge_heads(banded_attention(q, k, v, SWA_WINDOW, sink))


def short_conv_mixer(x_in, gate_b, gate_c, conv_w):
    z = gate_c * x_in
    y = lax.conv_general_dilated(z, conv_w.astype(z.dtype), window_strides=(1,),
                                 padding=((CONV_K - 1, 0),),
                                 dimension_numbers=('NWC', 'WIO', 'NWC'),
                                 feature_group_count=z.shape[-1])
    return gate_b * y


def nsa_mixer(q, kc, vc, ks, vs, kw, vw, gate_logits, q_gain, k_gain,
              pos_k, pos_v, wk1, wk2, wv1, wv2):
    b, t, _ = q.shape
    g = NSA_KV_HEADS
    r = NSA_HEADS // g
    d = HEAD_DIM
    q = rms_norm(heads(q, NSA_HEADS), q_gain)

    kw = rms_norm(heads(kw, g), k_gain[2])
    o_win = banded_attention(q, kw, heads(vw, g), NSA_WINDOW)

    n_cmp = (t - CMP_BLOCK) // CMP_STRIDE + 1
    cmp_start = jnp.arange(n_cmp) * CMP_STRIDE
    cmp_end = cmp_start + CMP_BLOCK - 1
    idx = cmp_start[:, None] + jnp.arange(CMP_BLOCK)[None, :]

    def compress(tok, pe, w1, w2):
        blk = heads(tok, g)[:, :, idx] + pe
        blk = blk.reshape(b, g, n_cmp, CMP_BLOCK * d)
        return jax.nn.gelu(blk @ w1) @ w2

    k_cmp = rms_norm(compress(kc, pos_k, wk1, wk2), k_gain[0])
    v_cmp = compress(vc, pos_v, wv1, wv2)

    n_sel = t // SEL_BLOCK
    sel_k = min(SEL_TOPK, n_sel)
    ks_blk = rms_norm(heads(ks, g), k_gain[1]).reshape(b, g, n_sel, SEL_BLOCK, d)
    vs_blk = heads(vs, g).reshape(b, g, n_sel, SEL_BLOCK, d)
    sel_start = jnp.arange(n_sel) * SEL_BLOCK
    overlap = ((cmp_start[:, None] < sel_start[None, :] + SEL_BLOCK)
               & (cmp_end[:, None] >= sel_start[None, :])).astype(jnp.float32)
    blk_id = jnp.arange(n_sel)
    bi = jnp.arange(b)[:, None, None, None]
    gi = jnp.arange(g)[None, :, None, None]

    n_blk = t // Q_BLOCK
    qb = q.reshape(b, g, r, n_blk, Q_BLOCK, d).transpose(3, 0, 1, 2, 4, 5)
    scale = d ** -0.5

    def one_block(args):
        qi, i = args
        qpos = i * Q_BLOCK + jnp.arange(Q_BLOCK)
        s = jnp.einsum('bgrqd,bgnd->bgrqn', qi, k_cmp) * scale
        p_cmp = masked_softmax(s, cmp_end[None, :] <= qpos[:, None])
        o_cmp = jnp.einsum('bgrqn,bgnd->bgrqd', p_cmp.astype(qi.dtype), v_cmp)
        imp = jnp.einsum('bgrqn,ns->bgqs', p_cmp, overlap)
        cur = qpos // SEL_BLOCK
        causal = blk_id[None, :] <= cur[:, None]
        forced = ((blk_id[None, :] == 0) | (blk_id[None, :] == cur[:, None])
                  | (blk_id[None, :] == cur[:, None] - 1))
        imp = jnp.where(forced, jnp.inf, imp)
        imp = jnp.where(causal, imp, -jnp.inf)
        top_s, top_i = lax.top_k(imp, sel_k)
        valid = top_s > -jnp.inf
        k_g = ks_blk[bi, gi, top_i]
        v_g = vs_blk[bi, gi, top_i]
        tok_pos = top_i[..., None] * SEL_BLOCK + jnp.arange(SEL_BLOCK)
        mask = valid[..., None] & (tok_pos <= qpos[:, None, None])
        mask = mask.reshape(b, g, 1, Q_BLOCK, sel_k * SEL_BLOCK)
        s = jnp.einsum('bgrqd,bgqkld->bgrqkl', qi, k_g) * scale
        s = s.reshape(b, g, r, Q_BLOCK, sel_k * SEL_BLOCK)
        p = masked_softmax(s, mask)
        v_g = v_g.reshape(b, g, Q_BLOCK, sel_k * SEL_BLOCK, d)
        o_sel = jnp.einsum('bgrqm,bgqmd->bgrqd', p.astype(v_g.dtype), v_g)
        return o_cmp, o_sel

    o_cmp, o_sel = lax.map(one_block, (qb, jnp.arange(n_blk)))
    o_cmp = o_cmp.transpose(1, 2, 3, 0, 4, 5).reshape(b, NSA_HEADS, t, d)
    o_sel = o_sel.transpose(1, 2, 3, 0, 4, 5).reshape(b, NSA_HEADS, t, d)
    gates = jax.nn.sigmoid(gate_logits.astype(jnp.float32)).astype(q.dtype)
    gates = gates.reshape(b, t, NSA_HEADS, 3).transpose(0, 2, 1, 3)
    o = gates[..., 0:1] * o_cmp + gates[..., 1:2] * o_sel + gates[..., 2:3] * o_win
    return merge_heads(o)


def rotary(x, pos):
    half = x.shape[-1] // 2
    inv = ROPE_BASE ** (-jnp.arange(half, dtype=jnp.float32) / half)
    ang = pos.astype(jnp.float32)[:, None] * inv[None, :]
    cos = jnp.cos(ang).astype(x.dtype)
    sin = jnp.sin(ang).astype(x.dtype)
    x1, x2 = x[..., :half], x[..., half:]
    return jnp.concatenate([x1 * cos - x2 * sin, x1 * sin + x2 * cos], axis=-1)


def retention_mixer(q, k, v, gate, norm_gain):
    b, t, _ = q.shape
    h, c = RET_HEADS, RET_CHUNK
    nc = t // c
    dt = q.dtype
    pos = jnp.arange(t)
    q = rotary(heads(q, h), pos)
    k = rotary(heads(k, h), pos) * (RET_QK_DIM ** -0.5)
    v = heads(v, h)
    log_gamma = jnp.log(1.0 - 2.0 ** (-5.0 - jnp.arange(h, dtype=jnp.float32)))
    j = jnp.arange(c, dtype=jnp.float32)
    diff = j[:, None] - j[None, :]
    dmask = jnp.where(diff >= 0, jnp.exp(diff * log_gamma[:, None, None]), 0.0)
    qc = q.reshape(b, h, nc, c, RET_QK_DIM)
    kc = k.reshape(b, h, nc, c, RET_QK_DIM)
    vc = v.reshape(b, h, nc, c, RET_V_DIM)
    att = jnp.einsum('bhncd,bhnmd->bhncm', qc, kc) * dmask[:, None].astype(dt)
    o = jnp.einsum('bhncm,bhnme->bhnce', att, vc)
    zeta = jnp.exp((c - 1 - j) * log_gamma[:, None]).astype(dt)
    s_chunk = jnp.einsum('bhnmd,bhnme->nbhde', kc * zeta[:, None, :, None], vc)
    decay_chunk = jnp.exp(c * log_gamma).astype(dt)[None, :, None, None]

    def step(r_prev, s_i):
        return r_prev * decay_chunk + s_i, r_prev

    _, r_before = lax.scan(step, jnp.zeros_like(s_chunk[0]), s_chunk)
    xi = jnp.exp((j + 1.0) * log_gamma[:, None]).astype(dt)
    o = o + jnp.einsum('bhncd,nbhde->bhnce', qc * xi[:, None, :, None], r_before)
    o = o.reshape(b, h, t, RET_V_DIM).astype(jnp.float32)
    mu = jnp.mean(o, axis=-1, keepdims=True)
    var = jnp.mean(jnp.square(o - mu), axis=-1, keepdims=True)
    o = ((o - mu) * lax.rsqrt(var + EPS)).astype(dt)
    o = merge_heads(o) * norm_gain
    return jax.nn.silu(gate) * o


def setup_inputs(seed: int = 0) -> dict:
    key = jax.random.key(seed)
    ks = iter(jax.random.split(key, 40))
    f32 = jnp.float32

    def nrm(shape, scale):
        return jax.random.normal(next(ks), shape, f32) * scale

    def gain(shape):
        return 1.0 + 0.02 * jax.random.normal(next(ks), shape, f32)

    L, D = DEPTH, D_MODEL
    return {
        "x": jax.random.normal(next(ks), (BATCH, SEQ, D), f32),
        "ffn1_norm": gain((L, D)),
        "ffn1_w_gate": nrm((L, D, D_FF), D ** -0.5),
        "ffn1_w_up": nrm((L, D, D_FF), D ** -0.5),
        "ffn1_w_down": nrm((L, D_FF, D), D_FF ** -0.5),
        "mix_norm": gain((L, D)),
        "w_in": nrm((L, D, IN_TOTAL), D ** -0.5),
        "merge_gate_bias": nrm((L, N_BRANCH * D), 0.01),
        "swa_q_gain": gain((L, HEAD_DIM)),
        "swa_k_gain": gain((L, HEAD_DIM)),
        "swa_sinks": nrm((L, SWA_HEADS), 1.0),
        "conv_w": nrm((L, CONV_K, 1, MIX_W), CONV_K ** -0.5),
        "nsa_q_gain": gain((L, HEAD_DIM)),
        "nsa_k_gain": gain((L, 3, HEAD_DIM)),
        "cmp_pos_k": nrm((L, CMP_BLOCK, HEAD_DIM), 0.02),
        "cmp_pos_v": nrm((L, CMP_BLOCK, HEAD_DIM), 0.02),
        "cmp_wk1": nrm((L, CMP_BLOCK * HEAD_DIM, CMP_HIDDEN), (CMP_BLOCK * HEAD_DIM) ** -0.5),
        "cmp_wk2": nrm((L, CMP_HIDDEN, HEAD_DIM), CMP_HIDDEN ** -0.5),
        "cmp_wv1": nrm((L, CMP_BLOCK * HEAD_DIM, CMP_HIDDEN), (CMP_BLOCK * HEAD_DIM) ** -0.5),
        "cmp_wv2": nrm((L, CMP_HIDDEN, HEAD_DIM), CMP_HIDDEN ** -0.5),
        "ret_norm_gain": gain((L, RET_V)),
        "w_branch": nrm((L, N_BRANCH, MIX_W, D), MIX_W ** -0.5),
        "w_out": nrm((L, D, D), D ** -0.5),
        "ffn2_norm": gain((L, D)),
        "ffn2_w_gate": nrm((L, D, D_FF), D ** -0.5),
        "ffn2_w_up": nrm((L, D, D_FF), D ** -0.5),
        "ffn2_w_down": nrm((L, D_FF, D), D_FF ** -0.5),
    }


def reference(x, ffn1_norm, ffn1_w_gate, ffn1_w_up, ffn1_w_down, mix_norm, w_in,
              merge_gate_bias, swa_q_gain, swa_k_gain, swa_sinks, conv_w, nsa_q_gain,
              nsa_k_gain, cmp_pos_k, cmp_pos_v, cmp_wk1, cmp_wk2, cmp_wv1, cmp_wv2,
              ret_norm_gain, w_branch, w_out, ffn2_norm, ffn2_w_gate, ffn2_w_up,
              ffn2_w_down):
    b, t, _ = x.shape
    for l in range(DEPTH):
        x = x + 0.5 * swiglu(rms_norm(x, ffn1_norm[l]), ffn1_w_gate[l], ffn1_w_up[l], ffn1_w_down[l])
        u = rms_norm(x, mix_norm[l])
        (a_q, a_k, a_v, b_x, b_b, b_c,
         c_q, c_kc, c_vc, c_ks, c_vs, c_kw, c_vw, c_g,
         d_q, d_k, d_v, d_g, gate_logits) = jnp.split(u @ w_in[l], SPLIT_POINTS, axis=-1)
        y_a = swa_sink_mixer(a_q, a_k, a_v, swa_q_gain[l], swa_k_gain[l], swa_sinks[l])
        y_b = short_conv_mixer(b_x, b_b, b_c, conv_w[l])
        y_c = nsa_mixer(c_q, c_kc, c_vc, c_ks, c_vs, c_kw, c_vw, c_g, nsa_q_gain[l],
                        nsa_k_gain[l], cmp_pos_k[l], cmp_pos_v[l], cmp_wk1[l], cmp_wk2[l],
                        cmp_wv1[l], cmp_wv2[l])
        y_d = retention_mixer(d_q, d_k, d_v, d_g, ret_norm_gain[l])
        ys = jnp.stack([y_a, y_b, y_c, y_d], axis=2)
        branch = jnp.einsum('btnw,nwd->btnd', ys, w_branch[l])
        gates = jax.nn.sigmoid((gate_logits + merge_gate_bias[l]).astype(jnp.float32))
        gates = gates.astype(x.dtype).reshape(b, t, N_BRANCH, D_MODEL)
        merged = jnp.sum(gates * branch, axis=2)
        x = x + merged @ w_out[l]
        x = x + 0.5 * swiglu(rms_norm(x, ffn2_norm[l]), ffn2_w_gate[l], ffn2_w_up[l], ffn2_w_down[l])
    return x
```

```python
import contextlib
import math
import numpy as np
import concourse.bass as bass
import concourse.mybir as mybir
from concourse.bass_utils import run_bass_kernel_spmd

F32 = mybir.dt.float32
BF16 = mybir.dt.bfloat16
AF = mybir.ActivationFunctionType
ALU = mybir.AluOpType

D = 1024
DFF = 2816
L_FULL = 2
EPS = 1e-6
NEG = -32768.0
BIG = 1.0e4
IN_TOTAL = 9240
OFF_F1G, OFF_F1U, OFF_WIN = 0, 2816, 5632
OFF_WO = OFF_WIN + IN_TOTAL
OFF_F2G = OFF_WO + 1024
OFF_F2U = OFF_F2G + 2816
WA_COLS = OFF_F2U + 2816
NFM = 33
OFF_CG = NFM * 128
OFF_GATES = OFF_CG + 24
OFF_TM = OFF_GATES + 4096
(C_AQ, C_AK, C_CQ, C_CKS, C_CKW, C_CKC, C_CVC, C_DQ, C_DK, C_DG, C_BX, C_BB, C_BC) = (
    0, 4, 5, 9, 10, 11, 12, 13, 15, 17, 21, 25, 29)
V_N1, V_NM, V_N2, V_GB, V_AQG, V_AKG, V_CQG, V_CKG, V_CW, V_RG = 0, 8, 16, 24, 56, 57, 58, 59, 62, 74
NVEC = 78


class Buf:
    __slots__ = ("name", "last_w", "readers")

    def __init__(self, name=""):
        self.name = name
        self.last_w = None
        self.readers = []


class Ins:
    __slots__ = ("eng", "fn", "deps", "raw", "signal", "rank", "dma_slot", "dma_val", "is_dma", "pre_waits")

    def __init__(self, eng, fn, is_dma):
        self.eng = eng
        self.fn = fn
        self.deps = set()
        self.raw = set()
        self.signal = False
        self.rank = None
        self.is_dma = is_dma
        self.dma_slot = None
        self.dma_val = None
        self.pre_waits = []


ENGS = ("pe", "act", "dve", "pool", "sp")
N_HW_SEM = 24
N_SW_SEM = 8
N_DMA_SEM = N_HW_SEM + N_SW_SEM


class Prog:
    def __init__(self, nc):
        self.nc = nc
        self.ins = []
        self.eng_obj = {"pe": nc.tensor, "act": nc.scalar, "dve": nc.vector,
                        "pool": nc.gpsimd, "sp": nc.sync}
        self.dma_count = 0
        self.hw_count = 0
        self.sw_count = 0
        self.dma_slot_last = [None] * N_DMA_SEM

    def op(self, eng, fn, reads=(), writes=(), dma=False):
        i = Ins(eng, fn, dma)
        iid = len(self.ins)
        for b in reads:
            if b.last_w is not None:
                i.deps.add(b.last_w)
                i.raw.add(b.last_w)
        for b in writes:
            if b.last_w is not None:
                i.deps.add(b.last_w)
            for r in b.readers:
                i.deps.add(r)
        for b in reads:
            b.readers.append(iid)
        for b in writes:
            b.last_w = iid
            b.readers = []
        if dma:
            if eng == "pool":
                k = self.sw_count
                slot = N_HW_SEM + k % N_SW_SEM
                val = 16 * (k // N_SW_SEM + 1)
                self.sw_count += 1
            else:
                k = self.hw_count
                slot = k % N_HW_SEM
                val = 16 * (k // N_HW_SEM + 1)
                self.hw_count += 1
            prev = self.dma_slot_last[slot]
            if prev is not None:
                i.deps.add(prev)
            self.dma_slot_last[slot] = iid
            i.dma_slot = slot
            i.dma_val = val
            self.dma_count += 1
        i.deps.discard(iid)
        self.ins.append(i)
        return iid

    def emit(self, final_wait_eng="sp"):
        nc = self.nc
        ins = self.ins
        waited_eng = {e: {s: -1 for s in ENGS} for e in ENGS}
        waited_dma = {e: {} for e in ENGS}
        for iid, i in enumerate(ins):
            e = i.eng
            need_eng = {}
            need_dma = {}
            for d in i.deps:
                di = ins[d]
                if di.is_dma:
                    if need_dma.get(di.dma_slot, 0) < di.dma_val:
                        need_dma[di.dma_slot] = di.dma_val
                else:
                    if di.eng == e and not i.is_dma:
                        if e == "pe":
                            continue
                    if need_eng.get(di.eng, -1) < d:
                        need_eng[di.eng] = d
            for s, d in need_eng.items():
                if waited_eng[e][s] >= d:
                    continue
                waited_eng[e][s] = d
                ins[d].signal = True
                i.pre_waits.append(("eng", s, d))
            for slot, val in need_dma.items():
                if waited_dma[e].get(slot, 0) >= val:
                    continue
                waited_dma[e][slot] = val
                i.pre_waits.append(("dma", slot, val))
        rk = {e: 0 for e in ENGS}
        counts = {e: 0 for e in ENGS}
        for i in ins:
            counts[i.eng] += 1
            if i.signal and not i.is_dma:
                rk[i.eng] += 1
                i.rank = rk[i.eng]
        self.stats = dict(counts=counts, signals=dict(rk), n=len(ins), dmas=self.dma_count)
        with contextlib.ExitStack() as st:
            esem = {e: st.enter_context(nc.semaphore("s_" + e)) for e in ENGS}
            dsem = [st.enter_context(nc.semaphore("d_%d" % k)) for k in range(N_DMA_SEM)]
            for i in ins:
                eo = self.eng_obj[i.eng]
                for w in i.pre_waits:
                    if w[0] == "eng":
                        eo.wait_ge(esem[w[1]], ins[w[2]].rank)
                    else:
                        eo.wait_ge(dsem[w[1]], w[2])
                r = i.fn(eo)
                if i.is_dma:
                    r.then_inc(dsem[i.dma_slot], 16)
                elif i.signal:
                    r.then_inc(esem[i.eng], 1)
            eo = self.eng_obj[final_wait_eng]
            for slot in range(N_DMA_SEM):
                d = self.dma_slot_last[slot]
                if d is not None:
                    eo.wait_ge(dsem[slot], ins[d].dma_val)


def make_consts(T):
    NT = T // 128
    c = {}
    c["ident"] = np.eye(128, dtype=np.float32)
    bd = np.zeros((128, 128), np.float32)
    bd[:64, :64] = 1.0 / 64
    bd[64:, 64:] = 1.0 / 64
    c["ones_bd"] = bd
    c["ones_d"] = np.full((128, 128), 1.0 / 1024, np.float32)
    c["ones_v"] = np.full((128, 128), 1.0 / 128, np.float32)
    E = np.zeros((128, 32, 128), np.float32)
    for q in range(32):
        for m in range(128):
            s = 2 * q + m // 64
            E[s, q, m] = 1.0
            E[64 + s, q, m] = 1.0
    c["E"] = E.reshape(128, 32 * 128)
    n = np.arange(512)
    s = np.arange(128)
    cs = n * 16
    ce = cs + 31
    ss = s * 64
    ov = ((cs[:, None] < ss[None, :] + 64) & (ce[:, None] >= ss[None, :])).astype(np.float32)
    c["ovl"] = ov.reshape(4, 128, 128).transpose(1, 0, 2).reshape(128, 4 * 128)
    j = np.arange(128)
    e = np.arange(256)
    rel = (e[None, :] - 128) - (j[:, None] // 64)
    fb = np.zeros((128, 256), np.float32)
    fb[(rel == 0) | (rel == -1)] = BIG
    fb[rel > 0] = -BIG
    c["fb"] = fb
    f0 = np.zeros((128, 128), np.float32)
    f0[:, 0] = BIG
    c["f0"] = f0
    m = np.arange(128)
    c["mask_cur"] = np.where(m[:, None] <= j[None, :], 0.0, NEG).astype(np.float32)
    c["mask_prev"] = np.where(m[:, None] > j[None, :], 0.0, NEG).astype(np.float32)
    cm = np.zeros((128, 17, 128), np.float32)
    for dl in range(17):
        ok = (16 * m[:, None] + 31 - j[None, :]) <= 128 * dl
        cm[:, dl, :] = np.where(ok, 0.0, NEG)
    c["cm"] = cm.reshape(128, 17 * 128)
    sg = np.zeros((24, 12, 128), np.float32)
    for br in range(3):
        for r in range(4):
            for mm in range(128):
                h = 4 * (mm // 64) + r
                sg[h * 3 + br, br * 4 + r, mm] = 1.0
    c["selg"] = sg.reshape(24, 12 * 128)
    gam = 1.0 - 2.0 ** (-5.0 - np.arange(4, dtype=np.float64))
    lg = np.log(gam)
    diff = j[None, :].astype(np.float64) - m[:, None]
    dm = np.zeros((128, 4, 128), np.float64)
    for h in range(4):
        dm[:, h, :] = np.where(diff >= 0, np.exp(diff * lg[h]), 0.0) * 0.125
    c["dmask"] = dm.reshape(128, 512).astype(np.float32)
    xi = np.zeros((128, 2, 128), np.float64)
    zt = np.zeros((128, 2, 128), np.float64)
    dec = np.zeros((128, 2), np.float64)
    for pp in range(2):
        for hh in range(2):
            h = 2 * pp + hh
            xi[64 * hh:64 * hh + 64, pp, :] = np.exp((j[None, :] + 1.0) * lg[h])
            zt[:, pp, 64 * hh:64 * hh + 64] = (np.exp((127.0 - m) * lg[h]) * 0.125)[:, None]
            dec[64 * hh:64 * hh + 64, pp] = np.exp(128.0 * lg[h])
    c["xi"] = xi.reshape(128, 256).astype(np.float32)
    c["zt"] = zt.reshape(128, 256).astype(np.float32)
    c["dec"] = dec.astype(np.float32)
    rm = np.zeros((128, 128), np.float32)
    for mm in range(128):
        if mm % 64 < 32:
            rm[mm + 32, mm] = -1.0
        else:
            rm[mm - 32, mm] = 1.0
    c["rm"] = rm
    half = 32
    inv = 10000.0 ** (-np.arange(half, dtype=np.float32) / half)
    p = np.arange(128)
    ang = np.arange(T, dtype=np.float32)[None, :] * inv[p % 32][:, None]
    c["cosT"] = np.cos(ang).astype(np.float32)
    c["sinT"] = np.sin(ang).astype(np.float32)
    return c


CONST_SHAPES = lambda T: {k: v.shape for k, v in make_consts(128 if False else T).items()}


def win_perm():
    aq0, ak0, av0, bx0, bb0, bc0, cq0, ckc0, cvc0, cks0, cvs0, ckw0, cvw0, cg0, dq0, dk0, dv0, dg0, gl0 = (
        0, 512, 640, 768, 1280, 1792, 2304, 2816, 2944, 3072, 3200, 3328, 3456, 3584, 3608, 3864, 4120, 4632, 5144)
    cols = []
    r64 = np.arange(64)
    for r in range(4):
        cols += list(aq0 + (0 * 4 + r) * 64 + r64) + list(aq0 + (4 + r) * 64 + r64)
    cols += list(ak0 + np.arange(128))
    for r in range(4):
        cols += list(cq0 + (0 * 4 + r) * 64 + r64) + list(cq0 + (4 + r) * 64 + r64)
    cols += list(cks0 + np.arange(128)) + list(ckw0 + np.arange(128))
    cols += list(ckc0 + np.arange(128)) + list(cvc0 + np.arange(128))
    cols += list(dq0 + np.arange(256)) + list(dk0 + np.arange(256)) + list(dg0 + np.arange(512))
    cols += list(bx0 + np.arange(512)) + list(bb0 + np.arange(512)) + list(bc0 + np.arange(512))
    cols += list(cg0 + np.arange(24))
    cols += list(gl0 + np.arange(4096))
    cols += list(av0 + np.arange(128)) + list(cvs0 + np.arange(128)) + list(cvw0 + np.arange(128))
    cols += list(dv0 + np.arange(512))
    assert len(cols) == IN_TOTAL and len(set(cols)) == IN_TOTAL
    return np.array(cols)


def prep_weights(inp, L):
    perm = win_perm()
    WA = np.concatenate([inp["ffn1_w_gate"][:L], inp["ffn1_w_up"][:L], inp["w_in"][:L][:, :, perm],
                         inp["w_out"][:L], inp["ffn2_w_gate"][:L], inp["ffn2_w_up"][:L]], axis=2)
    WD = np.concatenate([inp["ffn1_w_down"][:L], inp["ffn2_w_down"][:L]], axis=2)
    WB = inp["w_branch"][:L]
    att_rows = []
    r64 = np.arange(64)
    for r in range(4):
        att_rows += list((0 * 4 + r) * 64 + r64) + list((4 + r) * 64 + r64)
    att_rows = np.array(att_rows)
    WB = WB.copy()
    WB[:, 0] = WB[:, 0][:, att_rows, :]
    WB[:, 2] = WB[:, 2][:, att_rows, :]
    WB = WB.reshape(L, 2048, 1024)

    def w1l(w):
        a = w[:L].reshape(L, 32, 64, 256).transpose(0, 2, 1, 3).reshape(L, 64, 32 * 256)
        return np.concatenate([a, a], axis=1)
    W1 = np.stack([w1l(inp["cmp_wk1"]), w1l(inp["cmp_wv1"])], axis=1)
    w2k = inp["cmp_wk2"][:L]
    z = np.zeros_like(w2k)
    W2K = np.stack([np.concatenate([w2k, z], axis=2), np.concatenate([z, w2k], axis=2)], axis=1)
    W2V = inp["cmp_wv2"][:L]
    pek = np.stack([inp["cmp_pos_k"][:L], inp["cmp_pos_v"][:L]], axis=1)
    peT = pek.transpose(0, 1, 3, 2)
    peT = np.concatenate([peT, peT], axis=2)
    vec = np.zeros((L, 128, NVEC), np.float32)

    def fm(v, n):
        return v.reshape(L, n, 128).transpose(0, 2, 1)
    vec[:, :, V_N1:V_N1 + 8] = fm(inp["ffn1_norm"][:L], 8)
    vec[:, :, V_NM:V_NM + 8] = fm(inp["mix_norm"][:L], 8)
    vec[:, :, V_N2:V_N2 + 8] = fm(inp["ffn2_norm"][:L], 8)
    vec[:, :, V_GB:V_GB + 32] = fm(inp["merge_gate_bias"][:L], 32)
    vec[:, :, V_AQG] = np.tile(inp["swa_q_gain"][:L], (1, 2))
    vec[:, :, V_AKG] = np.tile(inp["swa_k_gain"][:L], (1, 2))
    vec[:, :, V_CQG] = np.tile(inp["nsa_q_gain"][:L], (1, 2))
    for k in range(3):
        vec[:, :, V_CKG + k] = np.tile(inp["nsa_k_gain"][:L, k], (1, 2))
    cw = inp["conv_w"][:L].reshape(L, 3, 512)
    for k in range(3):
        vec[:, :, V_CW + 4 * k:V_CW + 4 * k + 4] = fm(cw[:, k], 4)
    vec[:, :, V_RG:V_RG + 4] = fm(inp["ret_norm_gain"][:L], 4)
    sk = inp["swa_sinks"][:L].reshape(L, 2, 4)
    sinks = np.repeat(sk, 64, axis=1)
    f = lambda a: np.ascontiguousarray(a, dtype=np.float32)
    return dict(WA=f(WA), WD=f(WD), WB=f(WB), W1=f(W1), W2K=f(W2K), W2V=f(W2V), peT=f(peT), vec=f(vec), sinks=f(sinks))


def build(T, L, TB=256, stop=None):
    nc = bass.Bass("TRN2", target_bir_lowering=False)
    P = Prog(nc)
    NT = T // 128
    NB = T // TB
    TPB = TB // 128
    NCT = max(1, T // 2048)
    RING = 8
    consts_np_shapes = {k: v.shape for k, v in make_consts(T).items()}

    def din(name, shape):
        return nc.dram_tensor(name, list(shape), F32, kind="ExternalInput").ap()

    xT_in = din("xT", (1024, T))
    WA = din("WA", (L, 1024, WA_COLS))
    WD = din("WD", (L, 2816, 2048))
    WB = din("WB", (L, 2048, 1024))
    W1 = din("W1", (L, 2, 128, 8192))
    W2K = din("W2K", (L, 2, 256, 128))
    W2V = din("W2V", (L, 256, 64))
    peT = din("peT", (L, 2, 128, 32))
    vec_in = din("vec", (L, 128, NVEC))
    sinks_in = din("sinks", (L, 128, 4))
    cin = {k: din("c_" + k, s) for k, s in consts_np_shapes.items()}
    outT = nc.dram_tensor("outT", [1024, T], F32, kind="ExternalOutput").ap()

    WA_bf = [nc.dram_tensor("WAbf%d" % l, [128, 8, WA_COLS], BF16).ap() for l in range(L)]
    WD_bf = [nc.dram_tensor("WDbf%d" % l, [128, 22, 2048], BF16).ap() for l in range(L)]
    WB_bf = [nc.dram_tensor("WBbf%d" % l, [128, 16, 1024], BF16).ap() for l in range(L)]
    W1_bf = [nc.dram_tensor("W1bf%d" % l, [2, 128, 8192], BF16).ap() for l in range(L)]
    xs = [nc.dram_tensor("xs%d" % l, [1024, T], F32).ap() for l in range(max(L - 1, 1))]
    B_wscr = [Buf("wscr%d" % l) for l in range(L)]
    B_xs = [[Buf() for _ in range(NB)] for _ in range(max(L - 1, 1))]

    def sb(name, shape, dt=F32):
        return nc.alloc_sbuf_tensor("s_" + name, list(shape), dt)

    def mm(out, lhsT, rhs, start, stop_, reads, writes):
        P.op("pe", lambda e: e.matmul(out, lhsT, rhs, start=start, stop=stop_), reads, writes)

    def tr(out, in_, ident, reads, writes):
        P.op("pe", lambda e: e.matmul(out, in_, ident, start=True, stop=True), reads, writes)

    def act(out, in_, func, reads, writes, bias=None, scale=None):
        kw = {}
        if bias is not None:
            kw["bias"] = bias
        if scale is not None:
            kw["scale"] = scale
        P.op("act", lambda e: e.activation(out=out, in_=in_, func=func, **kw), reads, writes)

    def cp(eng, out, in_, reads, writes):
        if eng == "act":
            P.op("act", lambda e: e.activation(out=out, in_=in_, func=AF.Copy), reads, writes)
        else:
            P.op(eng, lambda e: e.tensor_copy(out=out, in_=in_), reads, writes)

    def tt(eng, out, in0, in1, op, reads, writes):
        P.op(eng, lambda e: e.tensor_tensor(out=out, in0=in0, in1=in1, op=op), reads, writes)

    def ts(eng, out, in0, s1, s2, op0, op1, reads, writes):
        if op1 is None:
            P.op(eng, lambda e: e.tensor_scalar(out=out, in0=in0, scalar1=s1, scalar2=None, op0=op0), reads, writes)
        else:
            P.op(eng, lambda e: e.tensor_scalar(out=out, in0=in0, scalar1=s1, scalar2=s2, op0=op0, op1=op1), reads, writes)

    def stt(eng, out, in0, scalar, in1, op0, op1, reads, writes):
        P.op(eng, lambda e: e.scalar_tensor_tensor(out=out, in0=in0, scalar=scalar, in1=in1, op0=op0, op1=op1),
             reads, writes)

    def memset(eng, ap, val, writes):
        P.op(eng, lambda e: e.memset(ap, val), (), writes)

    def rsqrt_eps(out, in_, reads, writes):
        P.op("act", lambda e: e.activation(out=out, in_=in_, func=AF.Sqrt, bias=eps_col[0:out.shape[0], :], scale=1.0), list(reads) + [B_epscol], writes)
        P.op("dve", lambda e: e.reciprocal(out=out, in_=out), writes, writes)

    def recip_add(out, in_, addend, reads, writes):
        P.op("dve", lambda e: e.tensor_scalar(out=out, in0=in_, scalar1=addend, scalar2=None, op0=ALU.add), reads, writes)
        P.op("dve", lambda e: e.reciprocal(out=out, in_=out), writes, writes)

    def dma(q, out, in_, reads, writes):
        P.op(q, lambda e: e.dma_start(out=out, in_=in_), reads, writes, dma=True)

    def load_const(name, dt, q="pool", src=None, shape=None):
        src = cin[name] if src is None else src
        shape = consts_np_shapes[name] if shape is None else shape
        t = sb("k_" + name, shape, dt)
        b = Buf(name)
        dma("pool" if dt == BF16 else "sp", t[:], src, (), [b])
        return t, b

    ident_bf, B_ident = load_const("ident", BF16)
    ones_bd, B_obd = load_const("ones_bd", F32)
    ones_d, B_od = load_const("ones_d", F32)
    ones_v, B_ov = load_const("ones_v", F32)
    E_sb, B_E = load_const("E", BF16)
    ovl_sb, B_ovl = load_const("ovl", BF16)
    fb_sb, B_fb = load_const("fb", F32)
    f0_sb, B_f0 = load_const("f0", F32)
    mcur_sb, B_mcur = load_const("mask_cur", BF16)
    mprev_sb, B_mprev = load_const("mask_prev", BF16)
    cm_sb, B_cm = load_const("cm", BF16)
    selg_sb, B_selg = load_const("selg", BF16)
    dmask_sb, B_dmask = load_const("dmask", F32)
    xi_sb, B_xi = load_const("xi", F32)
    zt_sb, B_zt = load_const("zt", F32)
    dec_sb, B_dec = load_const("dec", F32)
    rm_sb, B_rm = load_const("rm", F32)
    eps_col = sb("eps_col", [128, 1], F32)
    B_epscol = Buf()
    memset("dve", eps_col[:], EPS, [B_epscol])
    ones_col = sb("ones_col", [128, 1], BF16)
    B_onescol = Buf()
    memset("dve", ones_col[:], 1.0, [B_onescol])
    CONSTB = [B_ident, B_obd, B_od, B_ov, B_E, B_ovl, B_fb, B_f0, B_mcur, B_mprev, B_cm, B_selg, B_dmask, B_xi,
              B_zt, B_dec, B_rm, B_onescol]

    pb = [nc.alloc_psum_tensor("pb%d" % k, [128, 512], F32) for k in range(7)]
    B_pb = [Buf("pb%d" % k) for k in range(7)]
    pbt = nc.alloc_psum_tensor("pbt", [128, 512], F32)
    B_pbt = Buf("pbt")
    mm_rot = [0]

    def mmbank():
        k = mm_rot[0] % 2
        mm_rot[0] += 1
        return pb[k], B_pb[k]

    xT = sb("xT", [128, 8, TB], F32); B_x = [Buf() for _ in range(8)]
    xn = sb("xn", [128, 8, TB], BF16); B_xn = [Buf() for _ in range(8)]
    hT = sb("hT", [128, 22, TB], BF16); B_h = [Buf() for _ in range(22)]
    yT = sb("yT", [128, 4, 4, TB], BF16); B_y = [[Buf() for _ in range(4)] for _ in range(4)]
    mrg = sb("mrg", [128, 8, TB], BF16); B_mrg = [Buf() for _ in range(8)]
    aqT = sb("aqT", [128, 4, TB], BF16); B_aq = Buf()
    cqT = sb("cqT", [128, 4, TB], BF16); B_cq = Buf()
    dqT = sb("dqT", [128, 2, TB], BF16); B_dq = Buf()
    dkT = sb("dkT", [128, 2, TB], BF16); B_dk = Buf()
    qxi = sb("qxi", [128, 2, TB], BF16); B_qxi = Buf()
    sgd = sb("sgd", [128, 4, TB], F32); B_sgd = Buf()
    cgs = sb("cgs", [24, TB], BF16); B_cgs = Buf()
    cosb = sb("cosb", [128, TB], F32); sinb = sb("sinb", [128, TB], F32); B_cs = Buf()
    vec = sb("vec", [128, NVEC], F32); B_vec = Buf()
    esink = sb("esink", [128, 4], F32); B_esink = Buf()
    ksT = sb("ksT", [128, T], BF16); B_ks = [Buf() for _ in range(NT)]
    vsA = sb("vsA", [128, NT, 192], BF16); B_vs = [Buf() for _ in range(NT)]
    akR = sb("akR", [128, RING, 128], BF16); B_akR = [Buf() for _ in range(RING)]
    avR = sb("avR", [128, RING, 192], BF16); B_avR = [Buf() for _ in range(RING)]
    kwR = sb("kwR", [128, RING, 128], BF16); B_kwR = [Buf() for _ in range(RING)]
    vwR = sb("vwR", [128, RING, 192], BF16); B_vwR = [Buf() for _ in range(RING)]
    dvR = sb("dvR", [128, TPB, 512], BF16); B_dvR = [Buf() for _ in range(TPB)]
    kcmpT = sb("kcmpT", [128, NCT * 128], BF16); B_kcmp = Buf()
    vcmpA = sb("vcmpA", [128, NCT, 192], BF16); B_vcmp = Buf()
    kcC = sb("kcC", [128, 16 + TB], BF16); vcC = sb("vcC", [128, 16 + TB], BF16); B_kcC = Buf(); B_vcC = Buf()
    zC = sb("zC", [128, 4, 2 + TB], F32); B_z = [Buf() for _ in range(4)]
    Rst = sb("Rst", [128, 2, 256], F32); Rbf = sb("Rbf", [128, 2, 256], BF16); B_R = Buf(); B_Rbf = Buf()
    w2k_sb = sb("w2k", [128, 2, 2, 128], BF16); w2v_sb = sb("w2v", [128, 2, 64], BF16); B_w2 = Buf()
    peT_sb = sb("peT", [128, 2, 32], BF16); B_pe = Buf()
    cbias = sb("cbias", [128, 2, 2], F32); B_cbias = Buf()
    NTMP = 5
    tmpf = [sb("tmpf%d" % k, [128, 512], F32) for k in range(NTMP)]; B_tmpf = [Buf() for _ in range(NTMP)]
    tf_rot = [0]

    def tmp():
        k = tf_rot[0] % NTMP
        tf_rot[0] += 1
        return tmpf[k], B_tmpf[k]
    ptb = [sb("ptb%d" % k, [128, 512], BF16) for k in range(6)]; B_ptb = [Buf() for _ in range(6)]
    pt_rot = [0]

    def ptmp():
        k = pt_rot[0] % 6
        pt_rot[0] += 1
        return ptb[k], B_ptb[k]
    obr = [sb("obr%d" % k, [128, 512], F32) for k in range(3)]; B_obr = [Buf() for _ in range(3)]
    selbT = sb("selbT", [128, 2, 2, 128], BF16); B_selbT = [Buf(), Buf()]
    memset("dve", selbT[:].rearrange("p a b c -> p (a b c)"), 0.0, B_selbT)
    impb = sb("impb", [128, 128], F32); impw = sb("impw", [128, 128], F32); m8 = sb("m8", [128, 16], F32)
    selb = sb("selb", [128, 128], BF16); rdt = sb("rdt", [128, 4], F32)
    B_imp = Buf(); B_impw = Buf(); B_m8 = Buf(); B_selb = Buf(); B_rdt = Buf()
    kz = sb("kz", [128, 2, 128], BF16); B_kz = Buf()
    WSLOT = 4096
    NW = 3
    wring = [sb("wr%d" % k, [128, WSLOT], BF16) for k in range(NW)]; B_wr = [Buf() for _ in range(NW)]
    w_rot = [0]

    def wload(src_ap, n_in, n_col):
        k = w_rot[0] % NW
        w_rot[0] += 1
        assert n_in * n_col <= WSLOT
        v = wring[k][:, 0:n_in * n_col].rearrange("p (a b) -> p a b", a=n_in)
        dma("sp", v, src_ap, [B_wscr_cur[0]], [B_wr[k]])
        return v, B_wr[k]

    B_wscr_cur = [None]

    def convert_weights(l):
        b = B_wscr[l]
        for kc in range(8):
            dma("pool", WA_bf[l][:, kc, :], WA[l, kc * 128:(kc + 1) * 128, :], (), [b])
        for kc in range(22):
            dma("pool", WD_bf[l][:, kc, :], WD[l, kc * 128:(kc + 1) * 128, :], (), [b])
        for kc in range(16):
            dma("pool", WB_bf[l][:, kc, :], WB[l, kc * 128:(kc + 1) * 128, :], (), [b])
        for kv in range(2):
            dma("pool", W1_bf[l][kv], W1[l, kv], (), [b])

    def rmsnorm_block(vcol):
        ps, Bp = pb[2], B_pb[2]
        for c in range(8):
            t, Bt = tmp()
            tt("pool", t[:, 0:TB], xT[:, c, :], xT[:, c, :], ALU.mult, [B_x[c]], [Bt])
            mm(ps[:, 0:TB], ones_d[:], t[:, 0:TB], c == 0, c == 7, [Bt, B_od], [Bp])
        r, Br = tmp()
        rsqrt_eps(r[:, 0:TB], ps[:, 0:TB], [Bp], [Br])
        for c in range(8):
            stt("dve", xn[:, c, :], xT[:, c, :], vec[:, vcol + c:vcol + c + 1], r[:, 0:TB], ALU.mult, ALU.mult,
                [B_x[c], B_vec, Br], [B_xn[c]])

    def ffn_block(l, og, ou, od):
        for jg in range(0, 22, 4):
            nj = min(4, 22 - jg)
            wg, Bwg = wload(WA_bf[l][:, :, og + jg * 128: og + (jg + nj) * 128], 8, nj * 128)
            wu, Bwu = wload(WA_bf[l][:, :, ou + jg * 128: ou + (jg + nj) * 128], 8, nj * 128)
            for jj in range(nj):
                j = jg + jj
                pg, Bg = pb[0], B_pb[0]
                pu, Bu = pb[1], B_pb[1]
                for kc in range(8):
                    mm(pg[:, 0:TB], wg[:, kc, jj * 128:(jj + 1) * 128], xn[:, kc, :], kc == 0, kc == 7,
                       [Bwg, B_xn[kc]], [Bg])
                for kc in range(8):
                    mm(pu[:, 0:TB], wu[:, kc, jj * 128:(jj + 1) * 128], xn[:, kc, :], kc == 0, kc == 7,
                       [Bwu, B_xn[kc]], [Bu])
                s, Bs = tmp()
                act(s[:, 0:TB], pg[:, 0:TB], AF.Silu, [Bg], [Bs])
                tt("dve", hT[:, j, :], s[:, 0:TB], pu[:, 0:TB], ALU.mult, [Bs, Bu], [B_h[j]])
        for mg in range(0, 8, 1):
            wd, Bwd = wload(WD_bf[l][:, :, od + mg * 128: od + (mg + 1) * 128], 22, 128)
            for m2 in range(1):
                m = mg + m2
                po, Bo = mmbank()
                for j in range(22):
                    mm(po[:, 0:TB], wd[:, j, m2 * 128:(m2 + 1) * 128], hT[:, j, :], j == 0, j == 21,
                       [Bwd, B_h[j]], [Bo])
                stt("dve", xT[:, m, :], po[:, 0:TB], 0.5, xT[:, m, :], ALU.mult, ALU.add, [Bo, B_x[m]], [B_x[m]])

    def headnorm(ps, Bp, gcol, dest, Bdest):
        q, Bq = tmp()
        cp("act", q[:, 0:TB], ps[:, 0:TB], [Bp], [Bq])
        s, Bs = tmp()
        tt("pool", s[:, 0:TB], q[:, 0:TB], q[:, 0:TB], ALU.mult, [Bq], [Bs])
        p2, Bp2 = pb[2], B_pb[2]
        mm(p2[:, 0:TB], ones_bd[:], s[:, 0:TB], True, True, [Bs, B_obd], [Bp2])
        r, Br = tmp()
        rsqrt_eps(r[:, 0:TB], p2[:, 0:TB], [Bp2], [Br])
        stt("dve", dest, q[:, 0:TB], vec[:, gcol:gcol + 1], r[:, 0:TB], ALU.mult, ALU.mult, [Bq, B_vec, Br], Bdest)

    def rotary(ps, Bp, dest, Bdest):
        q, Bq = tmp()
        cp("act", q[:, 0:TB], ps[:, 0:TB], [Bp], [Bq])
        p2, Bp2 = pb[2], B_pb[2]
        mm(p2[:, 0:TB], rm_sb[:], q[:, 0:TB], True, True, [Bq, B_rm], [Bp2])
        a, Ba = tmp()
        tt("pool", a[:, 0:TB], q[:, 0:TB], cosb[:], ALU.mult, [Bq, B_cs], [Ba])
        b_, Bb = tmp()
        tt("dve", b_[:, 0:TB], p2[:, 0:TB], sinb[:], ALU.mult, [Bp2, B_cs], [Bb])
        tt("dve", dest, a[:, 0:TB], b_[:, 0:TB], ALU.add, [Ba, Bb], Bdest)

    def attend(g, qbuf, Bq, tcol, ktiles, onorm_dest_idx, sink=False):
        rows = slice(64 * g, 64 * g + 64)
        drows = slice(64 * (1 - g), 64 * (1 - g) + 64)
        po, Bo = pb[6], B_pb[6]
        qv = qbuf[rows, :, tcol:tcol + 128]
        n = len(ktiles)
        pend = None
        for idx, (kl, Bk, va, Bv, masks) in enumerate(ktiles):
            sp_, Bs = pb[4 + idx % 2], B_pb[4 + idx % 2]
            mm(sp_[:].rearrange("p (a b) -> p a b", a=4), kl, qv, True, len(masks) == 0, [Bk, Bq], [Bs])
            for mi, (ml, mr, mb) in enumerate(masks):
                mm(sp_[:].rearrange("p (a b) -> p a b", a=4), ml, mr, False, mi == len(masks) - 1, mb, [Bs])
            pt, Bpt = ptmp()
            act(pt[:], sp_[:], AF.Exp, [Bs], [Bpt], scale=0.125)
            if pend is not None:
                pidx, ppt, pBpt, pva, pBv = pend
                mm(po[:], pva, ppt[:], pidx == 0, False, [pBv, pBpt], [Bo])
                yield ppt, pBpt
            pend = (idx, pt, Bpt, va, Bv)
        pidx, ppt, pBpt, pva, pBv = pend
        mm(po[:], pva, ppt[:], pidx == 0, True, [pBv, pBpt], [Bo])
        yield ppt, pBpt
        rd, Brd = tmp()
        if sink:
            for r in range(4):
                recip_add(rd[rows, r * 128:(r + 1) * 128], po[drows, r * 128:(r + 1) * 128],
                          esink[rows, r:r + 1], [Bo, B_esink], [Brd])
        else:
            recip_add(rd[rows, :], po[drows, :], 1e-30, [Bo], [Brd])
        tt("dve", obr[onorm_dest_idx][rows, :], po[rows, :], rd[rows, :], ALU.mult, [Bo, Brd], [B_obr[onorm_dest_idx]])

    def bc4(ap2d):
        return ap2d.unsqueeze(1).to_broadcast([ap2d.shape[0], 4, 128])

    bxs = sb("bxs", [128, 4, TB], F32); B_bx = [Buf() for _ in range(4)]
    bbs = sb("bbs", [128, 4, TB], F32); B_bb = [Buf() for _ in range(4)]
    hc = sb("hc", [128, 128], F32); B_hc = Buf()
    hc2 = sb("hc2", [128, 128], F32); B_hc2 = Buf()
    gk = sb("gk", [128, 2, 128], BF16); B_gk = Buf()
    gpad = sb("gpad", [128, 2, 2, 2, 128], BF16); B_gpad = Buf()
    osb = sb("osb", [128, 512], F32); B_osb = Buf()
    macc = sb("macc", [128, 4, TB], F32); B_macc = [Buf() for _ in range(4)]
    cen = sb("cen", [128, 512], F32); B_cen = Buf()

    def headnorm_w(ps_ap, Bp, gcol, dest, Bdest, W):
        q, Bq = tmp()
        cp("act", q[:, 0:W], ps_ap, [Bp], [Bq])
        s_, Bs = tmp()
        tt("pool", s_[:, 0:W], q[:, 0:W], q[:, 0:W], ALU.mult, [Bq], [Bs])
        p2, Bp2 = pb[2], B_pb[2]
        mm(p2[:, 0:W], ones_bd[:], s_[:, 0:W], True, True, [Bs, B_obd], [Bp2])
        r, Br = tmp()
        rsqrt_eps(r[:, 0:W], p2[:, 0:W], [Bp2], [Br])
        stt("dve", dest, q[:, 0:W], vec[:, gcol:gcol + 1], r[:, 0:W], ALU.mult, ALU.mult, [Bq, B_vec, Br], Bdest)

    memset("dve", hc[:], 0.0, [B_hc])

    import os as _os2
    _katt = _os2.environ.get("KATT", "ABCDEF")

    _kret = int(_os2.environ.get("KRET", "9"))

    def _rl(n):
        return n <= _kret

    def _en(t):
        return t in _katt

    def run_att(gen):
        for _ in gen:
            pass

    for l in range(L):
        convert_weights(l)
    for l in range(L):
        B_wscr_cur[0] = B_wscr[l]
        src = xT_in if l == 0 else xs[l - 1]
        dst = outT if l == L - 1 else xs[l]
        srcv = src.rearrange("(c p) t -> p c t", p=128)
        dstv = dst.rearrange("(c p) t -> p c t", p=128)
        dma("sp", vec[:], vec_in[l], (), [B_vec])
        sk, Bsk = tmp()
        dma("sp", sk[:, 0:4], sinks_in[l], (), [Bsk])
        act(esink[:], sk[:, 0:4], AF.Exp, [Bsk], [B_esink])
        dma("pool", w2k_sb[:].rearrange("p g c m -> p (g c) m"),
            W2K[l].rearrange("g (c p) m -> p (g c) m", p=128), (), [B_w2])
        dma("pool", w2v_sb[:], W2V[l].rearrange("(c p) m -> p c m", p=128), (), [B_w2])
        dma("pool", peT_sb[:], peT[l].rearrange("k p n -> p k n"), (), [B_pe])
        memset("pool", kcmpT[:], 0.0, [B_kcmp])
        memset("pool", vcmpA[:], 0.0, [B_vcmp])
        memset("pool", vcmpA[:, :, 64:128], 1.0, [B_vcmp])
        memset("pool", kcC[:], 0.0, [B_kcC])
        memset("pool", vcC[:], 0.0, [B_vcC])
        memset("pool", zC[:], 0.0, B_z)
        memset("pool", Rst[:], 0.0, [B_R])
        memset("pool", Rbf[:], 0.0, [B_Rbf])
        memset("pool", vsA[:, :, 64:128], 1.0, B_vs)
        memset("pool", avR[:, :, 64:128], 1.0, B_avR)
        memset("pool", vwR[:, :, 64:128], 1.0, B_vwR)
        for kv in range(2):
            for ch in range(2):
                for half in range(2):
                    w1, Bw1 = wload(W1_bf[l][kv][:, half * 4096:(half + 1) * 4096]
                                    .rearrange("p (a b) -> p a b", a=16)[:, :, ch * 128:(ch + 1) * 128], 16, 128)
                    for li in range(16):
                        lg_ = half * 16 + li
                        col = kv * 2 + ch
                        mm(pb[3][:, col:col + 1], w1[0:64, li, :], peT_sb[0:64, kv, lg_:lg_ + 1],
                           lg_ == 0, lg_ == 31, [Bw1, B_pe], [B_pb[3]])
        cp("dve", cbias[:].rearrange("p a b -> p (a b)"), pb[3][:, 0:4], [B_pb[3]], [B_cbias])

        for blk in range(NB):
            t0 = blk * TB
            dma("sp", xT[:], srcv[:, :, t0:t0 + TB], [B_xs[l - 1][blk]] if l > 0 else [], B_x)
            dma("sp", cosb[:], cin["cosT"][:, t0:t0 + TB], (), [B_cs])
            dma("sp", sinb[:], cin["sinT"][:, t0:t0 + TB], (), [B_cs])
            rmsnorm_block(V_N1)
            ffn_block(l, OFF_F1G, OFF_F1U, 0)
            if stop == "ffn1":
                dma("sp", dstv[:, :, t0:t0 + TB], xT[:], B_x, [B_xs[min(l, len(B_xs) - 1)][blk]])
                continue
            rmsnorm_block(V_NM)

            def wgroup(c0, ncols):
                return wload(WA_bf[l][:, :, OFF_WIN + c0: OFF_WIN + c0 + ncols], 8, ncols)

            def proj(w, Bw, off, M=128):
                ps, Bp = mmbank()
                for kc in range(8):
                    mm(ps[0:M, 0:TB], w[:, kc, off:off + M], xn[:, kc, :], kc == 0, kc == 7, [Bw, B_xn[kc]], [Bp])
                return ps, Bp
            slots = [(TPB * blk + ti) % RING for ti in range(TPB)]
            handlers = []
            for r in range(4):
                handlers.append(lambda ps, Bp, r=r: headnorm_w(ps[:, 0:TB], Bp, V_AQG, aqT[:, r, :], [B_aq], TB))

            def h_ring(ps, Bp, gcol, ring, Bring):
                tk, Btk = ptmp()
                headnorm_w(ps[:, 0:TB], Bp, gcol, tk[:, 0:TB], [Btk], TB)
                for ti in range(TPB):
                    cp("pool", ring[:, slots[ti], :], tk[:, ti * 128:(ti + 1) * 128], [Btk], [Bring[slots[ti]]])
            handlers.append(lambda ps, Bp: h_ring(ps, Bp, V_AKG, akR, B_akR))
            for r in range(4):
                handlers.append(lambda ps, Bp, r=r: headnorm_w(ps[:, 0:TB], Bp, V_CQG, cqT[:, r, :], [B_cq], TB))
            handlers.append(lambda ps, Bp: headnorm_w(ps[:, 0:TB], Bp, V_CKG + 1, ksT[:, t0:t0 + TB],
                                                       B_ks[blk * TPB:(blk + 1) * TPB], TB))
            handlers.append(lambda ps, Bp: h_ring(ps, Bp, V_CKG + 2, kwR, B_kwR))
            handlers.append(lambda ps, Bp: cp("act", kcC[:, 16:16 + TB], ps[:, 0:TB], [Bp], [B_kcC]))
            handlers.append(lambda ps, Bp: cp("act", vcC[:, 16:16 + TB], ps[:, 0:TB], [Bp], [B_vcC]))
            for pp in range(2):
                handlers.append(lambda ps, Bp, pp=pp: rotary(ps, Bp, dqT[:, pp, :], [B_dq]))
            for pp in range(2):
                handlers.append(lambda ps, Bp, pp=pp: rotary(ps, Bp, dkT[:, pp, :], [B_dk]))
            for h in range(4):
                handlers.append(lambda ps, Bp, h=h: act(sgd[:, h, :], ps[:, 0:TB], AF.Silu, [Bp], [B_sgd]))
            for c in range(4):
                handlers.append(lambda ps, Bp, c=c: cp("act", bxs[:, c, :], ps[:, 0:TB], [Bp], [B_bx[c]]))
            for c in range(4):
                handlers.append(lambda ps, Bp, c=c: cp("act", bbs[:, c, :], ps[:, 0:TB], [Bp], [B_bb[c]]))
            for c in range(4):
                handlers.append(lambda ps, Bp, c=c: tt("dve", zC[:, c, 2:2 + TB], ps[:, 0:TB], bxs[:, c, :], ALU.mult,
                                                       [Bp, B_bx[c]], [B_z[c]]))
            assert len(handlers) == NFM
            dfr = [None]
            for g0 in range(0, NFM, 4):
                ng = min(4, NFM - g0)
                ncols = ng * 128 + (24 if g0 + ng == NFM else 0)
                w, Bw = wgroup(g0 * 128, ncols)
                for k in range(ng):
                    ps, Bp = proj(w, Bw, k * 128)
                    if dfr[0] is not None:
                        dfr[0]()
                    dfr[0] = (lambda hh=handlers[g0 + k], ps=ps, Bp=Bp: hh(ps, Bp))
                if g0 + ng == NFM:
                    ps, Bp = proj(w, Bw, ng * 128, M=24)
                    dfr[0]()
                    dfr[0] = None
                    act(cgs[:], ps[0:24, 0:TB], AF.Sigmoid, [Bp], [B_cgs])
            if stop == "projfm":
                dma("sp", dstv[:, :, t0:t0 + TB], xT[:], B_x, [B_xs[min(l, len(B_xs) - 1)][blk]])
                continue
            wtm1, Bwtm1 = wload(WA_bf[l][:, :, OFF_WIN + OFF_TM: OFF_WIN + OFF_TM + 384], 8, 384)
            wtm2, Bwtm2 = wload(WA_bf[l][:, :, OFF_WIN + OFF_TM + 384: OFF_WIN + OFF_TM + 896], 8, 512)
            for ti in range(TPB):
                tile_i = blk * TPB + ti
                ps, Bp = mmbank()
                for kc in range(8):
                    mm(ps[:, 0:384], xn[:, kc, ti * 128:(ti + 1) * 128], wtm1[:, kc, :], kc == 0, kc == 7,
                       [Bwtm1, B_xn[kc]], [Bp])

                def vput(dst3, off, ps=ps):
                    return (dst3.rearrange("p (a b) -> p a b", a=3)[:, 0:3:2, :],
                            ps[:, off:off + 128].rearrange("p (a b) -> p a b", a=2))
                for (dst3, off, Bd, e0, e1) in ((avR[:, slots[ti], :], 0, B_avR[slots[ti]], "act", "dve"),
                                                (vsA[:, tile_i, :], 128, B_vs[tile_i], "dve", "act"),
                                                (vwR[:, slots[ti], :], 256, B_vwR[slots[ti]], "act", "dve")):
                    cp(e0, dst3[:, 0:64], ps[:, off:off + 64], [Bp], [Bd])
                    cp(e1, dst3[:, 128:192], ps[:, off + 64:off + 128], [Bp], [Bd])
                ps, Bp = mmbank()
                for kc in range(8):
                    mm(ps[:, :], xn[:, kc, ti * 128:(ti + 1) * 128], wtm2[:, kc, :], kc == 0, kc == 7,
                       [Bwtm2, B_xn[kc]], [Bp])
                cp("act", dvR[:, ti, :], ps[:, :], [Bp], [B_dvR[ti]])

            if stop == "projtm":
                dma("sp", dstv[:, :, t0:t0 + TB], xT[:], B_x, [B_xs[min(l, len(B_xs) - 1)][blk]])
                continue
            for c in range(4):
                a, Ba = tmp()
                ts("pool", a[:, 0:TB], zC[:, c, 0:TB], vec[:, V_CW + c:V_CW + c + 1], None, ALU.mult, None,
                   [B_z[c], B_vec], [Ba])
                stt("dve", a[:, 0:TB], zC[:, c, 1:1 + TB], vec[:, V_CW + 4 + c:V_CW + 5 + c], a[:, 0:TB],
                    ALU.mult, ALU.add, [B_z[c], B_vec, Ba], [Ba])
                stt("dve", a[:, 0:TB], zC[:, c, 2:2 + TB], vec[:, V_CW + 8 + c:V_CW + 9 + c], a[:, 0:TB],
                    ALU.mult, ALU.add, [B_z[c], B_vec, Ba], [Ba])
                tt("pool", yT[:, 1, c, :], a[:, 0:TB], bbs[:, c, :], ALU.mult, [Ba, B_bb[c]], [B_y[1][c]])
                cp("pool", zC[:, c, 0:2], zC[:, c, TB:TB + 2], [B_z[c]], [B_z[c]])

            if stop == "conv":
                dma("sp", dstv[:, :, t0:t0 + TB], xT[:], B_x, [B_xs[min(l, len(B_xs) - 1)][blk]])
                continue
            import os as _os
            if _os.environ.get("KSKIP", "") != "cmp":
                per = TB // 16
                n_lo = 0 if blk == 0 else per * blk - 1
                n_hi = per * (blk + 1) - 2
                nn = n_hi - n_lo + 1
                cst = 16 if blk == 0 else 0
                pieces = []
                n_ = n_lo
                while n_ <= n_hi:
                    e_ = min(n_hi, (n_ // 128) * 128 + 127)
                    pieces.append((n_ // 128, n_, e_ - n_ + 1))
                    n_ = e_ + 1
                for kv, (car, Bcar) in enumerate(((kcC, B_kcC), (vcC, B_vcC))):
                    for ch in range(2):
                        for half in range(2):
                            w1, Bw1 = wload(W1_bf[l][kv][:, half * 4096:(half + 1) * 4096]
                                            .rearrange("p (a b) -> p a b", a=16)[:, :, ch * 128:(ch + 1) * 128], 16, 128)
                            for li in range(16):
                                lg_ = half * 16 + li
                                for g in range(2):
                                    rows = slice(64 * g, 64 * g + 64)
                                    rhs = car[rows, cst + lg_: cst + lg_ + 16 * (nn - 1) + 1: 16]
                                    bk = 3 if g == 0 else 2
                                    mm(pb[bk][:, 0:nn], w1[rows, li, :], rhs, lg_ == 0, lg_ == 31,
                                       [Bw1, Bcar], [B_pb[bk]])
                        ts("dve", hc[:, 0:nn], pb[3][:, 0:nn], cbias[:, kv, ch:ch + 1], None, ALU.add, None,
                           [B_pb[3], B_cbias], [B_hc])
                        ts("dve", hc[:, 64:64 + nn], pb[2][:, 0:nn], cbias[:, kv, ch:ch + 1], None, ALU.add, None,
                           [B_pb[2], B_cbias], [B_hc])
                        tt("pool", hc2[:], hc[:], hc[:], ALU.mult, [B_hc], [B_hc2])
                        ts("pool", hc2[:], hc2[:], 0.044715, 1.0, ALU.mult, ALU.add, [B_hc2], [B_hc2])
                        tt("pool", hc2[:], hc2[:], hc[:], ALU.mult, [B_hc2, B_hc], [B_hc2])
                        act(hc2[:], hc2[:], AF.Tanh, [B_hc2], [B_hc2], scale=0.7978845608028654)
                        ts("pool", hc2[:], hc2[:], 1.0, 0.5, ALU.add, ALU.mult, [B_hc2], [B_hc2])
                        if kv == 0:
                            tt("pool", gk[:, ch, :], hc2[:], hc[:], ALU.mult, [B_hc2, B_hc], [B_gk])
                        else:
                            if ch == 0:
                                memset("pool", gpad[:].rearrange("p a b c d -> p (a b c d)"), 0.0, [B_gpad])
                            for pi, (ptile, pn, pcnt) in enumerate(pieces):
                                for g in range(2):
                                    o0 = g * 64 + (pn - n_lo)
                                    tt("pool", gpad[:, pi, g, ch, pn % 128: pn % 128 + pcnt],
                                       hc2[:, o0:o0 + pcnt], hc[:, o0:o0 + pcnt], ALU.mult, [B_hc2, B_hc], [B_gpad])
                cp("pool", kcC[:, 0:16], kcC[:, TB:TB + 16], [B_kcC], [B_kcC])
                cp("pool", vcC[:, 0:16], vcC[:, TB:TB + 16], [B_vcC], [B_vcC])
                first = True
                for ch in range(2):
                    for g in range(2):
                        mm(pb[3][:, 0:nn], w2k_sb[:, g, ch, :], gk[:, ch, g * 64: g * 64 + nn], first, ch == 1 and g == 1,
                           [B_w2, B_gk], [B_pb[3]])
                        first = False
                headnorm_w(pb[3][:, 0:nn], B_pb[3], V_CKG + 0, kcmpT[:, n_lo:n_lo + nn], [B_kcmp], nn)
                for pi, (ptile, pn, pcnt) in enumerate(pieces):
                    for g in range(2):
                        for ch in range(2):
                            mm(pb[3][:, 256 + g * 64: 256 + g * 64 + 64], gpad[:, pi, g, ch, :], w2v_sb[:, ch, :],
                               ch == 0, ch == 1, [B_gpad, B_w2], [B_pb[3]])
                    tt("dve", vcmpA[:, ptile, 0:64], vcmpA[:, ptile, 0:64], pb[3][:, 256:320], ALU.add,
                       [B_pb[3], B_vcmp], [B_vcmp])
                    tt("dve", vcmpA[:, ptile, 128:192], vcmpA[:, ptile, 128:192], pb[3][:, 320:384], ALU.add,
                       [B_pb[3], B_vcmp], [B_vcmp])


            if stop == "cmp":
                dma("sp", dstv[:, :, t0:t0 + TB], xT[:], B_x, [B_xs[min(l, len(B_xs) - 1)][blk]])
                continue
            for ti in range(TPB):
                i = blk * TPB + ti
                tc = ti * 128
                if _en('A'):
                    for g in range(2):
                        rows = slice(64 * g, 64 * g + 64)
                        kts = []
                        for kt in ([i - 1, i] if i >= 1 else [i]):
                            s_ = kt % RING
                            msk = mcur_sb if kt == i else mprev_sb
                            kts.append((akR[rows, s_, :], B_akR[s_], avR[:, s_, 64 * g: 64 * g + 128], B_avR[s_],
                                        [(ident_bf[:], bc4(msk[:]), [B_ident, B_mcur, B_mprev])]))
                        run_att(attend(g, aqT, B_aq, tc, kts, 0, sink=True))
                    for r in range(4):
                        cp("act", yT[:, 0, r, tc:tc + 128], obr[0][:, r * 128:(r + 1) * 128], [B_obr[0]], [B_y[0][r]])
                if _en('B'):
                    for g in range(2):
                        rows = slice(64 * g, 64 * g + 64)
                        kts = []
                        for kt in range(max(0, i - 4), i + 1):
                            s_ = kt % RING
                            masks = []
                            if kt == i:
                                masks = [(ident_bf[:], bc4(mcur_sb[:]), [B_ident, B_mcur])]
                            elif kt == i - 4:
                                masks = [(ident_bf[:], bc4(mprev_sb[:]), [B_ident, B_mprev])]
                            kts.append((kwR[rows, s_, :], B_kwR[s_], vwR[:, s_, 64 * g: 64 * g + 128], B_vwR[s_], masks))
                        run_att(attend(g, cqT, B_cq, tc, kts, 2))
                if _en('C'):
                    n_ct = min(NCT, i // 16 + 1)
                    for g in range(2):
                        rows = slice(64 * g, 64 * g + 64)
                        kts = []
                        for c in range(n_ct):
                            dl = i - 16 * c
                            masks = []
                            if dl <= 16:
                                masks = [(ident_bf[:], bc4(cm_sb[:, dl * 128:(dl + 1) * 128]), [B_ident, B_cm])]
                            kts.append((kcmpT[rows, c * 128:(c + 1) * 128], B_kcmp, vcmpA[:, c, 64 * g: 64 * g + 128],
                                        B_vcmp, masks))
                        ip, Bip = pb[3], B_pb[3]
                        dp, Bdp = pb[2], B_pb[2]
                        pts = list(attend(g, cqT, B_cq, tc, kts, 1))
                        for r in range(4):
                            for c, (pt, Bpt) in enumerate(pts):
                                mm(ip[:, r * 128:(r + 1) * 128], pt[:, r * 128:(r + 1) * 128], ovl_sb[:, c * 128:(c + 1) * 128],
                                   c == 0, c == n_ct - 1, [Bpt, B_ovl], [Bip])
                        for r in range(4):
                            for c, (pt, Bpt) in enumerate(pts):
                                mm(dp[:, r:r + 1], pt[:, r * 128:(r + 1) * 128], ones_col[:], c == 0, c == n_ct - 1,
                                   [Bpt, B_onescol], [Bdp])
                        recip_add(rdt[:], dp[:, 0:4], 1e-30, [Bdp], [B_rdt])
                        ts("dve", impw[:], ip[:, 0:128], rdt[:, 0:1], None, ALU.mult, None, [Bip, B_rdt], [B_impw])
                        for r in range(1, 4):
                            stt("dve", impw[:], ip[:, r * 128:(r + 1) * 128], rdt[:, r:r + 1], impw[:], ALU.mult, ALU.add,
                                [Bip, B_rdt, B_impw], [B_impw])
                        tt("dve", impw[:], impw[:], fb_sb[:, 128 - 2 * i: 256 - 2 * i], ALU.add, [B_impw, B_fb], [B_impw])
                        tt("dve", impb[:], impw[:], f0_sb[:], ALU.add, [B_impw, B_f0], [B_imp])
                        P.op("dve", lambda e: e.max(out=m8[:, 0:8], in_=impb[:]), [B_imp], [B_m8])
                        P.op("dve", lambda e: e.match_replace(out=impw[:], in_to_replace=m8[:, 0:8], in_values=impb[:],
                                                              imm_value=-3.0e4), [B_imp, B_m8], [B_impw])
                        P.op("dve", lambda e: e.max(out=m8[:, 8:16], in_=impw[:]), [B_impw], [B_m8])
                        ts("dve", selb[:], impb[:], m8[:, 15:16], NEG, ALU.is_lt, ALU.mult, [B_imp, B_m8], [B_selb])
                        tr(pbt[:, g * 128:(g + 1) * 128], selb[:], ident_bf[:], [B_selb, B_ident], [B_pbt])
                        cp("act", selbT[0:64, g, 0, :], pbt[0:64, g * 128:(g + 1) * 128], [B_pbt], [B_selbT[g]])
                        cp("act", selbT[64:128, g, 1, :], pbt[64:128, g * 128:(g + 1) * 128], [B_pbt], [B_selbT[g]])
                if _en('C'):
                    cp("pool", cen[:], obr[1][:], [B_obr[1]], [B_cen])
                if _en('D'):
                    for g in range(2):
                        rows = slice(64 * g, 64 * g + 64)
                        kts = []
                        for kt in range(0, i + 1):
                            hf = 0 if kt < 32 else 1
                            masks = [(E_sb[:, (kt % 32) * 128:(kt % 32 + 1) * 128],
                                      bc4(selbT[:, g, hf, :]), [B_E, B_selbT[g]])]
                            if kt == i:
                                masks.append((ident_bf[:], bc4(mcur_sb[:]), [B_ident, B_mcur]))
                            kts.append((ksT[rows, kt * 128:(kt + 1) * 128], B_ks[kt], vsA[:, kt, 64 * g: 64 * g + 128],
                                        B_vs[kt], masks))
                        run_att(attend(g, cqT, B_cq, tc, kts, 1))
                if _en('E'):
                    gp, Bgp = pb[3], B_pb[3]
                    for br, srcb, Bsrc in ((0, cen, B_cen), (1, obr[1], B_obr[1]), (2, obr[2], B_obr[2])):
                        for r in range(4):
                            mm(gp[:, r * 128:(r + 1) * 128], selg_sb[:, (br * 4 + r) * 128:(br * 4 + r + 1) * 128],
                               cgs[:, tc:tc + 128], True, True, [B_selg, B_cgs], [Bgp])
                        if br == 0:
                            tt("dve", osb[:], gp[:], srcb[:], ALU.mult, [Bgp, Bsrc], [B_osb])
                        else:
                            t_, Bt_ = tmp()
                            tt("dve", t_[:], gp[:], srcb[:], ALU.mult, [Bgp, Bsrc], [Bt_])
                            tt("pool", osb[:], osb[:], t_[:], ALU.add, [B_osb, Bt_], [B_osb])
                    for r in range(4):
                        cp("act", yT[:, 2, r, tc:tc + 128], osb[:, r * 128:(r + 1) * 128], [B_osb], [B_y[2][r]])

                if _en('F'):
                    ap_, Bap = pb[4], B_pb[4]
                    for h in range(4):
                        pp, hh = h // 2, h % 2
                        rows = slice(64 * hh, 64 * hh + 64)
                        mm(pb[4 + hh][:, pp * 128:(pp + 1) * 128], dkT[rows, pp, tc:tc + 128], dqT[rows, pp, tc:tc + 128],
                           True, True, [B_dk, B_dq], [B_pb[4 + hh]])
                    if _rl(1):
                        am, Bam = ptmp()
                        for h in range(4):
                            pp, hh = h // 2, h % 2
                            tt("dve", am[:, h * 128:(h + 1) * 128], pb[4 + hh][:, pp * 128:(pp + 1) * 128],
                               dmask_sb[:, h * 128:(h + 1) * 128], ALU.mult, [B_pb[4 + hh], B_dmask], [Bam])

                    if _rl(2):
                        for pp in range(2):
                            tt("dve", qxi[:, pp, tc:tc + 128], dqT[:, pp, tc:tc + 128], xi_sb[:, pp * 128:(pp + 1) * 128],
                               ALU.mult, [B_dq, B_xi], [B_qxi])

                    if _rl(3):
                        op_, Bop = pb[6], B_pb[6]
                        for h in range(4):
                            pp, hh = h // 2, h % 2
                            rows = slice(64 * hh, 64 * hh + 64)
                            mm(op_[:, h * 128:(h + 1) * 128], dvR[:, ti, h * 128:(h + 1) * 128], am[:, h * 128:(h + 1) * 128],
                               True, False, [B_dvR[ti], Bam], [Bop])
                            mm(op_[:, h * 128:(h + 1) * 128], Rbf[rows, pp, hh * 128:(hh + 1) * 128], qxi[rows, pp, tc:tc + 128],
                               False, True, [B_Rbf, B_qxi], [Bop])

                    if _rl(4):
                        for pp in range(2):
                            tr(pbt[:, 256 + pp * 128: 256 + (pp + 1) * 128], dkT[:, pp, tc:tc + 128], ident_bf[:],
                               [B_dk, B_ident], [B_pbt])
                        tt("dve", kz[:].rearrange("p a b -> p (a b)"), pbt[:, 256:512], zt_sb[:], ALU.mult, [B_pbt, B_zt], [B_kz])

                    if _rl(5):
                        sp2, Bsp2 = pb[5], B_pb[5]
                        for pp in range(2):
                            mm(sp2[:, pp * 256:(pp + 1) * 256], kz[:, pp, :], dvR[:, ti, pp * 256:(pp + 1) * 256], True, True,
                               [B_kz, B_dvR[ti]], [Bsp2])
                        for pp in range(2):
                            stt("dve", Rst[:, pp, :], Rst[:, pp, :], dec_sb[:, pp:pp + 1], sp2[:, pp * 256:(pp + 1) * 256],
                                ALU.mult, ALU.add, [B_R, B_dec, Bsp2], [B_R])

                    if _rl(6):
                        cp("act", Rbf[:].rearrange("p a b -> p (a b)"), Rst[:].rearrange("p a b -> p (a b)"), [B_R], [B_Rbf])

                    if _rl(7):
                        o2, Bo2 = tmp()
                        cp("act", o2[:], op_[:], [Bop], [Bo2])
                        mp, Bmp = pb[2], B_pb[2]
                        mm(mp[:], ones_v[:], o2[:], True, True, [Bo2, B_ov], [Bmp])
                        c2, Bc2 = tmp()
                        tt("dve", c2[:], o2[:], mp[:], ALU.subtract, [Bo2, Bmp], [Bc2])

                    if _rl(8):
                        s2, Bs2 = tmp()
                        tt("pool", s2[:], c2[:], c2[:], ALU.mult, [Bc2], [Bs2])
                        mm(mp[:], ones_v[:], s2[:], True, True, [Bs2, B_ov], [Bmp])
                        r2, Br2 = tmp()
                        rsqrt_eps(r2[:], mp[:], [Bmp], [Br2])
                        tt("pool", c2[:], c2[:], r2[:], ALU.mult, [Bc2, Br2], [Bc2])
                        for h in range(4):
                            stt("dve", yT[:, 3, h, tc:tc + 128], c2[:, h * 128:(h + 1) * 128], vec[:, V_RG + h:V_RG + h + 1],
                                sgd[:, h, tc:tc + 128], ALU.mult, ALU.mult, [Bc2, B_vec, B_sgd], [B_y[3][h]])


                if stop == "att":
                    dma("sp", dstv[:, :, t0:t0 + TB], xT[:], B_x, [B_xs[min(l, len(B_xs) - 1)][blk]])
                    continue
            for m0 in range(0, 8, 4):
                for bi in range(4):
                    wbv, Bwb = wload(WB_bf[l][:, bi * 4:(bi + 1) * 4, m0 * 128:(m0 + 4) * 128], 4, 512)
                    gvw, Bgv = wload(WA_bf[l][:, :, OFF_WIN + OFF_GATES + bi * 1024 + m0 * 128:
                                              OFF_WIN + OFF_GATES + bi * 1024 + (m0 + 4) * 128], 8, 512)
                    for m in range(m0, m0 + 4):
                        mo = (m - m0) * 128
                        pg, Bg = pb[0], B_pb[0]
                        for kc in range(8):
                            mm(pg[:, 0:TB], gvw[:, kc, mo:mo + 128], xn[:, kc, :], kc == 0, kc == 7, [Bgv, B_xn[kc]], [Bg])
                        gs, Bgs = tmp()
                        act(gs[:, 0:TB], pg[:, 0:TB], AF.Sigmoid, [Bg, B_vec], [Bgs],
                            bias=vec[:, V_GB + bi * 8 + m: V_GB + bi * 8 + m + 1])
                        pbk, Bbk = pb[1], B_pb[1]
                        for c in range(4):
                            mm(pbk[:, 0:TB], wbv[:, c, mo:mo + 128], yT[:, bi, c, :], c == 0, c == 3,
                               [Bwb, B_y[bi][c]], [Bbk])
                        if bi == 0:
                            tt("dve", macc[:, m - m0, :], pbk[:, 0:TB], gs[:, 0:TB], ALU.mult, [Bbk, Bgs], [B_macc[m - m0]])
                        else:
                            t_, Bt_ = tmp()
                            tt("dve", t_[:, 0:TB], pbk[:, 0:TB], gs[:, 0:TB], ALU.mult, [Bbk, Bgs], [Bt_])
                            if bi < 3:
                                tt("pool", macc[:, m - m0, :], macc[:, m - m0, :], t_[:, 0:TB], ALU.add,
                                   [B_macc[m - m0], Bt_], [B_macc[m - m0]])
                            else:
                                tt("pool", mrg[:, m, :], macc[:, m - m0, :], t_[:, 0:TB], ALU.add,
                                   [B_macc[m - m0], Bt_], [B_mrg[m]])
            for m0 in range(0, 8, 4):
                wo, Bwo = wload(WA_bf[l][:, :, OFF_WO + m0 * 128: OFF_WO + (m0 + 4) * 128], 8, 512)
                for m in range(m0, m0 + 4):
                    po, Bo = mmbank()
                    for kc in range(8):
                        mm(po[:, 0:TB], wo[:, kc, (m - m0) * 128:(m - m0 + 1) * 128], mrg[:, kc, :], kc == 0, kc == 7,
                           [Bwo, B_mrg[kc]], [Bo])
                    tt("dve", xT[:, m, :], xT[:, m, :], po[:, 0:TB], ALU.add, [B_x[m], Bo], [B_x[m]])
            if stop != "mix":
                rmsnorm_block(V_N2)
                ffn_block(l, OFF_F2G, OFF_F2U, 1024)
            dma("sp", dstv[:, :, t0:t0 + TB], xT[:], B_x, [B_xs[min(l, len(B_xs) - 1)][blk]])

    P.emit()
    return nc, P


_CACHE = {}


def run_cores(x, inp, L, T, n_cores, TB=256, stop=None):
    key = (T, L, TB, stop)
    if key not in _CACHE:
        _CACHE[key] = build(T, L, TB, stop)
    nc, P = _CACHE[key]
    w = prep_weights(inp, L)
    cs = make_consts(T)
    in_maps = []
    for c in range(n_cores):
        m = {"xT": np.ascontiguousarray(x[c].T.astype(np.float32))}
        m.update(w)
        for k, v in cs.items():
            m["c_" + k] = np.ascontiguousarray(v)
        in_maps.append(m)
    res = run_bass_kernel_spmd(nc, in_maps, core_ids=list(range(n_cores)))
    return np.stack([np.ascontiguousarray(r["outT"].T) for r in res.results], axis=0)


ACTIVE_CORES = (0, 1, 4, 5)


def kernel(**inputs):
    x = np.asarray(inputs["x"], dtype=np.float32)
    B, T, _ = x.shape
    inp = {k: np.asarray(v, dtype=np.float32) for k, v in inputs.items() if k != "x"}
    key = (T, L_FULL, 256, None)
    if key not in _CACHE:
        _CACHE[key] = build(T, L_FULL, 256, None)
    nc, P = _CACHE[key]
    w = prep_weights(inp, L_FULL)
    cs = make_consts(T)
    base = dict(w)
    for k, v in cs.items():
        base["c_" + k] = np.ascontiguousarray(v)
    zero = {k: np.zeros_like(v) for k, v in w.items()}
    for k, v in cs.items():
        zero["c_" + k] = np.ascontiguousarray(v)
    in_maps = []
    bi = 0
    for c in range(8):
        if c in ACTIVE_CORES and bi < B:
            m = dict(base)
            m["xT"] = np.ascontiguousarray(x[bi].T)
            bi += 1
        else:
            m = dict(zero)
            m["xT"] = np.zeros((1024, T), np.float32)
        in_maps.append(m)
    assert bi == B
    res = run_bass_kernel_spmd(nc, in_maps, core_ids=list(range(8)))
    outs = [np.ascontiguousarray(res.results[c]["outT"].T) for c in ACTIVE_CORES[:B]]
    return np.stack(outs, axis=0).astype(np.float32)
```

```python
import contextlib
import math
import numpy as np
import concourse.bass as bass
import concourse.mybir as mybir
from concourse.bass_utils import run_bass_kernel_spmd

F32 = mybir.dt.float32
BF16 = mybir.dt.bfloat16
AF = mybir.ActivationFunctionType
ALU = mybir.AluOpType

D = 1024
DFF = 2816
L_FULL = 2
EPS = 1e-6
NEG = -32768.0
BIG = 1.0e4
IN_TOTAL = 9240
OFF_F1G, OFF_F1U, OFF_WIN = 0, 2816, 5632
OFF_WO = OFF_WIN + IN_TOTAL
OFF_F2G = OFF_WO + 1024
OFF_F2U = OFF_F2G + 2816
WA_COLS = OFF_F2U + 2816
NFM = 33
OFF_CG = NFM * 128
OFF_GATES = OFF_CG + 24
OFF_TM = OFF_GATES + 4096
(C_AQ, C_AK, C_CQ, C_CKS, C_CKW, C_CKC, C_CVC, C_DQ, C_DK, C_DG, C_BX, C_BB, C_BC) = (
    0, 4, 5, 9, 10, 11, 12, 13, 15, 17, 21, 25, 29)
V_N1, V_NM, V_N2, V_GB, V_AQG, V_AKG, V_CQG, V_CKG, V_CW, V_RG = 0, 8, 16, 24, 56, 57, 58, 59, 62, 74
NVEC = 78


class Buf:
    __slots__ = ("name", "last_w", "readers")

    def __init__(self, name=""):
        self.name = name
        self.last_w = None
        self.readers = []


class Ins:
    __slots__ = ("eng", "fn", "deps", "raw", "signal", "rank", "dma_slot", "dma_val", "is_dma", "pre_waits")

    def __init__(self, eng, fn, is_dma):
        self.eng = eng
        self.fn = fn
        self.deps = set()
        self.raw = set()
        self.signal = False
        self.rank = None
        self.is_dma = is_dma
        self.dma_slot = None
        self.dma_val = None
        self.pre_waits = []


ENGS = ("pe", "act", "dve", "pool", "sp")
N_HW_SEM = 24
N_SW_SEM = 8
N_DMA_SEM = N_HW_SEM + N_SW_SEM


class Prog:
    def __init__(self, nc):
        self.nc = nc
        self.ins = []
        self.eng_obj = {"pe": nc.tensor, "act": nc.scalar, "dve": nc.vector,
                        "pool": nc.gpsimd, "sp": nc.sync}
        self.dma_count = 0
        self.hw_count = 0
        self.sw_count = 0
        self.dma_slot_last = [None] * N_DMA_SEM

    def op(self, eng, fn, reads=(), writes=(), dma=False):
        i = Ins(eng, fn, dma)
        iid = len(self.ins)
        for b in reads:
            if b.last_w is not None:
                i.deps.add(b.last_w)
                i.raw.add(b.last_w)
        for b in writes:
            if b.last_w is not None:
                i.deps.add(b.last_w)
            for r in b.readers:
                i.deps.add(r)
        for b in reads:
            b.readers.append(iid)
        for b in writes:
            b.last_w = iid
            b.readers = []
        if dma:
            if eng == "pool":
                k = self.sw_count
                slot = N_HW_SEM + k % N_SW_SEM
                val = 16 * (k // N_SW_SEM + 1)
                self.sw_count += 1
            else:
                k = self.hw_count
                slot = k % N_HW_SEM
                val = 16 * (k // N_HW_SEM + 1)
                self.hw_count += 1
            prev = self.dma_slot_last[slot]
            if prev is not None:
                i.deps.add(prev)
            self.dma_slot_last[slot] = iid
            i.dma_slot = slot
            i.dma_val = val
            self.dma_count += 1
        i.deps.discard(iid)
        self.ins.append(i)
        return iid

    def emit(self, final_wait_eng="sp"):
        nc = self.nc
        ins = self.ins
        waited_eng = {e: {s: -1 for s in ENGS} for e in ENGS}
        waited_dma = {e: {} for e in ENGS}
        for iid, i in enumerate(ins):
            e = i.eng
            need_eng = {}
            need_dma = {}
            for d in i.deps:
                di = ins[d]
                if di.is_dma:
                    if need_dma.get(di.dma_slot, 0) < di.dma_val:
                        need_dma[di.dma_slot] = di.dma_val
                else:
                    if di.eng == e and not i.is_dma:
                        if e == "pe":
                            continue
                    if need_eng.get(di.eng, -1) < d:
                        need_eng[di.eng] = d
            for s, d in need_eng.items():
                if waited_eng[e][s] >= d:
                    continue
                waited_eng[e][s] = d
                ins[d].signal = True
                i.pre_waits.append(("eng", s, d))
            for slot, val in need_dma.items():
                if waited_dma[e].get(slot, 0) >= val:
                    continue
                waited_dma[e][slot] = val
                i.pre_waits.append(("dma", slot, val))
        rk = {e: 0 for e in ENGS}
        counts = {e: 0 for e in ENGS}
        for i in ins:
            counts[i.eng] += 1
            if i.signal and not i.is_dma:
                rk[i.eng] += 1
                i.rank = rk[i.eng]
        self.stats = dict(counts=counts, signals=dict(rk), n=len(ins), dmas=self.dma_count)
        with contextlib.ExitStack() as st:
            esem = {e: st.enter_context(nc.semaphore("s_" + e)) for e in ENGS}
            dsem = [st.enter_context(nc.semaphore("d_%d" % k)) for k in range(N_DMA_SEM)]
            for i in ins:
                eo = self.eng_obj[i.eng]
                for w in i.pre_waits:
                    if w[0] == "eng":
                        eo.wait_ge(esem[w[1]], ins[w[2]].rank)
                    else:
                        eo.wait_ge(dsem[w[1]], w[2])
                r = i.fn(eo)
                if i.is_dma:
                    r.then_inc(dsem[i.dma_slot], 16)
                elif i.signal:
                    r.then_inc(esem[i.eng], 1)
            eo = self.eng_obj[final_wait_eng]
            for slot in range(N_DMA_SEM):
                d = self.dma_slot_last[slot]
                if d is not None:
                    eo.wait_ge(dsem[slot], ins[d].dma_val)


def make_consts(T):
    NT = T // 128
    c = {}
    c["ident"] = np.eye(128, dtype=np.float32)
    bd = np.zeros((128, 128), np.float32)
    bd[:64, :64] = 1.0 / 64
    bd[64:, 64:] = 1.0 / 64
    c["ones_bd"] = bd
    c["ones_d"] = np.full((128, 128), 1.0 / 1024, np.float32)
    c["ones_v"] = np.full((128, 128), 1.0 / 128, np.float32)
    E = np.zeros((128, 32, 128), np.float32)
    for q in range(32):
        for m in range(128):
            s = 2 * q + m // 64
            E[s, q, m] = 1.0
            E[64 + s, q, m] = 1.0
    c["E"] = E.reshape(128, 32 * 128)
    n = np.arange(512)
    s = np.arange(128)
    cs = n * 16
    ce = cs + 31
    ss = s * 64
    ov = ((cs[:, None] < ss[None, :] + 64) & (ce[:, None] >= ss[None, :])).astype(np.float32)
    c["ovl"] = ov.reshape(4, 128, 128).transpose(1, 0, 2).reshape(128, 4 * 128)
    j = np.arange(128)
    e = np.arange(256)
    rel = (e[None, :] - 128) - (j[:, None] // 64)
    fb = np.zeros((128, 256), np.float32)
    fb[(rel == 0) | (rel == -1)] = BIG
    fb[rel > 0] = -BIG
    c["fb"] = fb
    f0 = np.zeros((128, 128), np.float32)
    f0[:, 0] = BIG
    c["f0"] = f0
    m = np.arange(128)
    c["mask_cur"] = np.where(m[:, None] <= j[None, :], 0.0, NEG).astype(np.float32)
    c["mask_prev"] = np.where(m[:, None] > j[None, :], 0.0, NEG).astype(np.float32)
    cm = np.zeros((128, 17, 128), np.float32)
    for dl in range(17):
        ok = (16 * m[:, None] + 31 - j[None, :]) <= 128 * dl
        cm[:, dl, :] = np.where(ok, 0.0, NEG)
    c["cm"] = cm.reshape(128, 17 * 128)
    sg = np.zeros((24, 12, 128), np.float32)
    for br in range(3):
        for r in range(4):
            for mm in range(128):
                h = 4 * (mm // 64) + r
                sg[h * 3 + br, br * 4 + r, mm] = 1.0
    c["selg"] = sg.reshape(24, 12 * 128)
    gam = 1.0 - 2.0 ** (-5.0 - np.arange(4, dtype=np.float64))
    lg = np.log(gam)
    diff = j[None, :].astype(np.float64) - m[:, None]
    dm = np.zeros((128, 4, 128), np.float64)
    for h in range(4):
        dm[:, h, :] = np.where(diff >= 0, np.exp(diff * lg[h]), 0.0) * 0.125
    c["dmask"] = dm.reshape(128, 512).astype(np.float32)
    xi = np.zeros((128, 2, 128), np.float64)
    zt = np.zeros((128, 2, 128), np.float64)
    dec = np.zeros((128, 2), np.float64)
    for pp in range(2):
        for hh in range(2):
            h = 2 * pp + hh
            xi[64 * hh:64 * hh + 64, pp, :] = np.exp((j[None, :] + 1.0) * lg[h])
            zt[:, pp, 64 * hh:64 * hh + 64] = (np.exp((127.0 - m) * lg[h]) * 0.125)[:, None]
            dec[64 * hh:64 * hh + 64, pp] = np.exp(128.0 * lg[h])
    c["xi"] = xi.reshape(128, 256).astype(np.float32)
    c["zt"] = zt.reshape(128, 256).astype(np.float32)
    c["dec"] = dec.astype(np.float32)
    rm = np.zeros((128, 128), np.float32)
    for mm in range(128):
        if mm % 64 < 32:
            rm[mm + 32, mm] = -1.0
        else:
            rm[mm - 32, mm] = 1.0
    c["rm"] = rm
    half = 32
    inv = 10000.0 ** (-np.arange(half, dtype=np.float32) / half)
    p = np.arange(128)
    ang = np.arange(T, dtype=np.float32)[None, :] * inv[p % 32][:, None]
    c["cosT"] = np.cos(ang).astype(np.float32)
    c["sinT"] = np.sin(ang).astype(np.float32)
    return c


CONST_SHAPES = lambda T: {k: v.shape for k, v in make_consts(128 if False else T).items()}


def win_perm():
    aq0, ak0, av0, bx0, bb0, bc0, cq0, ckc0, cvc0, cks0, cvs0, ckw0, cvw0, cg0, dq0, dk0, dv0, dg0, gl0 = (
        0, 512, 640, 768, 1280, 1792, 2304, 2816, 2944, 3072, 3200, 3328, 3456, 3584, 3608, 3864, 4120, 4632, 5144)
    cols = []
    r64 = np.arange(64)
    for r in range(4):
        cols += list(aq0 + (0 * 4 + r) * 64 + r64) + list(aq0 + (4 + r) * 64 + r64)
    cols += list(ak0 + np.arange(128))
    for r in range(4):
        cols += list(cq0 + (0 * 4 + r) * 64 + r64) + list(cq0 + (4 + r) * 64 + r64)
    cols += list(cks0 + np.arange(128)) + list(ckw0 + np.arange(128))
    cols += list(ckc0 + np.arange(128)) + list(cvc0 + np.arange(128))
    cols += list(dq0 + np.arange(256)) + list(dk0 + np.arange(256)) + list(dg0 + np.arange(512))
    cols += list(bx0 + np.arange(512)) + list(bb0 + np.arange(512)) + list(bc0 + np.arange(512))
    cols += list(cg0 + np.arange(24))
    cols += list(gl0 + np.arange(4096))
    cols += list(av0 + np.arange(128)) + list(cvs0 + np.arange(128)) + list(cvw0 + np.arange(128))
    cols += list(dv0 + np.arange(512))
    assert len(cols) == IN_TOTAL and len(set(cols)) == IN_TOTAL
    return np.array(cols)


def prep_weights(inp, L):
    perm = win_perm()
    WA = np.concatenate([inp["ffn1_w_gate"][:L], inp["ffn1_w_up"][:L], inp["w_in"][:L][:, :, perm],
                         inp["w_out"][:L], inp["ffn2_w_gate"][:L], inp["ffn2_w_up"][:L]], axis=2)
    WD = np.concatenate([inp["ffn1_w_down"][:L], inp["ffn2_w_down"][:L]], axis=2)
    WB = inp["w_branch"][:L]
    att_rows = []
    r64 = np.arange(64)
    for r in range(4):
        att_rows += list((0 * 4 + r) * 64 + r64) + list((4 + r) * 64 + r64)
    att_rows = np.array(att_rows)
    WB = WB.copy()
    WB[:, 0] = WB[:, 0][:, att_rows, :]
    WB[:, 2] = WB[:, 2][:, att_rows, :]
    WB = WB.reshape(L, 2048, 1024)

    def w1l(w):
        a = w[:L].reshape(L, 32, 64, 256).transpose(0, 2, 1, 3).reshape(L, 64, 32 * 256)
        return np.concatenate([a, a], axis=1)
    W1 = np.stack([w1l(inp["cmp_wk1"]), w1l(inp["cmp_wv1"])], axis=1)
    w2k = inp["cmp_wk2"][:L]
    z = np.zeros_like(w2k)
    W2K = np.stack([np.concatenate([w2k, z], axis=2), np.concatenate([z, w2k], axis=2)], axis=1)
    W2V = inp["cmp_wv2"][:L]
    pek = np.stack([inp["cmp_pos_k"][:L], inp["cmp_pos_v"][:L]], axis=1)
    peT = pek.transpose(0, 1, 3, 2)
    peT = np.concatenate([peT, peT], axis=2)
    vec = np.zeros((L, 128, NVEC), np.float32)

    def fm(v, n):
        return v.reshape(L, n, 128).transpose(0, 2, 1)
    vec[:, :, V_N1:V_N1 + 8] = fm(inp["ffn1_norm"][:L], 8)
    vec[:, :, V_NM:V_NM + 8] = fm(inp["mix_norm"][:L], 8)
    vec[:, :, V_N2:V_N2 + 8] = fm(inp["ffn2_norm"][:L], 8)
    vec[:, :, V_GB:V_GB + 32] = fm(inp["merge_gate_bias"][:L], 32)
    vec[:, :, V_AQG] = np.tile(inp["swa_q_gain"][:L], (1, 2))
    vec[:, :, V_AKG] = np.tile(inp["swa_k_gain"][:L], (1, 2))
    vec[:, :, V_CQG] = np.tile(inp["nsa_q_gain"][:L], (1, 2))
    for k in range(3):
        vec[:, :, V_CKG + k] = np.tile(inp["nsa_k_gain"][:L, k], (1, 2))
    cw = inp["conv_w"][:L].reshape(L, 3, 512)
    for k in range(3):
        vec[:, :, V_CW + 4 * k:V_CW + 4 * k + 4] = fm(cw[:, k], 4)
    vec[:, :, V_RG:V_RG + 4] = fm(inp["ret_norm_gain"][:L], 4)
    sk = inp["swa_sinks"][:L].reshape(L, 2, 4)
    sinks = np.repeat(sk, 64, axis=1)
    f = lambda a: np.ascontiguousarray(a, dtype=np.float32)
    return dict(WA=f(WA), WD=f(WD), WB=f(WB), W1=f(W1), W2K=f(W2K), W2V=f(W2V), peT=f(peT), vec=f(vec), sinks=f(sinks))


def build(T, L, TB=256, stop=None):
    nc = bass.Bass("TRN2", target_bir_lowering=False)
    P = Prog(nc)
    NT = T // 128
    NB = T // TB
    TPB = TB // 128
    NCT = max(1, T // 2048)
    RING = 8
    consts_np_shapes = {k: v.shape for k, v in make_consts(T).items()}

    def din(name, shape):
        return nc.dram_tensor(name, list(shape), F32, kind="ExternalInput").ap()

    xT_in = din("xT", (1024, T))
    WA = din("WA", (L, 1024, WA_COLS))
    WD = din("WD", (L, 2816, 2048))
    WB = din("WB", (L, 2048, 1024))
    W1 = din("W1", (L, 2, 128, 8192))
    W2K = din("W2K", (L, 2, 256, 128))
    W2V = din("W2V", (L, 256, 64))
    peT = din("peT", (L, 2, 128, 32))
    vec_in = din("vec", (L, 128, NVEC))
    sinks_in = din("sinks", (L, 128, 4))
    cin = {k: din("c_" + k, s) for k, s in consts_np_shapes.items()}
    outT = nc.dram_tensor("outT", [1024, T], F32, kind="ExternalOutput").ap()

    WA_bf = [nc.dram_tensor("WAbf%d" % l, [128, 8, WA_COLS], BF16).ap() for l in range(L)]
    WD_bf = [nc.dram_tensor("WDbf%d" % l, [128, 22, 2048], BF16).ap() for l in range(L)]
    WB_bf = [nc.dram_tensor("WBbf%d" % l, [128, 16, 1024], BF16).ap() for l in range(L)]
    W1_bf = [nc.dram_tensor("W1bf%d" % l, [2, 128, 8192], BF16).ap() for l in range(L)]
    xs = [nc.dram_tensor("xs%d" % l, [1024, T], F32).ap() for l in range(max(L - 1, 1))]
    B_wscr = [Buf("wscr%d" % l) for l in range(L)]
    B_xs = [[Buf() for _ in range(NB)] for _ in range(max(L - 1, 1))]

    def sb(name, shape, dt=F32):
        return nc.alloc_sbuf_tensor("s_" + name, list(shape), dt)

    def mm(out, lhsT, rhs, start, stop_, reads, writes):
        P.op("pe", lambda e: e.matmul(out, lhsT, rhs, start=start, stop=stop_), reads, writes)

    def tr(out, in_, ident, reads, writes):
        P.op("pe", lambda e: e.matmul(out, in_, ident, start=True, stop=True), reads, writes)

    def act(out, in_, func, reads, writes, bias=None, scale=None):
        kw = {}
        if bias is not None:
            kw["bias"] = bias
        if scale is not None:
            kw["scale"] = scale
        P.op("act", lambda e: e.activation(out=out, in_=in_, func=func, **kw), reads, writes)

    def cp(eng, out, in_, reads, writes):
        if eng == "act":
            P.op("act", lambda e: e.activation(out=out, in_=in_, func=AF.Copy), reads, writes)
        else:
            P.op(eng, lambda e: e.tensor_copy(out=out, in_=in_), reads, writes)

    def tt(eng, out, in0, in1, op, reads, writes):
        P.op(eng, lambda e: e.tensor_tensor(out=out, in0=in0, in1=in1, op=op), reads, writes)

    def ts(eng, out, in0, s1, s2, op0, op1, reads, writes):
        if op1 is None:
            P.op(eng, lambda e: e.tensor_scalar(out=out, in0=in0, scalar1=s1, scalar2=None, op0=op0), reads, writes)
        else:
            P.op(eng, lambda e: e.tensor_scalar(out=out, in0=in0, scalar1=s1, scalar2=s2, op0=op0, op1=op1), reads, writes)

    def stt(eng, out, in0, scalar, in1, op0, op1, reads, writes):
        P.op(eng, lambda e: e.scalar_tensor_tensor(out=out, in0=in0, scalar=scalar, in1=in1, op0=op0, op1=op1),
             reads, writes)

    def memset(eng, ap, val, writes):
        P.op(eng, lambda e: e.memset(ap, val), (), writes)

    def rsqrt_eps(out, in_, reads, writes):
        P.op("act", lambda e: e.activation(out=out, in_=in_, func=AF.Sqrt, bias=eps_col[0:out.shape[0], :], scale=1.0), list(reads) + [B_epscol], writes)
        P.op("dve", lambda e: e.reciprocal(out=out, in_=out), writes, writes)

    def recip_add(out, in_, addend, reads, writes):
        P.op("dve", lambda e: e.tensor_scalar(out=out, in0=in_, scalar1=addend, scalar2=None, op0=ALU.add), reads, writes)
        P.op("dve", lambda e: e.reciprocal(out=out, in_=out), writes, writes)

    def dma(q, out, in_, reads, writes):
        P.op(q, lambda e: e.dma_start(out=out, in_=in_), reads, writes, dma=True)

    def load_const(name, dt, q="pool", src=None, shape=None):
        src = cin[name] if src is None else src
        shape = consts_np_shapes[name] if shape is None else shape
        t = sb("k_" + name, shape, dt)
        b = Buf(name)
        dma("pool" if dt == BF16 else "sp", t[:], src, (), [b])
        return t, b

    ident_bf, B_ident = load_const("ident", BF16)
    ones_bd, B_obd = load_const("ones_bd", F32)
    ones_d, B_od = load_const("ones_d", F32)
    ones_v, B_ov = load_const("ones_v", F32)
    E_sb, B_E = load_const("E", BF16)
    ovl_sb, B_ovl = load_const("ovl", BF16)
    fb_sb, B_fb = load_const("fb", F32)
    f0_sb, B_f0 = load_const("f0", F32)
    mcur_sb, B_mcur = load_const("mask_cur", BF16)
    mprev_sb, B_mprev = load_const("mask_prev", BF16)
    cm_sb, B_cm = load_const("cm", BF16)
    selg_sb, B_selg = load_const("selg", BF16)
    dmask_sb, B_dmask = load_const("dmask", F32)
    xi_sb, B_xi = load_const("xi", F32)
    zt_sb, B_zt = load_const("zt", F32)
    dec_sb, B_dec = load_const("dec", F32)
    rm_sb, B_rm = load_const("rm", F32)
    eps_col = sb("eps_col", [128, 1], F32)
    B_epscol = Buf()
    memset("dve", eps_col[:], EPS, [B_epscol])
    ones_col = sb("ones_col", [128, 1], BF16)
    B_onescol = Buf()
    memset("dve", ones_col[:], 1.0, [B_onescol])
    CONSTB = [B_ident, B_obd, B_od, B_ov, B_E, B_ovl, B_fb, B_f0, B_mcur, B_mprev, B_cm, B_selg, B_dmask, B_xi,
              B_zt, B_dec, B_rm, B_onescol]

    pb = [nc.alloc_psum_tensor("pb%d" % k, [128, 512], F32) for k in range(7)]
    B_pb = [Buf("pb%d" % k) for k in range(7)]
    pbt = nc.alloc_psum_tensor("pbt", [128, 512], F32)
    B_pbt = Buf("pbt")
    mm_rot = [0]

    def mmbank():
        k = mm_rot[0] % 2
        mm_rot[0] += 1
        return pb[k], B_pb[k]

    xT = sb("xT", [128, 8, TB], F32); B_x = [Buf() for _ in range(8)]
    xn = sb("xn", [128, 8, TB], BF16); B_xn = [Buf() for _ in range(8)]
    hT = sb("hT", [128, 22, TB], BF16); B_h = [Buf() for _ in range(22)]
    yT = sb("yT", [128, 4, 4, TB], BF16); B_y = [[Buf() for _ in range(4)] for _ in range(4)]
    mrg = sb("mrg", [128, 8, TB], BF16); B_mrg = [Buf() for _ in range(8)]
    aqT = sb("aqT", [128, 2, 4, TB], BF16); B_aq = Buf()
    cqT = sb("cqT", [128, 2, 4, TB], BF16); B_cq = Buf()
    memset("dve", aqT[:].rearrange("p a b c -> p (a b c)"), 0.0, [B_aq])
    memset("dve", cqT[:].rearrange("p a b c -> p (a b c)"), 0.0, [B_cq])
    dqT = sb("dqT", [128, 2, TB], BF16); B_dq = Buf()
    dkT = sb("dkT", [128, 2, TB], BF16); B_dk = Buf()
    qxi = sb("qxi", [128, 2, TB], BF16); B_qxi = Buf()
    sgd = sb("sgd", [128, 4, TB], F32); B_sgd = Buf()
    cgs = sb("cgs", [24, TB], BF16); B_cgs = Buf()
    cosb = sb("cosb", [128, TB], F32); sinb = sb("sinb", [128, TB], F32); B_cs = Buf()
    vec = sb("vec", [128, NVEC], F32); B_vec = Buf()
    esink = sb("esink", [128, 4], F32); B_esink = Buf()
    ksT = sb("ksT", [128, T], BF16); B_ks = [Buf() for _ in range(NT)]
    vsA = sb("vsA", [128, NT, 192], BF16); B_vs = [Buf() for _ in range(NT)]
    akR = sb("akR", [128, RING, 128], BF16); B_akR = [Buf() for _ in range(RING)]
    avR = sb("avR", [128, RING, 192], BF16); B_avR = [Buf() for _ in range(RING)]
    kwR = sb("kwR", [128, RING, 128], BF16); B_kwR = [Buf() for _ in range(RING)]
    vwR = sb("vwR", [128, RING, 192], BF16); B_vwR = [Buf() for _ in range(RING)]
    dvR = sb("dvR", [128, TPB, 512], BF16); B_dvR = [Buf() for _ in range(TPB)]
    kcmpT = sb("kcmpT", [128, NCT * 128], BF16); B_kcmp = Buf()
    vcmpA = sb("vcmpA", [128, NCT, 192], BF16); B_vcmp = Buf()
    kcC = sb("kcC", [128, 16 + TB], BF16); vcC = sb("vcC", [128, 16 + TB], BF16); B_kcC = Buf(); B_vcC = Buf()
    zC = sb("zC", [128, 4, 2 + TB], F32); B_z = [Buf() for _ in range(4)]
    Rst = sb("Rst", [128, 2, 256], F32); Rbf = sb("Rbf", [128, 2, 256], BF16); B_R = Buf(); B_Rbf = Buf()
    w2k_sb = sb("w2k", [128, 2, 2, 128], BF16); w2v_sb = sb("w2v", [128, 2, 64], BF16); B_w2 = Buf()
    peT_sb = sb("peT", [128, 2, 32], BF16); B_pe = Buf()
    cbias = sb("cbias", [128, 2, 2], F32); B_cbias = Buf()
    NTMP = 5
    tmpf = [sb("tmpf%d" % k, [128, 512], F32) for k in range(NTMP)]; B_tmpf = [Buf() for _ in range(NTMP)]
    tf_rot = [0]

    def tmp():
        k = tf_rot[0] % NTMP
        tf_rot[0] += 1
        return tmpf[k], B_tmpf[k]
    ptb = [sb("ptb%d" % k, [128, 512], BF16) for k in range(6)]; B_ptb = [Buf() for _ in range(6)]
    pt_rot = [0]

    def ptmp():
        k = pt_rot[0] % 6
        pt_rot[0] += 1
        return ptb[k], B_ptb[k]
    obr = [sb("obr%d" % k, [128, 512], F32) for k in range(3)]; B_obr = [Buf() for _ in range(3)]
    selbT = sb("selbT", [128, 2, 2, 128], BF16); B_selbT = [Buf(), Buf()]
    memset("dve", selbT[:].rearrange("p a b c -> p (a b c)"), 0.0, B_selbT)
    impb = sb("impb", [128, 128], F32); impw = sb("impw", [128, 128], F32); m8 = sb("m8", [128, 16], F32)
    selb = sb("selb", [128, 128], BF16); rdt = sb("rdt", [128, 4], F32)
    B_imp = Buf(); B_impw = Buf(); B_m8 = Buf(); B_selb = Buf(); B_rdt = Buf()
    kz = sb("kz", [128, 2, 128], BF16); B_kz = Buf()
    WSLOT = 4096
    NW = 3
    wring = [sb("wr%d" % k, [128, WSLOT], BF16) for k in range(NW)]; B_wr = [Buf() for _ in range(NW)]
    w_rot = [0]

    def wload(src_ap, n_in, n_col):
        k = w_rot[0] % NW
        w_rot[0] += 1
        assert n_in * n_col <= WSLOT
        v = wring[k][:, 0:n_in * n_col].rearrange("p (a b) -> p a b", a=n_in)
        dma("sp", v, src_ap, [B_wscr_cur[0]], [B_wr[k]])
        return v, B_wr[k]

    B_wscr_cur = [None]

    def convert_weights(l):
        b = B_wscr[l]
        for kc in range(8):
            dma("pool", WA_bf[l][:, kc, :], WA[l, kc * 128:(kc + 1) * 128, :], (), [b])
        for kc in range(22):
            dma("pool", WD_bf[l][:, kc, :], WD[l, kc * 128:(kc + 1) * 128, :], (), [b])
        for kc in range(16):
            dma("pool", WB_bf[l][:, kc, :], WB[l, kc * 128:(kc + 1) * 128, :], (), [b])
        for kv in range(2):
            dma("pool", W1_bf[l][kv], W1[l, kv], (), [b])

    def rmsnorm_block(vcol):
        ps, Bp = pb[2], B_pb[2]
        for c in range(8):
            t, Bt = tmp()
            tt("pool", t[:, 0:TB], xT[:, c, :], xT[:, c, :], ALU.mult, [B_x[c]], [Bt])
            mm(ps[:, 0:TB], ones_d[:], t[:, 0:TB], c == 0, c == 7, [Bt, B_od], [Bp])
        r, Br = tmp()
        rsqrt_eps(r[:, 0:TB], ps[:, 0:TB], [Bp], [Br])
        for c in range(8):
            stt("dve", xn[:, c, :], xT[:, c, :], vec[:, vcol + c:vcol + c + 1], r[:, 0:TB], ALU.mult, ALU.mult,
                [B_x[c], B_vec, Br], [B_xn[c]])

    def ffn_block(l, og, ou, od):
        for jg in range(0, 22, 4):
            nj = min(4, 22 - jg)
            wg, Bwg = wload(WA_bf[l][:, :, og + jg * 128: og + (jg + nj) * 128], 8, nj * 128)
            wu, Bwu = wload(WA_bf[l][:, :, ou + jg * 128: ou + (jg + nj) * 128], 8, nj * 128)
            for jj in range(nj):
                j = jg + jj
                pg, Bg = pb[0], B_pb[0]
                pu, Bu = pb[1], B_pb[1]
                for kc in range(8):
                    mm(pg[:, 0:TB], wg[:, kc, jj * 128:(jj + 1) * 128], xn[:, kc, :], kc == 0, kc == 7,
                       [Bwg, B_xn[kc]], [Bg])
                for kc in range(8):
                    mm(pu[:, 0:TB], wu[:, kc, jj * 128:(jj + 1) * 128], xn[:, kc, :], kc == 0, kc == 7,
                       [Bwu, B_xn[kc]], [Bu])
                s, Bs = tmp()
                act(s[:, 0:TB], pg[:, 0:TB], AF.Silu, [Bg], [Bs])
                tt("dve", hT[:, j, :], s[:, 0:TB], pu[:, 0:TB], ALU.mult, [Bs, Bu], [B_h[j]])
        for mg in range(0, 8, 1):
            wd, Bwd = wload(WD_bf[l][:, :, od + mg * 128: od + (mg + 1) * 128], 22, 128)
            for m2 in range(1):
                m = mg + m2
                po, Bo = mmbank()
                for j in range(22):
                    mm(po[:, 0:TB], wd[:, j, m2 * 128:(m2 + 1) * 128], hT[:, j, :], j == 0, j == 21,
                       [Bwd, B_h[j]], [Bo])
                stt("dve", xT[:, m, :], po[:, 0:TB], 0.5, xT[:, m, :], ALU.mult, ALU.add, [Bo, B_x[m]], [B_x[m]])

    def headnorm(ps, Bp, gcol, dest, Bdest):
        q, Bq = tmp()
        cp("act", q[:, 0:TB], ps[:, 0:TB], [Bp], [Bq])
        s, Bs = tmp()
        tt("pool", s[:, 0:TB], q[:, 0:TB], q[:, 0:TB], ALU.mult, [Bq], [Bs])
        p2, Bp2 = pb[2], B_pb[2]
        mm(p2[:, 0:TB], ones_bd[:], s[:, 0:TB], True, True, [Bs, B_obd], [Bp2])
        r, Br = tmp()
        rsqrt_eps(r[:, 0:TB], p2[:, 0:TB], [Bp2], [Br])
        stt("dve", dest, q[:, 0:TB], vec[:, gcol:gcol + 1], r[:, 0:TB], ALU.mult, ALU.mult, [Bq, B_vec, Br], Bdest)

    def rotary(ps, Bp, dest, Bdest):
        q, Bq = tmp()
        cp("act", q[:, 0:TB], ps[:, 0:TB], [Bp], [Bq])
        p2, Bp2 = pb[2], B_pb[2]
        mm(p2[:, 0:TB], rm_sb[:], q[:, 0:TB], True, True, [Bq, B_rm], [Bp2])
        a, Ba = tmp()
        tt("pool", a[:, 0:TB], q[:, 0:TB], cosb[:], ALU.mult, [Bq, B_cs], [Ba])
        b_, Bb = tmp()
        tt("dve", b_[:, 0:TB], p2[:, 0:TB], sinb[:], ALU.mult, [Bp2, B_cs], [Bb])
        tt("dve", dest, a[:, 0:TB], b_[:, 0:TB], ALU.add, [Ba, Bb], Bdest)

    def attend(g, qbuf, Bq, tcol, ktiles, onorm_dest_idx, sink=False):
        rows = slice(64 * g, 64 * g + 64)
        drows = slice(64 * (1 - g), 64 * (1 - g) + 64)
        po, Bo = pb[6], B_pb[6]
        qv = qbuf[:, g, :, tcol:tcol + 128]
        n = len(ktiles)
        pend = None
        for idx, (kl, Bk, va, Bv, masks) in enumerate(ktiles):
            sp_, Bs = pb[4 + idx % 2], B_pb[4 + idx % 2]
            mm(sp_[:].rearrange("p (a b) -> p a b", a=4), kl, qv, True, len(masks) == 0, [Bk, Bq], [Bs])
            for mi, (ml, mr, mb) in enumerate(masks):
                mm(sp_[:].rearrange("p (a b) -> p a b", a=4), ml, mr, False, mi == len(masks) - 1, mb, [Bs])
            pt, Bpt = ptmp()
            act(pt[:], sp_[:], AF.Exp, [Bs], [Bpt], scale=0.125)
            if pend is not None:
                pidx, ppt, pBpt, pva, pBv = pend
                mm(po[:], pva, ppt[:], pidx == 0, False, [pBv, pBpt], [Bo])
                yield ppt, pBpt
            pend = (idx, pt, Bpt, va, Bv)
        pidx, ppt, pBpt, pva, pBv = pend
        mm(po[:], pva, ppt[:], pidx == 0, True, [pBv, pBpt], [Bo])
        yield ppt, pBpt
        rd, Brd = tmp()
        if sink:
            for r in range(4):
                recip_add(rd[rows, r * 128:(r + 1) * 128], po[drows, r * 128:(r + 1) * 128],
                          esink[rows, r:r + 1], [Bo, B_esink], [Brd])
        else:
            recip_add(rd[rows, :], po[drows, :], 1e-30, [Bo], [Brd])
        tt("dve", obr[onorm_dest_idx][rows, :], po[rows, :], rd[rows, :], ALU.mult, [Bo, Brd], [B_obr[onorm_dest_idx]])

    def bc4(ap2d):
        return ap2d.unsqueeze(1).to_broadcast([ap2d.shape[0], 4, 128])

    bxs = sb("bxs", [128, 4, TB], F32); B_bx = [Buf() for _ in range(4)]
    bbs = sb("bbs", [128, 4, TB], F32); B_bb = [Buf() for _ in range(4)]
    hc = sb("hc", [128, 128], F32); B_hc = Buf()
    hc2 = sb("hc2", [128, 128], F32); B_hc2 = Buf()
    gk = sb("gk", [128, 2, 128], BF16); B_gk = Buf()
    gpad = sb("gpad", [128, 2, 2, 2, 128], BF16); B_gpad = Buf()
    osb = sb("osb", [128, 512], F32); B_osb = Buf()
    macc = sb("macc", [128, 4, TB], F32); B_macc = [Buf() for _ in range(4)]

    def headnorm_w(ps_ap, Bp, gcol, dest, Bdest, W):
        q, Bq = tmp()
        cp("act", q[:, 0:W], ps_ap, [Bp], [Bq])
        s_, Bs = tmp()
        tt("pool", s_[:, 0:W], q[:, 0:W], q[:, 0:W], ALU.mult, [Bq], [Bs])
        p2, Bp2 = pb[2], B_pb[2]
        mm(p2[:, 0:W], ones_bd[:], s_[:, 0:W], True, True, [Bs, B_obd], [Bp2])
        r, Br = tmp()
        rsqrt_eps(r[:, 0:W], p2[:, 0:W], [Bp2], [Br])
        if isinstance(dest, tuple):
            for g_ in range(2):
                rw = slice(64 * g_, 64 * g_ + 64)
                stt("dve", dest[g_][rw, :], q[rw, 0:W], vec[rw, gcol:gcol + 1], r[rw, 0:W], ALU.mult, ALU.mult,
                    [Bq, B_vec, Br], Bdest)
        else:
            stt("dve", dest, q[:, 0:W], vec[:, gcol:gcol + 1], r[:, 0:W], ALU.mult, ALU.mult, [Bq, B_vec, Br], Bdest)

    memset("dve", hc[:], 0.0, [B_hc])

    import os as _os2
    _katt = _os2.environ.get("KATT", "ABCDEF")

    _kret = int(_os2.environ.get("KRET", "9"))

    def _rl(n):
        return n <= _kret

    def _en(t):
        return t in _katt

    def run_att(gen):
        for _ in gen:
            pass

    for l in range(L):
        convert_weights(l)
    for l in range(L):
        B_wscr_cur[0] = B_wscr[l]
        src = xT_in if l == 0 else xs[l - 1]
        dst = outT if l == L - 1 else xs[l]
        srcv = src.rearrange("(c p) t -> p c t", p=128)
        dstv = dst.rearrange("(c p) t -> p c t", p=128)
        dma("sp", vec[:], vec_in[l], (), [B_vec])
        sk, Bsk = tmp()
        dma("sp", sk[:, 0:4], sinks_in[l], (), [Bsk])
        act(esink[:], sk[:, 0:4], AF.Exp, [Bsk], [B_esink])
        dma("pool", w2k_sb[:].rearrange("p g c m -> p (g c) m"),
            W2K[l].rearrange("g (c p) m -> p (g c) m", p=128), (), [B_w2])
        dma("pool", w2v_sb[:], W2V[l].rearrange("(c p) m -> p c m", p=128), (), [B_w2])
        dma("pool", peT_sb[:], peT[l].rearrange("k p n -> p k n"), (), [B_pe])
        memset("pool", kcmpT[:], 0.0, [B_kcmp])
        memset("pool", vcmpA[:], 0.0, [B_vcmp])
        memset("pool", vcmpA[:, :, 64:128], 1.0, [B_vcmp])
        memset("pool", kcC[:], 0.0, [B_kcC])
        memset("pool", vcC[:], 0.0, [B_vcC])
        memset("pool", zC[:], 0.0, B_z)
        memset("pool", Rst[:], 0.0, [B_R])
        memset("pool", Rbf[:], 0.0, [B_Rbf])
        memset("pool", vsA[:, :, 64:128], 1.0, B_vs)
        memset("pool", avR[:, :, 64:128], 1.0, B_avR)
        memset("pool", vwR[:, :, 64:128], 1.0, B_vwR)
        for kv in range(2):
            for ch in range(2):
                for half in range(2):
                    w1, Bw1 = wload(W1_bf[l][kv][:, half * 4096:(half + 1) * 4096]
                                    .rearrange("p (a b) -> p a b", a=16)[:, :, ch * 128:(ch + 1) * 128], 16, 128)
                    for li in range(16):
                        lg_ = half * 16 + li
                        col = kv * 2 + ch
                        mm(pb[3][:, col:col + 1], w1[0:64, li, :], peT_sb[0:64, kv, lg_:lg_ + 1],
                           lg_ == 0, lg_ == 31, [Bw1, B_pe], [B_pb[3]])
        cp("dve", cbias[:].rearrange("p a b -> p (a b)"), pb[3][:, 0:4], [B_pb[3]], [B_cbias])

        for blk in range(NB):
            t0 = blk * TB
            dma("sp", xT[:], srcv[:, :, t0:t0 + TB], [B_xs[l - 1][blk]] if l > 0 else [], B_x)
            dma("sp", cosb[:], cin["cosT"][:, t0:t0 + TB], (), [B_cs])
            dma("sp", sinb[:], cin["sinT"][:, t0:t0 + TB], (), [B_cs])
            rmsnorm_block(V_N1)
            ffn_block(l, OFF_F1G, OFF_F1U, 0)
            if stop == "ffn1":
                dma("sp", dstv[:, :, t0:t0 + TB], xT[:], B_x, [B_xs[min(l, len(B_xs) - 1)][blk]])
                continue
            rmsnorm_block(V_NM)

            def wgroup(c0, ncols):
                return wload(WA_bf[l][:, :, OFF_WIN + c0: OFF_WIN + c0 + ncols], 8, ncols)

            def proj(w, Bw, off, M=128):
                ps, Bp = mmbank()
                for kc in range(8):
                    mm(ps[0:M, 0:TB], w[:, kc, off:off + M], xn[:, kc, :], kc == 0, kc == 7, [Bw, B_xn[kc]], [Bp])
                return ps, Bp
            slots = [(TPB * blk + ti) % RING for ti in range(TPB)]
            handlers = []
            for r in range(4):
                handlers.append(lambda ps, Bp, r=r: headnorm_w(ps[:, 0:TB], Bp, V_AQG, (aqT[:, 0, r, :], aqT[:, 1, r, :]), [B_aq], TB))

            def h_ring(ps, Bp, gcol, ring, Bring):
                tk, Btk = ptmp()
                headnorm_w(ps[:, 0:TB], Bp, gcol, tk[:, 0:TB], [Btk], TB)
                for ti in range(TPB):
                    cp("pool", ring[:, slots[ti], :], tk[:, ti * 128:(ti + 1) * 128], [Btk], [Bring[slots[ti]]])
            handlers.append(lambda ps, Bp: h_ring(ps, Bp, V_AKG, akR, B_akR))
            for r in range(4):
                handlers.append(lambda ps, Bp, r=r: headnorm_w(ps[:, 0:TB], Bp, V_CQG, (cqT[:, 0, r, :], cqT[:, 1, r, :]), [B_cq], TB))
            handlers.append(lambda ps, Bp: headnorm_w(ps[:, 0:TB], Bp, V_CKG + 1, ksT[:, t0:t0 + TB],
                                                       B_ks[blk * TPB:(blk + 1) * TPB], TB))
            handlers.append(lambda ps, Bp: h_ring(ps, Bp, V_CKG + 2, kwR, B_kwR))
            handlers.append(lambda ps, Bp: cp("act", kcC[:, 16:16 + TB], ps[:, 0:TB], [Bp], [B_kcC]))
            handlers.append(lambda ps, Bp: cp("act", vcC[:, 16:16 + TB], ps[:, 0:TB], [Bp], [B_vcC]))
            for pp in range(2):
                handlers.append(lambda ps, Bp, pp=pp: rotary(ps, Bp, dqT[:, pp, :], [B_dq]))
            for pp in range(2):
                handlers.append(lambda ps, Bp, pp=pp: rotary(ps, Bp, dkT[:, pp, :], [B_dk]))
            for h in range(4):
                handlers.append(lambda ps, Bp, h=h: act(sgd[:, h, :], ps[:, 0:TB], AF.Silu, [Bp], [B_sgd]))
            for c in range(4):
                handlers.append(lambda ps, Bp, c=c: cp("act", bxs[:, c, :], ps[:, 0:TB], [Bp], [B_bx[c]]))
            for c in range(4):
                handlers.append(lambda ps, Bp, c=c: cp("act", bbs[:, c, :], ps[:, 0:TB], [Bp], [B_bb[c]]))
            for c in range(4):
                handlers.append(lambda ps, Bp, c=c: tt("dve", zC[:, c, 2:2 + TB], ps[:, 0:TB], bxs[:, c, :], ALU.mult,
                                                       [Bp, B_bx[c]], [B_z[c]]))
            assert len(handlers) == NFM
            dfr = [None]
            for g0 in range(0, NFM, 4):
                ng = min(4, NFM - g0)
                ncols = ng * 128 + (24 if g0 + ng == NFM else 0)
                w, Bw = wgroup(g0 * 128, ncols)
                for k in range(ng):
                    ps, Bp = proj(w, Bw, k * 128)
                    if dfr[0] is not None:
                        dfr[0]()
                    dfr[0] = (lambda hh=handlers[g0 + k], ps=ps, Bp=Bp: hh(ps, Bp))
                if g0 + ng == NFM:
                    ps, Bp = proj(w, Bw, ng * 128, M=24)
                    dfr[0]()
                    dfr[0] = None
                    act(cgs[:], ps[0:24, 0:TB], AF.Sigmoid, [Bp], [B_cgs])
            if stop == "projfm":
                dma("sp", dstv[:, :, t0:t0 + TB], xT[:], B_x, [B_xs[min(l, len(B_xs) - 1)][blk]])
                continue
            wtm1, Bwtm1 = wload(WA_bf[l][:, :, OFF_WIN + OFF_TM: OFF_WIN + OFF_TM + 384], 8, 384)
            wtm2, Bwtm2 = wload(WA_bf[l][:, :, OFF_WIN + OFF_TM + 384: OFF_WIN + OFF_TM + 896], 8, 512)
            for ti in range(TPB):
                tile_i = blk * TPB + ti
                ps, Bp = mmbank()
                for kc in range(8):
                    mm(ps[:, 0:384], xn[:, kc, ti * 128:(ti + 1) * 128], wtm1[:, kc, :], kc == 0, kc == 7,
                       [Bwtm1, B_xn[kc]], [Bp])

                def vput(dst3, off, ps=ps):
                    return (dst3.rearrange("p (a b) -> p a b", a=3)[:, 0:3:2, :],
                            ps[:, off:off + 128].rearrange("p (a b) -> p a b", a=2))
                for (dst3, off, Bd, e0, e1) in ((avR[:, slots[ti], :], 0, B_avR[slots[ti]], "act", "dve"),
                                                (vsA[:, tile_i, :], 128, B_vs[tile_i], "dve", "act"),
                                                (vwR[:, slots[ti], :], 256, B_vwR[slots[ti]], "act", "dve")):
                    cp(e0, dst3[:, 0:64], ps[:, off:off + 64], [Bp], [Bd])
                    cp(e1, dst3[:, 128:192], ps[:, off + 64:off + 128], [Bp], [Bd])
                ps, Bp = mmbank()
                for kc in range(8):
                    mm(ps[:, :], xn[:, kc, ti * 128:(ti + 1) * 128], wtm2[:, kc, :], kc == 0, kc == 7,
                       [Bwtm2, B_xn[kc]], [Bp])
                cp("act", dvR[:, ti, :], ps[:, :], [Bp], [B_dvR[ti]])

            if stop == "projtm":
                dma("sp", dstv[:, :, t0:t0 + TB], xT[:], B_x, [B_xs[min(l, len(B_xs) - 1)][blk]])
                continue
            for c in range(4):
                a, Ba = tmp()
                ts("pool", a[:, 0:TB], zC[:, c, 0:TB], vec[:, V_CW + c:V_CW + c + 1], None, ALU.mult, None,
                   [B_z[c], B_vec], [Ba])
                stt("dve", a[:, 0:TB], zC[:, c, 1:1 + TB], vec[:, V_CW + 4 + c:V_CW + 5 + c], a[:, 0:TB],
                    ALU.mult, ALU.add, [B_z[c], B_vec, Ba], [Ba])
                stt("dve", a[:, 0:TB], zC[:, c, 2:2 + TB], vec[:, V_CW + 8 + c:V_CW + 9 + c], a[:, 0:TB],
                    ALU.mult, ALU.add, [B_z[c], B_vec, Ba], [Ba])
                tt("pool", yT[:, 1, c, :], a[:, 0:TB], bbs[:, c, :], ALU.mult, [Ba, B_bb[c]], [B_y[1][c]])
                cp("pool", zC[:, c, 0:2], zC[:, c, TB:TB + 2], [B_z[c]], [B_z[c]])

            if stop == "conv":
                dma("sp", dstv[:, :, t0:t0 + TB], xT[:], B_x, [B_xs[min(l, len(B_xs) - 1)][blk]])
                continue
            import os as _os
            if _os.environ.get("KSKIP", "") != "cmp":
                per = TB // 16
                n_lo = 0 if blk == 0 else per * blk - 1
                n_hi = per * (blk + 1) - 2
                nn = n_hi - n_lo + 1
                cst = 16 if blk == 0 else 0
                pieces = []
                n_ = n_lo
                while n_ <= n_hi:
                    e_ = min(n_hi, (n_ // 128) * 128 + 127)
                    pieces.append((n_ // 128, n_, e_ - n_ + 1))
                    n_ = e_ + 1
                for kv, (car, Bcar) in enumerate(((kcC, B_kcC), (vcC, B_vcC))):
                    for ch in range(2):
                        for half in range(2):
                            w1, Bw1 = wload(W1_bf[l][kv][:, half * 4096:(half + 1) * 4096]
                                            .rearrange("p (a b) -> p a b", a=16)[:, :, ch * 128:(ch + 1) * 128], 16, 128)
                            for li in range(16):
                                lg_ = half * 16 + li
                                for g in range(2):
                                    rows = slice(64 * g, 64 * g + 64)
                                    rhs = car[rows, cst + lg_: cst + lg_ + 16 * (nn - 1) + 1: 16]
                                    bk = 3 if g == 0 else 2
                                    mm(pb[bk][:, 0:nn], w1[rows, li, :], rhs, lg_ == 0, lg_ == 31,
                                       [Bw1, Bcar], [B_pb[bk]])
                        ts("dve", hc[:, 0:nn], pb[3][:, 0:nn], cbias[:, kv, ch:ch + 1], None, ALU.add, None,
                           [B_pb[3], B_cbias], [B_hc])
                        ts("dve", hc[:, 64:64 + nn], pb[2][:, 0:nn], cbias[:, kv, ch:ch + 1], None, ALU.add, None,
                           [B_pb[2], B_cbias], [B_hc])
                        tt("pool", hc2[:], hc[:], hc[:], ALU.mult, [B_hc], [B_hc2])
                        ts("pool", hc2[:], hc2[:], 0.044715, 1.0, ALU.mult, ALU.add, [B_hc2], [B_hc2])
                        tt("pool", hc2[:], hc2[:], hc[:], ALU.mult, [B_hc2, B_hc], [B_hc2])
                        act(hc2[:], hc2[:], AF.Tanh, [B_hc2], [B_hc2], scale=0.7978845608028654)
                        ts("pool", hc2[:], hc2[:], 1.0, 0.5, ALU.add, ALU.mult, [B_hc2], [B_hc2])
                        if kv == 0:
                            tt("pool", gk[:, ch, :], hc2[:], hc[:], ALU.mult, [B_hc2, B_hc], [B_gk])
                        else:
                            if ch == 0:
                                memset("pool", gpad[:].rearrange("p a b c d -> p (a b c d)"), 0.0, [B_gpad])
                            for pi, (ptile, pn, pcnt) in enumerate(pieces):
                                for g in range(2):
                                    o0 = g * 64 + (pn - n_lo)
                                    tt("pool", gpad[:, pi, g, ch, pn % 128: pn % 128 + pcnt],
                                       hc2[:, o0:o0 + pcnt], hc[:, o0:o0 + pcnt], ALU.mult, [B_hc2, B_hc], [B_gpad])
                cp("pool", kcC[:, 0:16], kcC[:, TB:TB + 16], [B_kcC], [B_kcC])
                cp("pool", vcC[:, 0:16], vcC[:, TB:TB + 16], [B_vcC], [B_vcC])
                first = True
                for ch in range(2):
                    for g in range(2):
                        mm(pb[3][:, 0:nn], w2k_sb[:, g, ch, :], gk[:, ch, g * 64: g * 64 + nn], first, ch == 1 and g == 1,
                           [B_w2, B_gk], [B_pb[3]])
                        first = False
                headnorm_w(pb[3][:, 0:nn], B_pb[3], V_CKG + 0, kcmpT[:, n_lo:n_lo + nn], [B_kcmp], nn)
                for pi, (ptile, pn, pcnt) in enumerate(pieces):
                    for g in range(2):
                        for ch in range(2):
                            mm(pb[3][:, 256 + g * 64: 256 + g * 64 + 64], gpad[:, pi, g, ch, :], w2v_sb[:, ch, :],
                               ch == 0, ch == 1, [B_gpad, B_w2], [B_pb[3]])
                    tt("dve", vcmpA[:, ptile, 0:64], vcmpA[:, ptile, 0:64], pb[3][:, 256:320], ALU.add,
                       [B_pb[3], B_vcmp], [B_vcmp])
                    tt("dve", vcmpA[:, ptile, 128:192], vcmpA[:, ptile, 128:192], pb[3][:, 320:384], ALU.add,
                       [B_pb[3], B_vcmp], [B_vcmp])


            if stop == "cmp":
                dma("sp", dstv[:, :, t0:t0 + TB], xT[:], B_x, [B_xs[min(l, len(B_xs) - 1)][blk]])
                continue
            for ti in range(TPB):
                i = blk * TPB + ti
                tc = ti * 128
                if _en('A'):
                    for g in range(2):
                        rows = slice(64 * g, 64 * g + 64)
                        kts = []
                        for kt in ([i - 1, i] if i >= 1 else [i]):
                            s_ = kt % RING
                            msk = mcur_sb if kt == i else mprev_sb
                            kts.append((akR[:, s_, :], B_akR[s_], avR[:, s_, 64 * g: 64 * g + 128], B_avR[s_],
                                        [(ident_bf[:], bc4(msk[:]), [B_ident, B_mcur, B_mprev])]))
                        run_att(attend(g, aqT, B_aq, tc, kts, 0, sink=True))
                    for r in range(4):
                        cp("act", yT[:, 0, r, tc:tc + 128], obr[0][:, r * 128:(r + 1) * 128], [B_obr[0]], [B_y[0][r]])
                if _en('B'):
                    for g in range(2):
                        rows = slice(64 * g, 64 * g + 64)
                        kts = []
                        for kt in range(max(0, i - 4), i + 1):
                            s_ = kt % RING
                            masks = []
                            if kt == i:
                                masks = [(ident_bf[:], bc4(mcur_sb[:]), [B_ident, B_mcur])]
                            elif kt == i - 4:
                                masks = [(ident_bf[:], bc4(mprev_sb[:]), [B_ident, B_mprev])]
                            kts.append((kwR[:, s_, :], B_kwR[s_], vwR[:, s_, 64 * g: 64 * g + 128], B_vwR[s_], masks))
                        run_att(attend(g, cqT, B_cq, tc, kts, 2))
                if _en('C'):
                    n_ct = min(NCT, i // 16 + 1)
                    for g in range(2):
                        rows = slice(64 * g, 64 * g + 64)
                        kts = []
                        for c in range(n_ct):
                            dl = i - 16 * c
                            masks = []
                            if dl <= 16:
                                masks = [(ident_bf[:], bc4(cm_sb[:, dl * 128:(dl + 1) * 128]), [B_ident, B_cm])]
                            kts.append((kcmpT[:, c * 128:(c + 1) * 128], B_kcmp, vcmpA[:, c, 64 * g: 64 * g + 128],
                                        B_vcmp, masks))
                        ip, Bip = pb[3], B_pb[3]
                        dp, Bdp = pb[2], B_pb[2]
                        pts = list(attend(g, cqT, B_cq, tc, kts, 0))
                        for r in range(4):
                            for c, (pt, Bpt) in enumerate(pts):
                                mm(ip[:, r * 128:(r + 1) * 128], pt[:, r * 128:(r + 1) * 128], ovl_sb[:, c * 128:(c + 1) * 128],
                                   c == 0, c == n_ct - 1, [Bpt, B_ovl], [Bip])
                        for r in range(4):
                            for c, (pt, Bpt) in enumerate(pts):
                                mm(dp[:, r:r + 1], pt[:, r * 128:(r + 1) * 128], ones_col[:], c == 0, c == n_ct - 1,
                                   [Bpt, B_onescol], [Bdp])
                        recip_add(rdt[:], dp[:, 0:4], 1e-30, [Bdp], [B_rdt])
                        ts("dve", impw[:], ip[:, 0:128], rdt[:, 0:1], None, ALU.mult, None, [Bip, B_rdt], [B_impw])
                        for r in range(1, 4):
                            stt("dve", impw[:], ip[:, r * 128:(r + 1) * 128], rdt[:, r:r + 1], impw[:], ALU.mult, ALU.add,
                                [Bip, B_rdt, B_impw], [B_impw])
                        tt("dve", impw[:], impw[:], fb_sb[:, 128 - 2 * i: 256 - 2 * i], ALU.add, [B_impw, B_fb], [B_impw])
                        tt("dve", impb[:], impw[:], f0_sb[:], ALU.add, [B_impw, B_f0], [B_imp])
                        P.op("dve", lambda e: e.max(out=m8[:, 0:8], in_=impb[:]), [B_imp], [B_m8])
                        P.op("dve", lambda e: e.match_replace(out=impw[:], in_to_replace=m8[:, 0:8], in_values=impb[:],
                                                              imm_value=-3.0e4), [B_imp, B_m8], [B_impw])
                        P.op("dve", lambda e: e.max(out=m8[:, 8:16], in_=impw[:]), [B_impw], [B_m8])
                        ts("dve", selb[:], impb[:], m8[:, 15:16], NEG, ALU.is_lt, ALU.mult, [B_imp, B_m8], [B_selb])
                        tr(pbt[:, g * 128:(g + 1) * 128], selb[:], ident_bf[:], [B_selb, B_ident], [B_pbt])
                        cp("act", selbT[0:64, g, 0, :], pbt[0:64, g * 128:(g + 1) * 128], [B_pbt], [B_selbT[g]])
                        cp("act", selbT[64:128, g, 1, :], pbt[64:128, g * 128:(g + 1) * 128], [B_pbt], [B_selbT[g]])
                if _en('D'):
                    for g in range(2):
                        rows = slice(64 * g, 64 * g + 64)
                        kts = []
                        for kt in range(0, i + 1):
                            hf = 0 if kt < 32 else 1
                            masks = [(E_sb[:, (kt % 32) * 128:(kt % 32 + 1) * 128],
                                      bc4(selbT[:, g, hf, :]), [B_E, B_selbT[g]])]
                            if kt == i:
                                masks.append((ident_bf[:], bc4(mcur_sb[:]), [B_ident, B_mcur]))
                            kts.append((ksT[:, kt * 128:(kt + 1) * 128], B_ks[kt], vsA[:, kt, 64 * g: 64 * g + 128],
                                        B_vs[kt], masks))
                        run_att(attend(g, cqT, B_cq, tc, kts, 1))
                if _en('E'):
                    gp, Bgp = pb[3], B_pb[3]
                    for br, srcb, Bsrc in ((0, obr[0], B_obr[0]), (1, obr[1], B_obr[1]), (2, obr[2], B_obr[2])):
                        for r in range(4):
                            mm(gp[:, r * 128:(r + 1) * 128], selg_sb[:, (br * 4 + r) * 128:(br * 4 + r + 1) * 128],
                               cgs[:, tc:tc + 128], True, True, [B_selg, B_cgs], [Bgp])
                        if br == 0:
                            tt("dve", osb[:], gp[:], srcb[:], ALU.mult, [Bgp, Bsrc], [B_osb])
                        else:
                            t_, Bt_ = tmp()
                            tt("dve", t_[:], gp[:], srcb[:], ALU.mult, [Bgp, Bsrc], [Bt_])
                            tt("pool", osb[:], osb[:], t_[:], ALU.add, [B_osb, Bt_], [B_osb])
                    for r in range(4):
                        cp("act", yT[:, 2, r, tc:tc + 128], osb[:, r * 128:(r + 1) * 128], [B_osb], [B_y[2][r]])

                if _en('F'):
                    ap_, Bap = pb[4], B_pb[4]
                    for h in range(4):
                        pp, hh = h // 2, h % 2
                        rows = slice(64 * hh, 64 * hh + 64)
                        mm(pb[4 + hh][:, pp * 128:(pp + 1) * 128], dkT[rows, pp, tc:tc + 128], dqT[rows, pp, tc:tc + 128],
                           True, True, [B_dk, B_dq], [B_pb[4 + hh]])
                    if _rl(1):
                        am, Bam = ptmp()
                        for h in range(4):
                            pp, hh = h // 2, h % 2
                            tt("dve", am[:, h * 128:(h + 1) * 128], pb[4 + hh][:, pp * 128:(pp + 1) * 128],
                               dmask_sb[:, h * 128:(h + 1) * 128], ALU.mult, [B_pb[4 + hh], B_dmask], [Bam])

                    if _rl(2):
                        for pp in range(2):
                            tt("dve", qxi[:, pp, tc:tc + 128], dqT[:, pp, tc:tc + 128], xi_sb[:, pp * 128:(pp + 1) * 128],
                               ALU.mult, [B_dq, B_xi], [B_qxi])

                    if _rl(3):
                        op_, Bop = pb[6], B_pb[6]
                        for h in range(4):
                            pp, hh = h // 2, h % 2
                            rows = slice(64 * hh, 64 * hh + 64)
                            mm(op_[:, h * 128:(h + 1) * 128], dvR[:, ti, h * 128:(h + 1) * 128], am[:, h * 128:(h + 1) * 128],
                               True, False, [B_dvR[ti], Bam], [Bop])
                            mm(op_[:, h * 128:(h + 1) * 128], Rbf[rows, pp, hh * 128:(hh + 1) * 128], qxi[rows, pp, tc:tc + 128],
                               False, True, [B_Rbf, B_qxi], [Bop])

                    if _rl(4):
                        for pp in range(2):
                            tr(pbt[:, 256 + pp * 128: 256 + (pp + 1) * 128], dkT[:, pp, tc:tc + 128], ident_bf[:],
                               [B_dk, B_ident], [B_pbt])
                        tt("dve", kz[:].rearrange("p a b -> p (a b)"), pbt[:, 256:512], zt_sb[:], ALU.mult, [B_pbt, B_zt], [B_kz])

                    if _rl(5):
                        sp2, Bsp2 = pb[5], B_pb[5]
                        for pp in range(2):
                            mm(sp2[:, pp * 256:(pp + 1) * 256], kz[:, pp, :], dvR[:, ti, pp * 256:(pp + 1) * 256], True, True,
                               [B_kz, B_dvR[ti]], [Bsp2])
                        for pp in range(2):
                            stt("dve", Rst[:, pp, :], Rst[:, pp, :], dec_sb[:, pp:pp + 1], sp2[:, pp * 256:(pp + 1) * 256],
                                ALU.mult, ALU.add, [B_R, B_dec, Bsp2], [B_R])

                    if _rl(6):
                        cp("act", Rbf[:].rearrange("p a b -> p (a b)"), Rst[:].rearrange("p a b -> p (a b)"), [B_R], [B_Rbf])

                    if _rl(7):
                        o2, Bo2 = tmp()
                        cp("act", o2[:], op_[:], [Bop], [Bo2])
                        mp, Bmp = pb[2], B_pb[2]
                        mm(mp[:], ones_v[:], o2[:], True, True, [Bo2, B_ov], [Bmp])
                        c2, Bc2 = tmp()
                        tt("dve", c2[:], o2[:], mp[:], ALU.subtract, [Bo2, Bmp], [Bc2])

                    if _rl(8):
                        s2, Bs2 = tmp()
                        tt("pool", s2[:], c2[:], c2[:], ALU.mult, [Bc2], [Bs2])
                        mm(mp[:], ones_v[:], s2[:], True, True, [Bs2, B_ov], [Bmp])
                        r2, Br2 = tmp()
                        rsqrt_eps(r2[:], mp[:], [Bmp], [Br2])
                        tt("pool", c2[:], c2[:], r2[:], ALU.mult, [Bc2, Br2], [Bc2])
                        for h in range(4):
                            stt("dve", yT[:, 3, h, tc:tc + 128], c2[:, h * 128:(h + 1) * 128], vec[:, V_RG + h:V_RG + h + 1],
                                sgd[:, h, tc:tc + 128], ALU.mult, ALU.mult, [Bc2, B_vec, B_sgd], [B_y[3][h]])


                if stop == "att":
                    dma("sp", dstv[:, :, t0:t0 + TB], xT[:], B_x, [B_xs[min(l, len(B_xs) - 1)][blk]])
                    continue
            for m0 in range(0, 8, 4):
                for bi in range(4):
                    wbv, Bwb = wload(WB_bf[l][:, bi * 4:(bi + 1) * 4, m0 * 128:(m0 + 4) * 128], 4, 512)
                    gvw, Bgv = wload(WA_bf[l][:, :, OFF_WIN + OFF_GATES + bi * 1024 + m0 * 128:
                                              OFF_WIN + OFF_GATES + bi * 1024 + (m0 + 4) * 128], 8, 512)
                    for m in range(m0, m0 + 4):
                        mo = (m - m0) * 128
                        pg, Bg = pb[0], B_pb[0]
                        for kc in range(8):
                            mm(pg[:, 0:TB], gvw[:, kc, mo:mo + 128], xn[:, kc, :], kc == 0, kc == 7, [Bgv, B_xn[kc]], [Bg])
                        gs, Bgs = tmp()
                        act(gs[:, 0:TB], pg[:, 0:TB], AF.Sigmoid, [Bg, B_vec], [Bgs],
                            bias=vec[:, V_GB + bi * 8 + m: V_GB + bi * 8 + m + 1])
                        pbk, Bbk = pb[1], B_pb[1]
                        for c in range(4):
                            mm(pbk[:, 0:TB], wbv[:, c, mo:mo + 128], yT[:, bi, c, :], c == 0, c == 3,
                               [Bwb, B_y[bi][c]], [Bbk])
                        if bi == 0:
                            tt("dve", macc[:, m - m0, :], pbk[:, 0:TB], gs[:, 0:TB], ALU.mult, [Bbk, Bgs], [B_macc[m - m0]])
                        else:
                            t_, Bt_ = tmp()
                            tt("dve", t_[:, 0:TB], pbk[:, 0:TB], gs[:, 0:TB], ALU.mult, [Bbk, Bgs], [Bt_])
                            if bi < 3:
                                tt("pool", macc[:, m - m0, :], macc[:, m - m0, :], t_[:, 0:TB], ALU.add,
                                   [B_macc[m - m0], Bt_], [B_macc[m - m0]])
                            else:
                                tt("pool", mrg[:, m, :], macc[:, m - m0, :], t_[:, 0:TB], ALU.add,
                                   [B_macc[m - m0], Bt_], [B_mrg[m]])
            for m0 in range(0, 8, 4):
                wo, Bwo = wload(WA_bf[l][:, :, OFF_WO + m0 * 128: OFF_WO + (m0 + 4) * 128], 8, 512)
                for m in range(m0, m0 + 4):
                    po, Bo = mmbank()
                    for kc in range(8):
                        mm(po[:, 0:TB], wo[:, kc, (m - m0) * 128:(m - m0 + 1) * 128], mrg[:, kc, :], kc == 0, kc == 7,
                           [Bwo, B_mrg[kc]], [Bo])
                    tt("dve", xT[:, m, :], xT[:, m, :], po[:, 0:TB], ALU.add, [B_x[m], Bo], [B_x[m]])
            if stop != "mix":
                rmsnorm_block(V_N2)
                ffn_block(l, OFF_F2G, OFF_F2U, 1024)
            dma("sp", dstv[:, :, t0:t0 + TB], xT[:], B_x, [B_xs[min(l, len(B_xs) - 1)][blk]])

    P.emit()
    return nc, P


_CACHE = {}


def run_cores(x, inp, L, T, n_cores, TB=256, stop=None):
    key = (T, L, TB, stop)
    if key not in _CACHE:
        _CACHE[key] = build(T, L, TB, stop)
    nc, P = _CACHE[key]
    w = prep_weights(inp, L)
    cs = make_consts(T)
    in_maps = []
    for c in range(n_cores):
        m = {"xT": np.ascontiguousarray(x[c].T.astype(np.float32))}
        m.update(w)
        for k, v in cs.items():
            m["c_" + k] = np.ascontiguousarray(v)
        in_maps.append(m)
    res = run_bass_kernel_spmd(nc, in_maps, core_ids=list(range(n_cores)))
    return np.stack([np.ascontiguousarray(r["outT"].T) for r in res.results], axis=0)


def kernel(**inputs):
    x = np.asarray(inputs["x"], dtype=np.float32)
    B, T, _ = x.shape
    inp = {k: np.asarray(v, dtype=np.float32) for k, v in inputs.items() if k != "x"}
    out = run_cores(x, inp, L_FULL, T, B)
    return out.astype(np.float32)
```

```python
import contextlib
import math
import numpy as np
import concourse.bass as bass
import concourse.mybir as mybir
from concourse.bass_utils import run_bass_kernel_spmd

F32 = mybir.dt.float32
BF16 = mybir.dt.bfloat16
AF = mybir.ActivationFunctionType
ALU = mybir.AluOpType

D = 1024
DFF = 2816
L_FULL = 2
EPS = 1e-6
NEG = -32768.0
BIG = 1.0e4
IN_TOTAL = 9240
OFF_F1G, OFF_F1U, OFF_WIN = 0, 2816, 5632
OFF_WO = OFF_WIN + IN_TOTAL
OFF_F2G = OFF_WO + 1024
OFF_F2U = OFF_F2G + 2816
WA_COLS = OFF_F2U + 2816
NFM = 33
OFF_CG = NFM * 128
OFF_GATES = OFF_CG + 24
OFF_TM = OFF_GATES + 4096
(C_AQ, C_AK, C_CQ, C_CKS, C_CKW, C_CKC, C_CVC, C_DQ, C_DK, C_DG, C_BX, C_BB, C_BC) = (
    0, 4, 5, 9, 10, 11, 12, 13, 15, 17, 21, 25, 29)
V_N1, V_NM, V_N2, V_GB, V_AQG, V_AKG, V_CQG, V_CKG, V_CW, V_RG = 0, 8, 16, 24, 56, 57, 58, 59, 62, 74
NVEC = 78


class Buf:
    __slots__ = ("name", "last_w", "readers")

    def __init__(self, name=""):
        self.name = name
        self.last_w = None
        self.readers = []


class Ins:
    __slots__ = ("eng", "fn", "deps", "raw", "signal", "rank", "dma_slot", "dma_val", "is_dma", "pre_waits")

    def __init__(self, eng, fn, is_dma):
        self.eng = eng
        self.fn = fn
        self.deps = set()
        self.raw = set()
        self.signal = False
        self.rank = None
        self.is_dma = is_dma
        self.dma_slot = None
        self.dma_val = None
        self.pre_waits = []


ENGS = ("pe", "act", "dve", "pool", "sp")
N_HW_SEM = 24
N_SW_SEM = 8
N_DMA_SEM = N_HW_SEM + N_SW_SEM


class Prog:
    def __init__(self, nc):
        self.nc = nc
        self.ins = []
        self.eng_obj = {"pe": nc.tensor, "act": nc.scalar, "dve": nc.vector,
                        "pool": nc.gpsimd, "sp": nc.sync}
        self.dma_count = 0
        self.hw_count = 0
        self.sw_count = 0
        self.dma_slot_last = [None] * N_DMA_SEM

    def op(self, eng, fn, reads=(), writes=(), dma=False):
        i = Ins(eng, fn, dma)
        iid = len(self.ins)
        for b in reads:
            if b.last_w is not None:
                i.deps.add(b.last_w)
                i.raw.add(b.last_w)
        for b in writes:
            if b.last_w is not None:
                i.deps.add(b.last_w)
            for r in b.readers:
                i.deps.add(r)
        for b in reads:
            b.readers.append(iid)
        for b in writes:
            b.last_w = iid
            b.readers = []
        if dma:
            if eng == "pool":
                k = self.sw_count
                slot = N_HW_SEM + k % N_SW_SEM
                val = 16 * (k // N_SW_SEM + 1)
                self.sw_count += 1
            else:
                k = self.hw_count
                slot = k % N_HW_SEM
                val = 16 * (k // N_HW_SEM + 1)
                self.hw_count += 1
            prev = self.dma_slot_last[slot]
            if prev is not None:
                i.deps.add(prev)
            self.dma_slot_last[slot] = iid
            i.dma_slot = slot
            i.dma_val = val
            self.dma_count += 1
        i.deps.discard(iid)
        self.ins.append(i)
        return iid

    def emit(self, final_wait_eng="sp"):
        nc = self.nc
        ins = self.ins
        waited_eng = {e: {s: -1 for s in ENGS} for e in ENGS}
        waited_dma = {e: {} for e in ENGS}
        for iid, i in enumerate(ins):
            e = i.eng
            need_eng = {}
            need_dma = {}
            for d in i.deps:
                di = ins[d]
                if di.is_dma:
                    if need_dma.get(di.dma_slot, 0) < di.dma_val:
                        need_dma[di.dma_slot] = di.dma_val
                else:
                    if di.eng == e and not i.is_dma:
                        if e == "pe":
                            continue
                    if need_eng.get(di.eng, -1) < d:
                        need_eng[di.eng] = d
            for s, d in need_eng.items():
                if waited_eng[e][s] >= d:
                    continue
                waited_eng[e][s] = d
                ins[d].signal = True
                i.pre_waits.append(("eng", s, d))
            for slot, val in need_dma.items():
                if waited_dma[e].get(slot, 0) >= val:
                    continue
                waited_dma[e][slot] = val
                i.pre_waits.append(("dma", slot, val))
        rk = {e: 0 for e in ENGS}
        counts = {e: 0 for e in ENGS}
        for i in ins:
            counts[i.eng] += 1
            if i.signal and not i.is_dma:
                rk[i.eng] += 1
                i.rank = rk[i.eng]
        self.stats = dict(counts=counts, signals=dict(rk), n=len(ins), dmas=self.dma_count)
        with contextlib.ExitStack() as st:
            esem = {e: st.enter_context(nc.semaphore("s_" + e)) for e in ENGS}
            dsem = [st.enter_context(nc.semaphore("d_%d" % k)) for k in range(N_DMA_SEM)]
            for i in ins:
                eo = self.eng_obj[i.eng]
                for w in i.pre_waits:
                    if w[0] == "eng":
                        eo.wait_ge(esem[w[1]], ins[w[2]].rank)
                    else:
                        eo.wait_ge(dsem[w[1]], w[2])
                r = i.fn(eo)
                if i.is_dma:
                    r.then_inc(dsem[i.dma_slot], 16)
                elif i.signal:
                    r.then_inc(esem[i.eng], 1)
            eo = self.eng_obj[final_wait_eng]
            for slot in range(N_DMA_SEM):
                d = self.dma_slot_last[slot]
                if d is not None:
                    eo.wait_ge(dsem[slot], ins[d].dma_val)


def make_consts(T):
    NT = T // 128
    c = {}
    c["ident"] = np.eye(128, dtype=np.float32)
    bd = np.zeros((128, 128), np.float32)
    bd[:64, :64] = 1.0 / 64
    bd[64:, 64:] = 1.0 / 64
    c["ones_bd"] = bd
    c["ones_d"] = np.full((128, 128), 1.0 / 1024, np.float32)
    c["ones_v"] = np.full((128, 128), 1.0 / 128, np.float32)
    E = np.zeros((128, 32, 128), np.float32)
    for q in range(32):
        for m in range(128):
            s = 2 * q + m // 64
            E[s, q, m] = 1.0
            E[64 + s, q, m] = 1.0
    c["E"] = E.reshape(128, 32 * 128)
    n = np.arange(512)
    s = np.arange(128)
    cs = n * 16
    ce = cs + 31
    ss = s * 64
    ov = ((cs[:, None] < ss[None, :] + 64) & (ce[:, None] >= ss[None, :])).astype(np.float32)
    c["ovl"] = ov.reshape(4, 128, 128).transpose(1, 0, 2).reshape(128, 4 * 128)
    j = np.arange(128)
    e = np.arange(256)
    rel = (e[None, :] - 128) - (j[:, None] // 64)
    fb = np.zeros((128, 256), np.float32)
    fb[(rel == 0) | (rel == -1)] = BIG
    fb[rel > 0] = -BIG
    c["fb"] = fb
    f0 = np.zeros((128, 128), np.float32)
    f0[:, 0] = BIG
    c["f0"] = f0
    m = np.arange(128)
    c["mask_cur"] = np.where(m[:, None] <= j[None, :], 0.0, NEG).astype(np.float32)
    c["mask_prev"] = np.where(m[:, None] > j[None, :], 0.0, NEG).astype(np.float32)
    cm = np.zeros((128, 17, 128), np.float32)
    for dl in range(17):
        ok = (16 * m[:, None] + 31 - j[None, :]) <= 128 * dl
        cm[:, dl, :] = np.where(ok, 0.0, NEG)
    c["cm"] = cm.reshape(128, 17 * 128)
    sg = np.zeros((24, 12, 128), np.float32)
    for br in range(3):
        for r in range(4):
            for mm in range(128):
                h = 4 * (mm // 64) + r
                sg[h * 3 + br, br * 4 + r, mm] = 1.0
    c["selg"] = sg.reshape(24, 12 * 128)
    gam = 1.0 - 2.0 ** (-5.0 - np.arange(4, dtype=np.float64))
    lg = np.log(gam)
    diff = j[None, :].astype(np.float64) - m[:, None]
    dm = np.zeros((128, 4, 128), np.float64)
    for h in range(4):
        dm[:, h, :] = np.where(diff >= 0, np.exp(diff * lg[h]), 0.0) * 0.125
    c["dmask"] = dm.reshape(128, 512).astype(np.float32)
    xi = np.zeros((128, 2, 128), np.float64)
    zt = np.zeros((128, 2, 128), np.float64)
    dec = np.zeros((128, 2), np.float64)
    for pp in range(2):
        for hh in range(2):
            h = 2 * pp + hh
            xi[64 * hh:64 * hh + 64, pp, :] = np.exp((j[None, :] + 1.0) * lg[h])
            zt[:, pp, 64 * hh:64 * hh + 64] = (np.exp((127.0 - m) * lg[h]) * 0.125)[:, None]
            dec[64 * hh:64 * hh + 64, pp] = np.exp(128.0 * lg[h])
    c["xi"] = xi.reshape(128, 256).astype(np.float32)
    c["zt"] = zt.reshape(128, 256).astype(np.float32)
    c["dec"] = dec.astype(np.float32)
    rm = np.zeros((128, 128), np.float32)
    for mm in range(128):
        if mm % 64 < 32:
            rm[mm + 32, mm] = -1.0
        else:
            rm[mm - 32, mm] = 1.0
    c["rm"] = rm
    half = 32
    inv = 10000.0 ** (-np.arange(half, dtype=np.float32) / half)
    p = np.arange(128)
    ang = np.arange(T, dtype=np.float32)[None, :] * inv[p % 32][:, None]
    c["cosT"] = np.cos(ang).astype(np.float32)
    c["sinT"] = np.sin(ang).astype(np.float32)
    return c


CONST_SHAPES = lambda T: {k: v.shape for k, v in make_consts(128 if False else T).items()}


def win_perm():
    aq0, ak0, av0, bx0, bb0, bc0, cq0, ckc0, cvc0, cks0, cvs0, ckw0, cvw0, cg0, dq0, dk0, dv0, dg0, gl0 = (
        0, 512, 640, 768, 1280, 1792, 2304, 2816, 2944, 3072, 3200, 3328, 3456, 3584, 3608, 3864, 4120, 4632, 5144)
    cols = []
    r64 = np.arange(64)
    for r in range(4):
        cols += list(aq0 + (0 * 4 + r) * 64 + r64) + list(aq0 + (4 + r) * 64 + r64)
    cols += list(ak0 + np.arange(128))
    for r in range(4):
        cols += list(cq0 + (0 * 4 + r) * 64 + r64) + list(cq0 + (4 + r) * 64 + r64)
    cols += list(cks0 + np.arange(128)) + list(ckw0 + np.arange(128))
    cols += list(ckc0 + np.arange(128)) + list(cvc0 + np.arange(128))
    cols += list(dq0 + np.arange(256)) + list(dk0 + np.arange(256)) + list(dg0 + np.arange(512))
    cols += list(bx0 + np.arange(512)) + list(bb0 + np.arange(512)) + list(bc0 + np.arange(512))
    cols += list(cg0 + np.arange(24))
    cols += list(gl0 + np.arange(4096))
    cols += list(av0 + np.arange(128)) + list(cvs0 + np.arange(128)) + list(cvw0 + np.arange(128))
    cols += list(dv0 + np.arange(512))
    assert len(cols) == IN_TOTAL and len(set(cols)) == IN_TOTAL
    return np.array(cols)


def prep_weights(inp, L):
    perm = win_perm()
    WA = np.concatenate([inp["ffn1_w_gate"][:L], inp["ffn1_w_up"][:L], inp["w_in"][:L][:, :, perm],
                         inp["w_out"][:L], inp["ffn2_w_gate"][:L], inp["ffn2_w_up"][:L]], axis=2)
    WD = np.concatenate([inp["ffn1_w_down"][:L], inp["ffn2_w_down"][:L]], axis=2)
    WB = inp["w_branch"][:L]
    att_rows = []
    r64 = np.arange(64)
    for r in range(4):
        att_rows += list((0 * 4 + r) * 64 + r64) + list((4 + r) * 64 + r64)
    att_rows = np.array(att_rows)
    WB = WB.copy()
    WB[:, 0] = WB[:, 0][:, att_rows, :]
    WB[:, 2] = WB[:, 2][:, att_rows, :]
    WB = WB.reshape(L, 2048, 1024)

    def w1l(w):
        a = w[:L].reshape(L, 32, 64, 256).transpose(0, 2, 1, 3).reshape(L, 64, 32 * 256)
        return np.concatenate([a, a], axis=1)
    W1 = np.stack([w1l(inp["cmp_wk1"]), w1l(inp["cmp_wv1"])], axis=1)
    w2k = inp["cmp_wk2"][:L]
    z = np.zeros_like(w2k)
    W2K = np.stack([np.concatenate([w2k, z], axis=2), np.concatenate([z, w2k], axis=2)], axis=1)
    W2V = inp["cmp_wv2"][:L]
    pek = np.stack([inp["cmp_pos_k"][:L], inp["cmp_pos_v"][:L]], axis=1)
    peT = pek.transpose(0, 1, 3, 2)
    peT = np.concatenate([peT, peT], axis=2)
    vec = np.zeros((L, 128, NVEC), np.float32)

    def fm(v, n):
        return v.reshape(L, n, 128).transpose(0, 2, 1)
    vec[:, :, V_N1:V_N1 + 8] = fm(inp["ffn1_norm"][:L], 8)
    vec[:, :, V_NM:V_NM + 8] = fm(inp["mix_norm"][:L], 8)
    vec[:, :, V_N2:V_N2 + 8] = fm(inp["ffn2_norm"][:L], 8)
    vec[:, :, V_GB:V_GB + 32] = fm(inp["merge_gate_bias"][:L], 32)
    vec[:, :, V_AQG] = np.tile(inp["swa_q_gain"][:L], (1, 2))
    vec[:, :, V_AKG] = np.tile(inp["swa_k_gain"][:L], (1, 2))
    vec[:, :, V_CQG] = np.tile(inp["nsa_q_gain"][:L], (1, 2))
    for k in range(3):
        vec[:, :, V_CKG + k] = np.tile(inp["nsa_k_gain"][:L, k], (1, 2))
    cw = inp["conv_w"][:L].reshape(L, 3, 512)
    for k in range(3):
        vec[:, :, V_CW + 4 * k:V_CW + 4 * k + 4] = fm(cw[:, k], 4)
    vec[:, :, V_RG:V_RG + 4] = fm(inp["ret_norm_gain"][:L], 4)
    sk = inp["swa_sinks"][:L].reshape(L, 2, 4)
    sinks = np.repeat(sk, 64, axis=1)
    f = lambda a: np.ascontiguousarray(a, dtype=np.float32)
    return dict(WA=f(WA), WD=f(WD), WB=f(WB), W1=f(W1), W2K=f(W2K), W2V=f(W2V), peT=f(peT), vec=f(vec), sinks=f(sinks))


def build(T, L, TB=256, stop=None):
    nc = bass.Bass("TRN2", target_bir_lowering=False)
    P = Prog(nc)
    NT = T // 128
    NB = T // TB
    TPB = TB // 128
    NCT = max(1, T // 2048)
    RING = 8
    consts_np_shapes = {k: v.shape for k, v in make_consts(T).items()}

    def din(name, shape):
        return nc.dram_tensor(name, list(shape), F32, kind="ExternalInput").ap()

    xT_in = din("xT", (1024, T))
    WA = din("WA", (L, 1024, WA_COLS))
    WD = din("WD", (L, 2816, 2048))
    WB = din("WB", (L, 2048, 1024))
    W1 = din("W1", (L, 2, 128, 8192))
    W2K = din("W2K", (L, 2, 256, 128))
    W2V = din("W2V", (L, 256, 64))
    peT = din("peT", (L, 2, 128, 32))
    vec_in = din("vec", (L, 128, NVEC))
    sinks_in = din("sinks", (L, 128, 4))
    cin = {k: din("c_" + k, s) for k, s in consts_np_shapes.items()}
    outT = nc.dram_tensor("outT", [1024, T], F32, kind="ExternalOutput").ap()

    WA_bf = [nc.dram_tensor("WAbf%d" % l, [128, 8, WA_COLS], BF16).ap() for l in range(L)]
    WD_bf = [nc.dram_tensor("WDbf%d" % l, [128, 22, 2048], BF16).ap() for l in range(L)]
    WB_bf = [nc.dram_tensor("WBbf%d" % l, [128, 16, 1024], BF16).ap() for l in range(L)]
    W1_bf = [nc.dram_tensor("W1bf%d" % l, [2, 128, 8192], BF16).ap() for l in range(L)]
    xs = [nc.dram_tensor("xs%d" % l, [1024, T], F32).ap() for l in range(max(L - 1, 1))]
    B_wscr = [Buf("wscr%d" % l) for l in range(L)]
    B_xs = [[Buf() for _ in range(NB)] for _ in range(max(L - 1, 1))]

    def sb(name, shape, dt=F32):
        return nc.alloc_sbuf_tensor("s_" + name, list(shape), dt)

    def mm(out, lhsT, rhs, start, stop_, reads, writes):
        P.op("pe", lambda e: e.matmul(out, lhsT, rhs, start=start, stop=stop_), reads, writes)

    def tr(out, in_, ident, reads, writes):
        P.op("pe", lambda e: e.matmul(out, in_, ident, start=True, stop=True), reads, writes)

    def act(out, in_, func, reads, writes, bias=None, scale=None):
        kw = {}
        if bias is not None:
            kw["bias"] = bias
        if scale is not None:
            kw["scale"] = scale
        P.op("act", lambda e: e.activation(out=out, in_=in_, func=func, **kw), reads, writes)

    def cp(eng, out, in_, reads, writes):
        if eng == "act":
            P.op("act", lambda e: e.activation(out=out, in_=in_, func=AF.Copy), reads, writes)
        else:
            P.op(eng, lambda e: e.tensor_copy(out=out, in_=in_), reads, writes)

    def tt(eng, out, in0, in1, op, reads, writes):
        P.op(eng, lambda e: e.tensor_tensor(out=out, in0=in0, in1=in1, op=op), reads, writes)

    def ts(eng, out, in0, s1, s2, op0, op1, reads, writes):
        if op1 is None:
            P.op(eng, lambda e: e.tensor_scalar(out=out, in0=in0, scalar1=s1, scalar2=None, op0=op0), reads, writes)
        else:
            P.op(eng, lambda e: e.tensor_scalar(out=out, in0=in0, scalar1=s1, scalar2=s2, op0=op0, op1=op1), reads, writes)

    def stt(eng, out, in0, scalar, in1, op0, op1, reads, writes):
        P.op(eng, lambda e: e.scalar_tensor_tensor(out=out, in0=in0, scalar=scalar, in1=in1, op0=op0, op1=op1),
             reads, writes)

    def memset(eng, ap, val, writes):
        P.op(eng, lambda e: e.memset(ap, val), (), writes)

    def rsqrt_eps(out, in_, reads, writes):
        P.op("act", lambda e: e.activation(out=out, in_=in_, func=AF.Sqrt, bias=eps_col[0:out.shape[0], :], scale=1.0), list(reads) + [B_epscol], writes)
        P.op("dve", lambda e: e.reciprocal(out=out, in_=out), writes, writes)

    def recip_add(out, in_, addend, reads, writes):
        P.op("dve", lambda e: e.tensor_scalar(out=out, in0=in_, scalar1=addend, scalar2=None, op0=ALU.add), reads, writes)
        P.op("dve", lambda e: e.reciprocal(out=out, in_=out), writes, writes)

    def dma(q, out, in_, reads, writes):
        P.op(q, lambda e: e.dma_start(out=out, in_=in_), reads, writes, dma=True)

    def load_const(name, dt, q="pool", src=None, shape=None):
        src = cin[name] if src is None else src
        shape = consts_np_shapes[name] if shape is None else shape
        t = sb("k_" + name, shape, dt)
        b = Buf(name)
        dma("pool" if dt == BF16 else "sp", t[:], src, (), [b])
        return t, b

    ident_bf, B_ident = load_const("ident", BF16)
    ones_bd, B_obd = load_const("ones_bd", F32)
    ones_d, B_od = load_const("ones_d", F32)
    ones_v, B_ov = load_const("ones_v", F32)
    E_sb, B_E = load_const("E", BF16)
    ovl_sb, B_ovl = load_const("ovl", BF16)
    fb_sb, B_fb = load_const("fb", F32)
    f0_sb, B_f0 = load_const("f0", F32)
    mcur_sb, B_mcur = load_const("mask_cur", BF16)
    mprev_sb, B_mprev = load_const("mask_prev", BF16)
    cm_sb, B_cm = load_const("cm", BF16)
    selg_sb, B_selg = load_const("selg", BF16)
    dmask_sb, B_dmask = load_const("dmask", F32)
    xi_sb, B_xi = load_const("xi", F32)
    zt_sb, B_zt = load_const("zt", F32)
    dec_sb, B_dec = load_const("dec", F32)
    rm_sb, B_rm = load_const("rm", F32)
    eps_col = sb("eps_col", [128, 1], F32)
    B_epscol = Buf()
    memset("dve", eps_col[:], EPS, [B_epscol])
    ones_col = sb("ones_col", [128, 1], BF16)
    B_onescol = Buf()
    memset("dve", ones_col[:], 1.0, [B_onescol])
    CONSTB = [B_ident, B_obd, B_od, B_ov, B_E, B_ovl, B_fb, B_f0, B_mcur, B_mprev, B_cm, B_selg, B_dmask, B_xi,
              B_zt, B_dec, B_rm, B_onescol]

    pb = [nc.alloc_psum_tensor("pb%d" % k, [128, 512], F32) for k in range(7)]
    B_pb = [Buf("pb%d" % k) for k in range(7)]
    pbt = nc.alloc_psum_tensor("pbt", [128, 512], F32)
    B_pbt = Buf("pbt")
    mm_rot = [0]

    def mmbank():
        k = mm_rot[0] % 2
        mm_rot[0] += 1
        return pb[k], B_pb[k]

    xT = sb("xT", [128, 8, TB], F32); B_x = [Buf() for _ in range(8)]
    xn = sb("xn", [128, 8, TB], BF16); B_xn = [Buf() for _ in range(8)]
    hT = sb("hT", [128, 22, TB], BF16); B_h = [Buf() for _ in range(22)]
    yT = sb("yT", [128, 4, 4, TB], BF16); B_y = [[Buf() for _ in range(4)] for _ in range(4)]
    mrg = sb("mrg", [128, 8, TB], BF16); B_mrg = [Buf() for _ in range(8)]
    aqT = sb("aqT", [128, 2, 4, TB], BF16); B_aq = Buf()
    cqT = sb("cqT", [128, 2, 4, TB], BF16); B_cq = Buf()
    memset("dve", aqT[:].rearrange("p a b c -> p (a b c)"), 0.0, [B_aq])
    memset("dve", cqT[:].rearrange("p a b c -> p (a b c)"), 0.0, [B_cq])
    dqT = sb("dqT", [128, 2, TB], BF16); B_dq = Buf()
    dkT = sb("dkT", [128, 2, TB], BF16); B_dk = Buf()
    qxi = sb("qxi", [128, 2, TB], BF16); B_qxi = Buf()
    sgd = sb("sgd", [128, 4, TB], F32); B_sgd = Buf()
    cgs = sb("cgs", [24, TB], BF16); B_cgs = Buf()
    cosb = sb("cosb", [128, TB], F32); sinb = sb("sinb", [128, TB], F32); B_cs = Buf()
    vec = sb("vec", [128, NVEC], F32); B_vec = Buf()
    esink = sb("esink", [128, 4], F32); B_esink = Buf()
    ksT = sb("ksT", [128, T], BF16); B_ks = [Buf() for _ in range(NT)]
    vsA = sb("vsA", [128, NT, 192], BF16); B_vs = [Buf() for _ in range(NT)]
    akR = sb("akR", [128, RING, 128], BF16); B_akR = [Buf() for _ in range(RING)]
    avR = sb("avR", [128, RING, 192], BF16); B_avR = [Buf() for _ in range(RING)]
    kwR = sb("kwR", [128, RING, 128], BF16); B_kwR = [Buf() for _ in range(RING)]
    vwR = sb("vwR", [128, RING, 192], BF16); B_vwR = [Buf() for _ in range(RING)]
    dvR = sb("dvR", [128, TPB, 512], BF16); B_dvR = [Buf() for _ in range(TPB)]
    kcmpT = sb("kcmpT", [128, NCT * 128], BF16); B_kcmp = Buf()
    vcmpA = sb("vcmpA", [128, NCT, 192], BF16); B_vcmp = Buf()
    kcC = sb("kcC", [128, 16 + TB], BF16); vcC = sb("vcC", [128, 16 + TB], BF16); B_kcC = Buf(); B_vcC = Buf()
    zC = sb("zC", [128, 4, 2 + TB], F32); B_z = [Buf() for _ in range(4)]
    Rst = sb("Rst", [128, 2, 256], F32); Rbf = sb("Rbf", [128, 2, 256], BF16); B_R = Buf(); B_Rbf = Buf()
    w2k_sb = sb("w2k", [128, 2, 2, 128], BF16); w2v_sb = sb("w2v", [128, 2, 64], BF16); B_w2 = Buf()
    peT_sb = sb("peT", [128, 2, 32], BF16); B_pe = Buf()
    cbias = sb("cbias", [128, 2, 2], F32); B_cbias = Buf()
    NTMP = 5
    tmpf = [sb("tmpf%d" % k, [128, 512], F32) for k in range(NTMP)]; B_tmpf = [Buf() for _ in range(NTMP)]
    tf_rot = [0]

    def tmp():
        k = tf_rot[0] % NTMP
        tf_rot[0] += 1
        return tmpf[k], B_tmpf[k]
    ptb = [sb("ptb%d" % k, [128, 512], BF16) for k in range(6)]; B_ptb = [Buf() for _ in range(6)]
    pt_rot = [0]

    def ptmp():
        k = pt_rot[0] % 6
        pt_rot[0] += 1
        return ptb[k], B_ptb[k]
    obr = [sb("obr%d" % k, [128, 512], F32) for k in range(3)]; B_obr = [Buf() for _ in range(3)]
    selbT = sb("selbT", [128, 2, 2, 128], BF16); B_selbT = [Buf(), Buf()]
    memset("dve", selbT[:].rearrange("p a b c -> p (a b c)"), 0.0, B_selbT)
    impb = sb("impb", [128, 128], F32); impw = sb("impw", [128, 128], F32); m8 = sb("m8", [128, 16], F32)
    selb = sb("selb", [128, 128], BF16); rdt = sb("rdt", [128, 4], F32)
    B_imp = Buf(); B_impw = Buf(); B_m8 = Buf(); B_selb = Buf(); B_rdt = Buf()
    kz = sb("kz", [128, 2, 128], BF16); B_kz = Buf()
    WSLOT = 4096
    NW = 3
    wring = [sb("wr%d" % k, [128, WSLOT], BF16) for k in range(NW)]; B_wr = [Buf() for _ in range(NW)]
    w_rot = [0]

    def wload(src_ap, n_in, n_col):
        k = w_rot[0] % NW
        w_rot[0] += 1
        assert n_in * n_col <= WSLOT
        v = wring[k][:, 0:n_in * n_col].rearrange("p (a b) -> p a b", a=n_in)
        dma("sp", v, src_ap, [B_wscr_cur[0]], [B_wr[k]])
        return v, B_wr[k]

    B_wscr_cur = [None]

    def convert_weights(l):
        b = B_wscr[l]
        for kc in range(8):
            dma("pool", WA_bf[l][:, kc, :], WA[l, kc * 128:(kc + 1) * 128, :], (), [b])
        for kc in range(22):
            dma("pool", WD_bf[l][:, kc, :], WD[l, kc * 128:(kc + 1) * 128, :], (), [b])
        for kc in range(16):
            dma("pool", WB_bf[l][:, kc, :], WB[l, kc * 128:(kc + 1) * 128, :], (), [b])
        for kv in range(2):
            dma("pool", W1_bf[l][kv], W1[l, kv], (), [b])

    def rmsnorm_block(vcol):
        ps, Bp = pb[2], B_pb[2]
        for c in range(8):
            t, Bt = tmp()
            tt("pool", t[:, 0:TB], xT[:, c, :], xT[:, c, :], ALU.mult, [B_x[c]], [Bt])
            mm(ps[:, 0:TB], ones_d[:], t[:, 0:TB], c == 0, c == 7, [Bt, B_od], [Bp])
        r, Br = tmp()
        rsqrt_eps(r[:, 0:TB], ps[:, 0:TB], [Bp], [Br])
        for c in range(8):
            stt("dve", xn[:, c, :], xT[:, c, :], vec[:, vcol + c:vcol + c + 1], r[:, 0:TB], ALU.mult, ALU.mult,
                [B_x[c], B_vec, Br], [B_xn[c]])

    def ffn_block(l, og, ou, od):
        for jg in range(0, 22, 4):
            nj = min(4, 22 - jg)
            wg, Bwg = wload(WA_bf[l][:, :, og + jg * 128: og + (jg + nj) * 128], 8, nj * 128)
            wu, Bwu = wload(WA_bf[l][:, :, ou + jg * 128: ou + (jg + nj) * 128], 8, nj * 128)
            for jj in range(nj):
                j = jg + jj
                pg, Bg = pb[0], B_pb[0]
                pu, Bu = pb[1], B_pb[1]
                for kc in range(8):
                    mm(pg[:, 0:TB], wg[:, kc, jj * 128:(jj + 1) * 128], xn[:, kc, :], kc == 0, kc == 7,
                       [Bwg, B_xn[kc]], [Bg])
                for kc in range(8):
                    mm(pu[:, 0:TB], wu[:, kc, jj * 128:(jj + 1) * 128], xn[:, kc, :], kc == 0, kc == 7,
                       [Bwu, B_xn[kc]], [Bu])
                s, Bs = tmp()
                act(s[:, 0:TB], pg[:, 0:TB], AF.Silu, [Bg], [Bs])
                tt("dve", hT[:, j, :], s[:, 0:TB], pu[:, 0:TB], ALU.mult, [Bs, Bu], [B_h[j]])
        for mg in range(0, 8, 1):
            wd, Bwd = wload(WD_bf[l][:, :, od + mg * 128: od + (mg + 1) * 128], 22, 128)
            for m2 in range(1):
                m = mg + m2
                po, Bo = mmbank()
                for j in range(22):
                    mm(po[:, 0:TB], wd[:, j, m2 * 128:(m2 + 1) * 128], hT[:, j, :], j == 0, j == 21,
                       [Bwd, B_h[j]], [Bo])
                stt("dve", xT[:, m, :], po[:, 0:TB], 0.5, xT[:, m, :], ALU.mult, ALU.add, [Bo, B_x[m]], [B_x[m]])

    def headnorm(ps, Bp, gcol, dest, Bdest):
        q, Bq = tmp()
        cp("act", q[:, 0:TB], ps[:, 0:TB], [Bp], [Bq])
        s, Bs = tmp()
        tt("pool", s[:, 0:TB], q[:, 0:TB], q[:, 0:TB], ALU.mult, [Bq], [Bs])
        p2, Bp2 = pb[2], B_pb[2]
        mm(p2[:, 0:TB], ones_bd[:], s[:, 0:TB], True, True, [Bs, B_obd], [Bp2])
        r, Br = tmp()
        rsqrt_eps(r[:, 0:TB], p2[:, 0:TB], [Bp2], [Br])
        stt("dve", dest, q[:, 0:TB], vec[:, gcol:gcol + 1], r[:, 0:TB], ALU.mult, ALU.mult, [Bq, B_vec, Br], Bdest)

    def rotary(ps, Bp, dest, Bdest):
        q, Bq = tmp()
        cp("act", q[:, 0:TB], ps[:, 0:TB], [Bp], [Bq])
        p2, Bp2 = pb[2], B_pb[2]
        mm(p2[:, 0:TB], rm_sb[:], q[:, 0:TB], True, True, [Bq, B_rm], [Bp2])
        a, Ba = tmp()
        tt("pool", a[:, 0:TB], q[:, 0:TB], cosb[:], ALU.mult, [Bq, B_cs], [Ba])
        b_, Bb = tmp()
        tt("dve", b_[:, 0:TB], p2[:, 0:TB], sinb[:], ALU.mult, [Bp2, B_cs], [Bb])
        tt("dve", dest, a[:, 0:TB], b_[:, 0:TB], ALU.add, [Ba, Bb], Bdest)

    def attend(g, qbuf, Bq, tcol, ktiles, onorm_dest_idx, sink=False, dest=None):
        rows = slice(64 * g, 64 * g + 64)
        drows = slice(64 * (1 - g), 64 * (1 - g) + 64)
        ob = (6, 0, 1)[po_rot[0] % 3]
        po_rot[0] += 1
        po, Bo = pb[ob], B_pb[ob]
        dbuf, Bdbuf = dest if dest is not None else (obr[onorm_dest_idx], B_obr[onorm_dest_idx])
        qv = qbuf[:, g, :, tcol:tcol + 128]
        n = len(ktiles)
        pend = None
        for idx, (kl, Bk, va, Bv, masks) in enumerate(ktiles):
            sp_, Bs = pb[4 + idx % 2], B_pb[4 + idx % 2]
            mm(sp_[:].rearrange("p (a b) -> p a b", a=4), kl, qv, True, len(masks) == 0, [Bk, Bq], [Bs])
            for mi, (ml, mr, mb) in enumerate(masks):
                mm(sp_[:].rearrange("p (a b) -> p a b", a=4), ml, mr, False, mi == len(masks) - 1, mb, [Bs])
            pt, Bpt = ptmp()
            act(pt[:], sp_[:], AF.Exp, [Bs], [Bpt], scale=0.125)
            if pend is not None:
                pidx, ppt, pBpt, pva, pBv = pend
                mm(po[:], pva, ppt[:], pidx == 0, False, [pBv, pBpt], [Bo])
                yield ppt, pBpt
            pend = (idx, pt, Bpt, va, Bv)
        pidx, ppt, pBpt, pva, pBv = pend
        mm(po[:], pva, ppt[:], pidx == 0, True, [pBv, pBpt], [Bo])
        yield ppt, pBpt
        rd, Brd = tmp()
        if sink:
            for r in range(4):
                recip_add(rd[rows, r * 128:(r + 1) * 128], po[drows, r * 128:(r + 1) * 128],
                          esink[rows, r:r + 1], [Bo, B_esink], [Brd])
        else:
            recip_add(rd[rows, :], po[drows, :], 1e-30, [Bo], [Brd])
        tt("dve", dbuf[rows, :], po[rows, :], rd[rows, :], ALU.mult, [Bo, Brd], [Bdbuf])

    po_rot = [0]

    def bc4(ap2d):
        return ap2d.unsqueeze(1).to_broadcast([ap2d.shape[0], 4, 128])

    bxs = sb("bxs", [128, 4, TB], F32); B_bx = [Buf() for _ in range(4)]
    bbs = sb("bbs", [128, 4, TB], F32); B_bb = [Buf() for _ in range(4)]
    hc = sb("hc", [128, 128], F32); B_hc = Buf()
    hc2 = sb("hc2", [128, 128], F32); B_hc2 = Buf()
    gk = sb("gk", [128, 2, 128], BF16); B_gk = Buf()
    gpad = sb("gpad", [128, 2, 2, 2, 128], BF16); B_gpad = Buf()
    osb = sb("osb", [128, 512], F32); B_osb = Buf()
    macc = sb("macc", [128, 4, TB], F32); B_macc = [Buf() for _ in range(4)]

    def headnorm_w(ps_ap, Bp, gcol, dest, Bdest, W):
        q, Bq = tmp()
        cp("act", q[:, 0:W], ps_ap, [Bp], [Bq])
        s_, Bs = tmp()
        tt("pool", s_[:, 0:W], q[:, 0:W], q[:, 0:W], ALU.mult, [Bq], [Bs])
        p2, Bp2 = pb[2], B_pb[2]
        mm(p2[:, 0:W], ones_bd[:], s_[:, 0:W], True, True, [Bs, B_obd], [Bp2])
        r, Br = tmp()
        rsqrt_eps(r[:, 0:W], p2[:, 0:W], [Bp2], [Br])
        if isinstance(dest, tuple):
            for g_ in range(2):
                rw = slice(64 * g_, 64 * g_ + 64)
                stt("dve", dest[g_][rw, :], q[rw, 0:W], vec[rw, gcol:gcol + 1], r[rw, 0:W], ALU.mult, ALU.mult,
                    [Bq, B_vec, Br], Bdest)
        else:
            stt("dve", dest, q[:, 0:W], vec[:, gcol:gcol + 1], r[:, 0:W], ALU.mult, ALU.mult, [Bq, B_vec, Br], Bdest)

    memset("dve", hc[:], 0.0, [B_hc])

    import os as _os2
    _katt = _os2.environ.get("KATT", "ABCDEF")

    _kret = int(_os2.environ.get("KRET", "9"))

    def _rl(n):
        return n <= _kret

    def _en(t):
        return t in _katt

    def run_att(gen):
        for _ in gen:
            pass

    for l in range(L):
        convert_weights(l)
    for l in range(L):
        B_wscr_cur[0] = B_wscr[l]
        src = xT_in if l == 0 else xs[l - 1]
        dst = outT if l == L - 1 else xs[l]
        srcv = src.rearrange("(c p) t -> p c t", p=128)
        dstv = dst.rearrange("(c p) t -> p c t", p=128)
        dma("sp", vec[:], vec_in[l], (), [B_vec])
        sk, Bsk = tmp()
        dma("sp", sk[:, 0:4], sinks_in[l], (), [Bsk])
        act(esink[:], sk[:, 0:4], AF.Exp, [Bsk], [B_esink])
        dma("pool", w2k_sb[:].rearrange("p g c m -> p (g c) m"),
            W2K[l].rearrange("g (c p) m -> p (g c) m", p=128), (), [B_w2])
        dma("pool", w2v_sb[:], W2V[l].rearrange("(c p) m -> p c m", p=128), (), [B_w2])
        dma("pool", peT_sb[:], peT[l].rearrange("k p n -> p k n"), (), [B_pe])
        memset("pool", kcmpT[:], 0.0, [B_kcmp])
        memset("pool", vcmpA[:], 0.0, [B_vcmp])
        memset("pool", vcmpA[:, :, 64:128], 1.0, [B_vcmp])
        memset("pool", kcC[:], 0.0, [B_kcC])
        memset("pool", vcC[:], 0.0, [B_vcC])
        memset("pool", zC[:], 0.0, B_z)
        memset("pool", Rst[:], 0.0, [B_R])
        memset("pool", Rbf[:], 0.0, [B_Rbf])
        memset("pool", vsA[:, :, 64:128], 1.0, B_vs)
        memset("pool", avR[:, :, 64:128], 1.0, B_avR)
        memset("pool", vwR[:, :, 64:128], 1.0, B_vwR)
        for kv in range(2):
            for ch in range(2):
                for half in range(2):
                    w1, Bw1 = wload(W1_bf[l][kv][:, half * 4096:(half + 1) * 4096]
                                    .rearrange("p (a b) -> p a b", a=16)[:, :, ch * 128:(ch + 1) * 128], 16, 128)
                    for li in range(16):
                        lg_ = half * 16 + li
                        col = kv * 2 + ch
                        mm(pb[3][:, col:col + 1], w1[0:64, li, :], peT_sb[0:64, kv, lg_:lg_ + 1],
                           lg_ == 0, lg_ == 31, [Bw1, B_pe], [B_pb[3]])
        cp("dve", cbias[:].rearrange("p a b -> p (a b)"), pb[3][:, 0:4], [B_pb[3]], [B_cbias])

        for blk in range(NB):
            t0 = blk * TB
            dma("sp", xT[:], srcv[:, :, t0:t0 + TB], [B_xs[l - 1][blk]] if l > 0 else [], B_x)
            dma("sp", cosb[:], cin["cosT"][:, t0:t0 + TB], (), [B_cs])
            dma("sp", sinb[:], cin["sinT"][:, t0:t0 + TB], (), [B_cs])
            rmsnorm_block(V_N1)
            ffn_block(l, OFF_F1G, OFF_F1U, 0)
            if stop == "ffn1":
                dma("sp", dstv[:, :, t0:t0 + TB], xT[:], B_x, [B_xs[min(l, len(B_xs) - 1)][blk]])
                continue
            rmsnorm_block(V_NM)

            def wgroup(c0, ncols):
                return wload(WA_bf[l][:, :, OFF_WIN + c0: OFF_WIN + c0 + ncols], 8, ncols)

            def proj(w, Bw, off, M=128):
                ps, Bp = mmbank()
                for kc in range(8):
                    mm(ps[0:M, 0:TB], w[:, kc, off:off + M], xn[:, kc, :], kc == 0, kc == 7, [Bw, B_xn[kc]], [Bp])
                return ps, Bp
            slots = [(TPB * blk + ti) % RING for ti in range(TPB)]
            handlers = []
            for r in range(4):
                handlers.append(lambda ps, Bp, r=r: headnorm_w(ps[:, 0:TB], Bp, V_AQG, (aqT[:, 0, r, :], aqT[:, 1, r, :]), [B_aq], TB))

            def h_ring(ps, Bp, gcol, ring, Bring):
                tk, Btk = ptmp()
                headnorm_w(ps[:, 0:TB], Bp, gcol, tk[:, 0:TB], [Btk], TB)
                for ti in range(TPB):
                    cp("pool", ring[:, slots[ti], :], tk[:, ti * 128:(ti + 1) * 128], [Btk], [Bring[slots[ti]]])
            handlers.append(lambda ps, Bp: h_ring(ps, Bp, V_AKG, akR, B_akR))
            for r in range(4):
                handlers.append(lambda ps, Bp, r=r: headnorm_w(ps[:, 0:TB], Bp, V_CQG, (cqT[:, 0, r, :], cqT[:, 1, r, :]), [B_cq], TB))
            handlers.append(lambda ps, Bp: headnorm_w(ps[:, 0:TB], Bp, V_CKG + 1, ksT[:, t0:t0 + TB],
                                                       B_ks[blk * TPB:(blk + 1) * TPB], TB))
            handlers.append(lambda ps, Bp: h_ring(ps, Bp, V_CKG + 2, kwR, B_kwR))
            handlers.append(lambda ps, Bp: cp("act", kcC[:, 16:16 + TB], ps[:, 0:TB], [Bp], [B_kcC]))
            handlers.append(lambda ps, Bp: cp("act", vcC[:, 16:16 + TB], ps[:, 0:TB], [Bp], [B_vcC]))
            for pp in range(2):
                handlers.append(lambda ps, Bp, pp=pp: rotary(ps, Bp, dqT[:, pp, :], [B_dq]))
            for pp in range(2):
                handlers.append(lambda ps, Bp, pp=pp: rotary(ps, Bp, dkT[:, pp, :], [B_dk]))
            for h in range(4):
                handlers.append(lambda ps, Bp, h=h: act(sgd[:, h, :], ps[:, 0:TB], AF.Silu, [Bp], [B_sgd]))
            for c in range(4):
                handlers.append(lambda ps, Bp, c=c: cp("act", bxs[:, c, :], ps[:, 0:TB], [Bp], [B_bx[c]]))
            for c in range(4):
                handlers.append(lambda ps, Bp, c=c: cp("act", bbs[:, c, :], ps[:, 0:TB], [Bp], [B_bb[c]]))
            for c in range(4):
                handlers.append(lambda ps, Bp, c=c: tt("dve", zC[:, c, 2:2 + TB], ps[:, 0:TB], bxs[:, c, :], ALU.mult,
                                                       [Bp, B_bx[c]], [B_z[c]]))
            assert len(handlers) == NFM
            dfr = [None]
            for g0 in range(0, NFM, 4):
                ng = min(4, NFM - g0)
                ncols = ng * 128 + (24 if g0 + ng == NFM else 0)
                w, Bw = wgroup(g0 * 128, ncols)
                for k in range(ng):
                    ps, Bp = proj(w, Bw, k * 128)
                    if dfr[0] is not None:
                        dfr[0]()
                    dfr[0] = (lambda hh=handlers[g0 + k], ps=ps, Bp=Bp: hh(ps, Bp))
                if g0 + ng == NFM:
                    ps, Bp = proj(w, Bw, ng * 128, M=24)
                    dfr[0]()
                    dfr[0] = None
                    act(cgs[:], ps[0:24, 0:TB], AF.Sigmoid, [Bp], [B_cgs])
            if stop == "projfm":
                dma("sp", dstv[:, :, t0:t0 + TB], xT[:], B_x, [B_xs[min(l, len(B_xs) - 1)][blk]])
                continue
            wtm1, Bwtm1 = wload(WA_bf[l][:, :, OFF_WIN + OFF_TM: OFF_WIN + OFF_TM + 384], 8, 384)
            wtm2, Bwtm2 = wload(WA_bf[l][:, :, OFF_WIN + OFF_TM + 384: OFF_WIN + OFF_TM + 896], 8, 512)
            for ti in range(TPB):
                tile_i = blk * TPB + ti
                ps, Bp = mmbank()
                for kc in range(8):
                    mm(ps[:, 0:384], xn[:, kc, ti * 128:(ti + 1) * 128], wtm1[:, kc, :], kc == 0, kc == 7,
                       [Bwtm1, B_xn[kc]], [Bp])

                def vput(dst3, off, ps=ps):
                    return (dst3.rearrange("p (a b) -> p a b", a=3)[:, 0:3:2, :],
                            ps[:, off:off + 128].rearrange("p (a b) -> p a b", a=2))
                for (dst3, off, Bd, e0, e1) in ((avR[:, slots[ti], :], 0, B_avR[slots[ti]], "act", "dve"),
                                                (vsA[:, tile_i, :], 128, B_vs[tile_i], "dve", "act"),
                                                (vwR[:, slots[ti], :], 256, B_vwR[slots[ti]], "act", "dve")):
                    cp(e0, dst3[:, 0:64], ps[:, off:off + 64], [Bp], [Bd])
                    cp(e1, dst3[:, 128:192], ps[:, off + 64:off + 128], [Bp], [Bd])
                ps, Bp = mmbank()
                for kc in range(8):
                    mm(ps[:, :], xn[:, kc, ti * 128:(ti + 1) * 128], wtm2[:, kc, :], kc == 0, kc == 7,
                       [Bwtm2, B_xn[kc]], [Bp])
                cp("act", dvR[:, ti, :], ps[:, :], [Bp], [B_dvR[ti]])

            if stop == "projtm":
                dma("sp", dstv[:, :, t0:t0 + TB], xT[:], B_x, [B_xs[min(l, len(B_xs) - 1)][blk]])
                continue
            for c in range(4):
                a, Ba = tmp()
                ts("pool", a[:, 0:TB], zC[:, c, 0:TB], vec[:, V_CW + c:V_CW + c + 1], None, ALU.mult, None,
                   [B_z[c], B_vec], [Ba])
                stt("dve", a[:, 0:TB], zC[:, c, 1:1 + TB], vec[:, V_CW + 4 + c:V_CW + 5 + c], a[:, 0:TB],
                    ALU.mult, ALU.add, [B_z[c], B_vec, Ba], [Ba])
                stt("dve", a[:, 0:TB], zC[:, c, 2:2 + TB], vec[:, V_CW + 8 + c:V_CW + 9 + c], a[:, 0:TB],
                    ALU.mult, ALU.add, [B_z[c], B_vec, Ba], [Ba])
                tt("pool", yT[:, 1, c, :], a[:, 0:TB], bbs[:, c, :], ALU.mult, [Ba, B_bb[c]], [B_y[1][c]])
                cp("pool", zC[:, c, 0:2], zC[:, c, TB:TB + 2], [B_z[c]], [B_z[c]])

            if stop == "conv":
                dma("sp", dstv[:, :, t0:t0 + TB], xT[:], B_x, [B_xs[min(l, len(B_xs) - 1)][blk]])
                continue
            import os as _os
            if _os.environ.get("KSKIP", "") != "cmp":
                per = TB // 16
                n_lo = 0 if blk == 0 else per * blk - 1
                n_hi = per * (blk + 1) - 2
                nn = n_hi - n_lo + 1
                cst = 16 if blk == 0 else 0
                pieces = []
                n_ = n_lo
                while n_ <= n_hi:
                    e_ = min(n_hi, (n_ // 128) * 128 + 127)
                    pieces.append((n_ // 128, n_, e_ - n_ + 1))
                    n_ = e_ + 1
                for kv, (car, Bcar) in enumerate(((kcC, B_kcC), (vcC, B_vcC))):
                    for ch in range(2):
                        for half in range(2):
                            w1, Bw1 = wload(W1_bf[l][kv][:, half * 4096:(half + 1) * 4096]
                                            .rearrange("p (a b) -> p a b", a=16)[:, :, ch * 128:(ch + 1) * 128], 16, 128)
                            for li in range(16):
                                lg_ = half * 16 + li
                                for g in range(2):
                                    rows = slice(64 * g, 64 * g + 64)
                                    rhs = car[rows, cst + lg_: cst + lg_ + 16 * (nn - 1) + 1: 16]
                                    bk = 3 if g == 0 else 2
                                    mm(pb[bk][:, 0:nn], w1[rows, li, :], rhs, lg_ == 0, lg_ == 31,
                                       [Bw1, Bcar], [B_pb[bk]])
                        ts("dve", hc[:, 0:nn], pb[3][:, 0:nn], cbias[:, kv, ch:ch + 1], None, ALU.add, None,
                           [B_pb[3], B_cbias], [B_hc])
                        ts("dve", hc[:, 64:64 + nn], pb[2][:, 0:nn], cbias[:, kv, ch:ch + 1], None, ALU.add, None,
                           [B_pb[2], B_cbias], [B_hc])
                        tt("pool", hc2[:], hc[:], hc[:], ALU.mult, [B_hc], [B_hc2])
                        ts("pool", hc2[:], hc2[:], 0.044715, 1.0, ALU.mult, ALU.add, [B_hc2], [B_hc2])
                        tt("pool", hc2[:], hc2[:], hc[:], ALU.mult, [B_hc2, B_hc], [B_hc2])
                        act(hc2[:], hc2[:], AF.Tanh, [B_hc2], [B_hc2], scale=0.7978845608028654)
                        ts("pool", hc2[:], hc2[:], 1.0, 0.5, ALU.add, ALU.mult, [B_hc2], [B_hc2])
                        if kv == 0:
                            tt("pool", gk[:, ch, :], hc2[:], hc[:], ALU.mult, [B_hc2, B_hc], [B_gk])
                        else:
                            if ch == 0:
                                memset("pool", gpad[:].rearrange("p a b c d -> p (a b c d)"), 0.0, [B_gpad])
                            for pi, (ptile, pn, pcnt) in enumerate(pieces):
                                for g in range(2):
                                    o0 = g * 64 + (pn - n_lo)
                                    tt("pool", gpad[:, pi, g, ch, pn % 128: pn % 128 + pcnt],
                                       hc2[:, o0:o0 + pcnt], hc[:, o0:o0 + pcnt], ALU.mult, [B_hc2, B_hc], [B_gpad])
                cp("pool", kcC[:, 0:16], kcC[:, TB:TB + 16], [B_kcC], [B_kcC])
                cp("pool", vcC[:, 0:16], vcC[:, TB:TB + 16], [B_vcC], [B_vcC])
                first = True
                for ch in range(2):
                    for g in range(2):
                        mm(pb[3][:, 0:nn], w2k_sb[:, g, ch, :], gk[:, ch, g * 64: g * 64 + nn], first, ch == 1 and g == 1,
                           [B_w2, B_gk], [B_pb[3]])
                        first = False
                headnorm_w(pb[3][:, 0:nn], B_pb[3], V_CKG + 0, kcmpT[:, n_lo:n_lo + nn], [B_kcmp], nn)
                for pi, (ptile, pn, pcnt) in enumerate(pieces):
                    for g in range(2):
                        for ch in range(2):
                            mm(pb[3][:, 256 + g * 64: 256 + g * 64 + 64], gpad[:, pi, g, ch, :], w2v_sb[:, ch, :],
                               ch == 0, ch == 1, [B_gpad, B_w2], [B_pb[3]])
                    tt("dve", vcmpA[:, ptile, 0:64], vcmpA[:, ptile, 0:64], pb[3][:, 256:320], ALU.add,
                       [B_pb[3], B_vcmp], [B_vcmp])
                    tt("dve", vcmpA[:, ptile, 128:192], vcmpA[:, ptile, 128:192], pb[3][:, 320:384], ALU.add,
                       [B_pb[3], B_vcmp], [B_vcmp])


            if stop == "cmp":
                dma("sp", dstv[:, :, t0:t0 + TB], xT[:], B_x, [B_xs[min(l, len(B_xs) - 1)][blk]])
                continue
            for ti in range(TPB):
                i = blk * TPB + ti
                tc = ti * 128
                if _en('C'):
                    n_ct = min(NCT, i // 16 + 1)
                    for g in range(2):
                        rows = slice(64 * g, 64 * g + 64)
                        kts = []
                        for c in range(n_ct):
                            dl = i - 16 * c
                            masks = []
                            if dl <= 16:
                                masks = [(ident_bf[:], bc4(cm_sb[:, dl * 128:(dl + 1) * 128]), [B_ident, B_cm])]
                            kts.append((kcmpT[:, c * 128:(c + 1) * 128], B_kcmp, vcmpA[:, c, 64 * g: 64 * g + 128],
                                        B_vcmp, masks))
                        ip, Bip = pb[3], B_pb[3]
                        dp, Bdp = pb[2], B_pb[2]
                        pts = list(attend(g, cqT, B_cq, tc, kts, 0))
                        for r in range(4):
                            for c, (pt, Bpt) in enumerate(pts):
                                mm(ip[:, r * 128:(r + 1) * 128], pt[:, r * 128:(r + 1) * 128], ovl_sb[:, c * 128:(c + 1) * 128],
                                   c == 0, c == n_ct - 1, [Bpt, B_ovl], [Bip])
                        for r in range(4):
                            for c, (pt, Bpt) in enumerate(pts):
                                mm(dp[:, r:r + 1], pt[:, r * 128:(r + 1) * 128], ones_col[:], c == 0, c == n_ct - 1,
                                   [Bpt, B_onescol], [Bdp])
                        recip_add(rdt[:], dp[:, 0:4], 1e-30, [Bdp], [B_rdt])
                        ts("dve", impw[:], ip[:, 0:128], rdt[:, 0:1], None, ALU.mult, None, [Bip, B_rdt], [B_impw])
                        for r in range(1, 4):
                            stt("dve", impw[:], ip[:, r * 128:(r + 1) * 128], rdt[:, r:r + 1], impw[:], ALU.mult, ALU.add,
                                [Bip, B_rdt, B_impw], [B_impw])
                        tt("dve", impw[:], impw[:], fb_sb[:, 128 - 2 * i: 256 - 2 * i], ALU.add, [B_impw, B_fb], [B_impw])
                        tt("dve", impb[:], impw[:], f0_sb[:], ALU.add, [B_impw, B_f0], [B_imp])
                        P.op("dve", lambda e: e.max(out=m8[:, 0:8], in_=impb[:]), [B_imp], [B_m8])
                        P.op("dve", lambda e: e.match_replace(out=impw[:], in_to_replace=m8[:, 0:8], in_values=impb[:],
                                                              imm_value=-3.0e4), [B_imp, B_m8], [B_impw])
                        P.op("dve", lambda e: e.max(out=m8[:, 8:16], in_=impw[:]), [B_impw], [B_m8])
                        ts("dve", selb[:], impb[:], m8[:, 15:16], NEG, ALU.is_lt, ALU.mult, [B_imp, B_m8], [B_selb])
                        tr(pbt[:, g * 128:(g + 1) * 128], selb[:], ident_bf[:], [B_selb, B_ident], [B_pbt])
                        cp("act", selbT[0:64, g, 0, :], pbt[0:64, g * 128:(g + 1) * 128], [B_pbt], [B_selbT[g]])
                        cp("act", selbT[64:128, g, 1, :], pbt[64:128, g * 128:(g + 1) * 128], [B_pbt], [B_selbT[g]])
                if _en('A'):
                    for g in range(2):
                        rows = slice(64 * g, 64 * g + 64)
                        kts = []
                        for kt in ([i - 1, i] if i >= 1 else [i]):
                            s_ = kt % RING
                            msk = mcur_sb if kt == i else mprev_sb
                            kts.append((akR[:, s_, :], B_akR[s_], avR[:, s_, 64 * g: 64 * g + 128], B_avR[s_],
                                        [(ident_bf[:], bc4(msk[:]), [B_ident, B_mcur, B_mprev])]))
                        run_att(attend(g, aqT, B_aq, tc, kts, None, sink=True, dest=(osb, B_osb)))
                    for r in range(4):
                        cp("act", yT[:, 0, r, tc:tc + 128], osb[:, r * 128:(r + 1) * 128], [B_osb], [B_y[0][r]])
                if _en('B'):
                    for g in range(2):
                        rows = slice(64 * g, 64 * g + 64)
                        kts = []
                        for kt in range(max(0, i - 4), i + 1):
                            s_ = kt % RING
                            masks = []
                            if kt == i:
                                masks = [(ident_bf[:], bc4(mcur_sb[:]), [B_ident, B_mcur])]
                            elif kt == i - 4:
                                masks = [(ident_bf[:], bc4(mprev_sb[:]), [B_ident, B_mprev])]
                            kts.append((kwR[:, s_, :], B_kwR[s_], vwR[:, s_, 64 * g: 64 * g + 128], B_vwR[s_], masks))
                        run_att(attend(g, cqT, B_cq, tc, kts, 2))
                if _en('F'):
                    ap_, Bap = pb[4], B_pb[4]
                    for h in range(4):
                        pp, hh = h // 2, h % 2
                        rows = slice(64 * hh, 64 * hh + 64)
                        mm(pb[4 + hh][:, pp * 128:(pp + 1) * 128], dkT[rows, pp, tc:tc + 128], dqT[rows, pp, tc:tc + 128],
                           True, True, [B_dk, B_dq], [B_pb[4 + hh]])
                    if _rl(1):
                        am, Bam = ptmp()
                        for h in range(4):
                            pp, hh = h // 2, h % 2
                            tt("dve", am[:, h * 128:(h + 1) * 128], pb[4 + hh][:, pp * 128:(pp + 1) * 128],
                               dmask_sb[:, h * 128:(h + 1) * 128], ALU.mult, [B_pb[4 + hh], B_dmask], [Bam])

                    if _rl(2):
                        for pp in range(2):
                            tt("dve", qxi[:, pp, tc:tc + 128], dqT[:, pp, tc:tc + 128], xi_sb[:, pp * 128:(pp + 1) * 128],
                               ALU.mult, [B_dq, B_xi], [B_qxi])

                    if _rl(3):
                        op_, Bop = pb[6], B_pb[6]
                        for h in range(4):
                            pp, hh = h // 2, h % 2
                            rows = slice(64 * hh, 64 * hh + 64)
                            mm(op_[:, h * 128:(h + 1) * 128], dvR[:, ti, h * 128:(h + 1) * 128], am[:, h * 128:(h + 1) * 128],
                               True, False, [B_dvR[ti], Bam], [Bop])
                            mm(op_[:, h * 128:(h + 1) * 128], Rbf[rows, pp, hh * 128:(hh + 1) * 128], qxi[rows, pp, tc:tc + 128],
                               False, True, [B_Rbf, B_qxi], [Bop])

                    if _rl(4):
                        for pp in range(2):
                            tr(pbt[:, 256 + pp * 128: 256 + (pp + 1) * 128], dkT[:, pp, tc:tc + 128], ident_bf[:],
                               [B_dk, B_ident], [B_pbt])
                        tt("dve", kz[:].rearrange("p a b -> p (a b)"), pbt[:, 256:512], zt_sb[:], ALU.mult, [B_pbt, B_zt], [B_kz])

                    if _rl(5):
                        sp2, Bsp2 = pb[5], B_pb[5]
                        for pp in range(2):
                            mm(sp2[:, pp * 256:(pp + 1) * 256], kz[:, pp, :], dvR[:, ti, pp * 256:(pp + 1) * 256], True, True,
                               [B_kz, B_dvR[ti]], [Bsp2])
                        for pp in range(2):
                            stt("dve", Rst[:, pp, :], Rst[:, pp, :], dec_sb[:, pp:pp + 1], sp2[:, pp * 256:(pp + 1) * 256],
                                ALU.mult, ALU.add, [B_R, B_dec, Bsp2], [B_R])

                    if _rl(6):
                        cp("act", Rbf[:].rearrange("p a b -> p (a b)"), Rst[:].rearrange("p a b -> p (a b)"), [B_R], [B_Rbf])

                    if _rl(7):
                        o2, Bo2 = tmp()
                        cp("act", o2[:], op_[:], [Bop], [Bo2])
                        mp, Bmp = pb[2], B_pb[2]
                        mm(mp[:], ones_v[:], o2[:], True, True, [Bo2, B_ov], [Bmp])
                        c2, Bc2 = tmp()
                        tt("dve", c2[:], o2[:], mp[:], ALU.subtract, [Bo2, Bmp], [Bc2])

                    if _rl(8):
                        s2, Bs2 = tmp()
                        tt("pool", s2[:], c2[:], c2[:], ALU.mult, [Bc2], [Bs2])
                        mm(mp[:], ones_v[:], s2[:], True, True, [Bs2, B_ov], [Bmp])
                        r2, Br2 = tmp()
                        rsqrt_eps(r2[:], mp[:], [Bmp], [Br2])
                        tt("pool", c2[:], c2[:], r2[:], ALU.mult, [Bc2, Br2], [Bc2])
                        for h in range(4):
                            stt("dve", yT[:, 3, h, tc:tc + 128], c2[:, h * 128:(h + 1) * 128], vec[:, V_RG + h:V_RG + h + 1],
                                sgd[:, h, tc:tc + 128], ALU.mult, ALU.mult, [Bc2, B_vec, B_sgd], [B_y[3][h]])


                if _en('D'):
                    for g in range(2):
                        rows = slice(64 * g, 64 * g + 64)
                        kts = []
                        for kt in range(0, i + 1):
                            hf = 0 if kt < 32 else 1
                            masks = [(E_sb[:, (kt % 32) * 128:(kt % 32 + 1) * 128],
                                      bc4(selbT[:, g, hf, :]), [B_E, B_selbT[g]])]
                            if kt == i:
                                masks.append((ident_bf[:], bc4(mcur_sb[:]), [B_ident, B_mcur]))
                            kts.append((ksT[:, kt * 128:(kt + 1) * 128], B_ks[kt], vsA[:, kt, 64 * g: 64 * g + 128],
                                        B_vs[kt], masks))
                        run_att(attend(g, cqT, B_cq, tc, kts, 1))
                if _en('E'):
                    for br, srcb, Bsrc in ((0, obr[0], B_obr[0]), (1, obr[1], B_obr[1]), (2, obr[2], B_obr[2])):
                        gp, Bgp = (pb[3], B_pb[3]) if br != 1 else (pb[2], B_pb[2])
                        for r in range(4):
                            mm(gp[:, r * 128:(r + 1) * 128], selg_sb[:, (br * 4 + r) * 128:(br * 4 + r + 1) * 128],
                               cgs[:, tc:tc + 128], True, True, [B_selg, B_cgs], [Bgp])
                        if br == 0:
                            tt("dve", osb[:], gp[:], srcb[:], ALU.mult, [Bgp, Bsrc], [B_osb])
                        else:
                            t_, Bt_ = tmp()
                            tt("dve", t_[:], gp[:], srcb[:], ALU.mult, [Bgp, Bsrc], [Bt_])
                            tt("pool", osb[:], osb[:], t_[:], ALU.add, [B_osb, Bt_], [B_osb])
                    for r in range(4):
                        cp("act", yT[:, 2, r, tc:tc + 128], osb[:, r * 128:(r + 1) * 128], [B_osb], [B_y[2][r]])

                if stop == "att":
                    dma("sp", dstv[:, :, t0:t0 + TB], xT[:], B_x, [B_xs[min(l, len(B_xs) - 1)][blk]])
                    continue
            for m0 in range(0, 8, 4):
                for bi in range(4):
                    wbv, Bwb = wload(WB_bf[l][:, bi * 4:(bi + 1) * 4, m0 * 128:(m0 + 4) * 128], 4, 512)
                    gvw, Bgv = wload(WA_bf[l][:, :, OFF_WIN + OFF_GATES + bi * 1024 + m0 * 128:
                                              OFF_WIN + OFF_GATES + bi * 1024 + (m0 + 4) * 128], 8, 512)
                    for m in range(m0, m0 + 4):
                        mo = (m - m0) * 128
                        pg, Bg = pb[0], B_pb[0]
                        for kc in range(8):
                            mm(pg[:, 0:TB], gvw[:, kc, mo:mo + 128], xn[:, kc, :], kc == 0, kc == 7, [Bgv, B_xn[kc]], [Bg])
                        gs, Bgs = tmp()
                        act(gs[:, 0:TB], pg[:, 0:TB], AF.Sigmoid, [Bg, B_vec], [Bgs],
                            bias=vec[:, V_GB + bi * 8 + m: V_GB + bi * 8 + m + 1])
                        pbk, Bbk = pb[1], B_pb[1]
                        for c in range(4):
                            mm(pbk[:, 0:TB], wbv[:, c, mo:mo + 128], yT[:, bi, c, :], c == 0, c == 3,
                               [Bwb, B_y[bi][c]], [Bbk])
                        if bi == 0:
                            tt("dve", macc[:, m - m0, :], pbk[:, 0:TB], gs[:, 0:TB], ALU.mult, [Bbk, Bgs], [B_macc[m - m0]])
                        else:
                            t_, Bt_ = tmp()
                            tt("dve", t_[:, 0:TB], pbk[:, 0:TB], gs[:, 0:TB], ALU.mult, [Bbk, Bgs], [Bt_])
                            if bi < 3:
                                tt("pool", macc[:, m - m0, :], macc[:, m - m0, :], t_[:, 0:TB], ALU.add,
                                   [B_macc[m - m0], Bt_], [B_macc[m - m0]])
                            else:
                                tt("pool", mrg[:, m, :], macc[:, m - m0, :], t_[:, 0:TB], ALU.add,
                                   [B_macc[m - m0], Bt_], [B_mrg[m]])
            for m0 in range(0, 8, 4):
                wo, Bwo = wload(WA_bf[l][:, :, OFF_WO + m0 * 128: OFF_WO + (m0 + 4) * 128], 8, 512)
                for m in range(m0, m0 + 4):
                    po, Bo = mmbank()
                    for kc in range(8):
                        mm(po[:, 0:TB], wo[:, kc, (m - m0) * 128:(m - m0 + 1) * 128], mrg[:, kc, :], kc == 0, kc == 7,
                           [Bwo, B_mrg[kc]], [Bo])
                    tt("dve", xT[:, m, :], xT[:, m, :], po[:, 0:TB], ALU.add, [B_x[m], Bo], [B_x[m]])
            if stop != "mix":
                rmsnorm_block(V_N2)
                ffn_block(l, OFF_F2G, OFF_F2U, 1024)
            dma("sp", dstv[:, :, t0:t0 + TB], xT[:], B_x, [B_xs[min(l, len(B_xs) - 1)][blk]])

    P.emit()
    return nc, P


_CACHE = {}


def run_cores(x, inp, L, T, n_cores, TB=256, stop=None):
    key = (T, L, TB, stop)
    if key not in _CACHE:
        _CACHE[key] = build(T, L, TB, stop)
    nc, P = _CACHE[key]
    w = prep_weights(inp, L)
    cs = make_consts(T)
    in_maps = []
    for c in range(n_cores):
        m = {"xT": np.ascontiguousarray(x[c].T.astype(np.float32))}
        m.update(w)
        for k, v in cs.items():
            m["c_" + k] = np.ascontiguousarray(v)
        in_maps.append(m)
    res = run_bass_kernel_spmd(nc, in_maps, core_ids=list(range(n_cores)))
    return np.stack([np.ascontiguousarray(r["outT"].T) for r in res.results], axis=0)


def kernel(**inputs):
    x = np.asarray(inputs["x"], dtype=np.float32)
    B, T, _ = x.shape
    inp = {k: np.asarray(v, dtype=np.float32) for k, v in inputs.items() if k != "x"}
    out = run_cores(x, inp, L_FULL, T, B)
    return out.astype(np.float32)
```

```python
import contextlib
import math
import numpy as np
import concourse.bass as bass
import concourse.mybir as mybir
from concourse.bass_utils import run_bass_kernel_spmd

F32 = mybir.dt.float32
BF16 = mybir.dt.bfloat16
AF = mybir.ActivationFunctionType
ALU = mybir.AluOpType

D = 1024
DFF = 2816
L_FULL = 2
EPS = 1e-6
NEG = -32768.0
BIG = 1.0e4
IN_TOTAL = 9240
OFF_F1G, OFF_F1U, OFF_WIN = 0, 2816, 5632
OFF_WO = OFF_WIN + IN_TOTAL
OFF_F2G = OFF_WO + 1024
OFF_F2U = OFF_F2G + 2816
WA_COLS = OFF_F2U + 2816
NFM = 33
OFF_CG = NFM * 128
OFF_GATES = OFF_CG + 24
OFF_TM = OFF_GATES + 4096
(C_AQ, C_AK, C_CQ, C_CKS, C_CKW, C_CKC, C_CVC, C_DQ, C_DK, C_DG, C_BX, C_BB, C_BC) = (
    0, 4, 5, 9, 10, 11, 12, 13, 15, 17, 21, 25, 29)
V_N1, V_NM, V_N2, V_GB, V_AQG, V_AKG, V_CQG, V_CKG, V_CW, V_RG = 0, 8, 16, 24, 56, 57, 58, 59, 62, 74
NVEC = 78


class Buf:
    __slots__ = ("name", "last_w", "readers")

    def __init__(self, name=""):
        self.name = name
        self.last_w = None
        self.readers = []


class Ins:
    __slots__ = ("eng", "fn", "deps", "raw", "signal", "rank", "dma_slot", "dma_val", "is_dma", "pre_waits")

    def __init__(self, eng, fn, is_dma):
        self.eng = eng
        self.fn = fn
        self.deps = set()
        self.raw = set()
        self.signal = False
        self.rank = None
        self.is_dma = is_dma
        self.dma_slot = None
        self.dma_val = None
        self.pre_waits = []


ENGS = ("pe", "act", "dve", "pool", "sp")
N_HW_SEM = 24
N_SW_SEM = 8
N_DMA_SEM = N_HW_SEM + N_SW_SEM


class Prog:
    def __init__(self, nc):
        self.nc = nc
        self.ins = []
        self.eng_obj = {"pe": nc.tensor, "act": nc.scalar, "dve": nc.vector,
                        "pool": nc.gpsimd, "sp": nc.sync}
        self.dma_count = 0
        self.hw_count = 0
        self.sw_count = 0
        self.dma_slot_last = [None] * N_DMA_SEM

    def op(self, eng, fn, reads=(), writes=(), dma=False):
        i = Ins(eng, fn, dma)
        iid = len(self.ins)
        for b in reads:
            if b.last_w is not None:
                i.deps.add(b.last_w)
                i.raw.add(b.last_w)
        for b in writes:
            if b.last_w is not None:
                i.deps.add(b.last_w)
            for r in b.readers:
                i.deps.add(r)
        for b in reads:
            b.readers.append(iid)
        for b in writes:
            b.last_w = iid
            b.readers = []
        if dma:
            if eng == "pool":
                k = self.sw_count
                slot = N_HW_SEM + k % N_SW_SEM
                val = 16 * (k // N_SW_SEM + 1)
                self.sw_count += 1
            else:
                k = self.hw_count
                slot = k % N_HW_SEM
                val = 16 * (k // N_HW_SEM + 1)
                self.hw_count += 1
            prev = self.dma_slot_last[slot]
            if prev is not None:
                i.deps.add(prev)
            self.dma_slot_last[slot] = iid
            i.dma_slot = slot
            i.dma_val = val
            self.dma_count += 1
        i.deps.discard(iid)
        self.ins.append(i)
        return iid

    def emit(self, final_wait_eng="sp"):
        nc = self.nc
        ins = self.ins
        waited_eng = {e: {s: -1 for s in ENGS} for e in ENGS}
        waited_dma = {e: {} for e in ENGS}
        for iid, i in enumerate(ins):
            e = i.eng
            need_eng = {}
            need_dma = {}
            for d in i.deps:
                di = ins[d]
                if di.is_dma:
                    if need_dma.get(di.dma_slot, 0) < di.dma_val:
                        need_dma[di.dma_slot] = di.dma_val
                else:
                    if di.eng == e and not i.is_dma:
                        if e == "pe":
                            continue
                    if need_eng.get(di.eng, -1) < d:
                        need_eng[di.eng] = d
            for s, d in need_eng.items():
                if waited_eng[e][s] >= d:
                    continue
                waited_eng[e][s] = d
                ins[d].signal = True
                i.pre_waits.append(("eng", s, d))
            for slot, val in need_dma.items():
                if waited_dma[e].get(slot, 0) >= val:
                    continue
                waited_dma[e][slot] = val
                i.pre_waits.append(("dma", slot, val))
        rk = {e: 0 for e in ENGS}
        counts = {e: 0 for e in ENGS}
        for i in ins:
            counts[i.eng] += 1
            if i.signal and not i.is_dma:
                rk[i.eng] += 1
                i.rank = rk[i.eng]
        self.stats = dict(counts=counts, signals=dict(rk), n=len(ins), dmas=self.dma_count)
        with contextlib.ExitStack() as st:
            esem = {e: st.enter_context(nc.semaphore("s_" + e)) for e in ENGS}
            dsem = [st.enter_context(nc.semaphore("d_%d" % k)) for k in range(N_DMA_SEM)]
            for i in ins:
                eo = self.eng_obj[i.eng]
                for w in i.pre_waits:
                    if w[0] == "eng":
                        eo.wait_ge(esem[w[1]], ins[w[2]].rank)
                    else:
                        eo.wait_ge(dsem[w[1]], w[2])
                r = i.fn(eo)
                if i.is_dma:
                    r.then_inc(dsem[i.dma_slot], 16)
                elif i.signal:
                    r.then_inc(esem[i.eng], 1)
            eo = self.eng_obj[final_wait_eng]
            for slot in range(N_DMA_SEM):
                d = self.dma_slot_last[slot]
                if d is not None:
                    eo.wait_ge(dsem[slot], ins[d].dma_val)


def make_consts(T):
    NT = T // 128
    c = {}
    c["ident"] = np.eye(128, dtype=np.float32)
    bd = np.zeros((128, 128), np.float32)
    bd[:64, :64] = 1.0 / 64
    bd[64:, 64:] = 1.0 / 64
    c["ones_bd"] = bd
    c["ones_d"] = np.full((128, 128), 1.0 / 1024, np.float32)
    c["ones_v"] = np.full((128, 128), 1.0 / 128, np.float32)
    E = np.zeros((128, 32, 128), np.float32)
    for q in range(32):
        for m in range(128):
            s = 2 * q + m // 64
            E[s, q, m] = 1.0
            E[64 + s, q, m] = 1.0
    c["E"] = E.reshape(128, 32 * 128)
    n = np.arange(512)
    s = np.arange(128)
    cs = n * 16
    ce = cs + 31
    ss = s * 64
    ov = ((cs[:, None] < ss[None, :] + 64) & (ce[:, None] >= ss[None, :])).astype(np.float32)
    c["ovl"] = ov.reshape(4, 128, 128).transpose(1, 0, 2).reshape(128, 4 * 128)
    j = np.arange(128)
    e = np.arange(256)
    rel = (e[None, :] - 128) - (j[:, None] // 64)
    fb = np.zeros((128, 256), np.float32)
    fb[(rel == 0) | (rel == -1)] = BIG
    fb[rel > 0] = -BIG
    c["fb"] = fb
    f0 = np.zeros((128, 128), np.float32)
    f0[:, 0] = BIG
    c["f0"] = f0
    m = np.arange(128)
    c["mask_cur"] = np.where(m[:, None] <= j[None, :], 0.0, NEG).astype(np.float32)
    c["mask_prev"] = np.where(m[:, None] > j[None, :], 0.0, NEG).astype(np.float32)
    cm = np.zeros((128, 17, 128), np.float32)
    for dl in range(17):
        ok = (16 * m[:, None] + 31 - j[None, :]) <= 128 * dl
        cm[:, dl, :] = np.where(ok, 0.0, NEG)
    c["cm"] = cm.reshape(128, 17 * 128)
    sg = np.zeros((24, 12, 128), np.float32)
    for br in range(3):
        for r in range(4):
            for mm in range(128):
                h = 4 * (mm // 64) + r
                sg[h * 3 + br, br * 4 + r, mm] = 1.0
    c["selg"] = sg.reshape(24, 12 * 128)
    gam = 1.0 - 2.0 ** (-5.0 - np.arange(4, dtype=np.float64))
    lg = np.log(gam)
    diff = j[None, :].astype(np.float64) - m[:, None]
    dm = np.zeros((128, 4, 128), np.float64)
    for h in range(4):
        dm[:, h, :] = np.where(diff >= 0, np.exp(diff * lg[h]), 0.0) * 0.125
    c["dmask"] = dm.reshape(128, 512).astype(np.float32)
    xi = np.zeros((128, 2, 128), np.float64)
    zt = np.zeros((128, 2, 128), np.float64)
    dec = np.zeros((128, 2), np.float64)
    for pp in range(2):
        for hh in range(2):
            h = 2 * pp + hh
            xi[64 * hh:64 * hh + 64, pp, :] = np.exp((j[None, :] + 1.0) * lg[h])
            zt[:, pp, 64 * hh:64 * hh + 64] = (np.exp((127.0 - m) * lg[h]) * 0.125)[:, None]
            dec[64 * hh:64 * hh + 64, pp] = np.exp(128.0 * lg[h])
    c["xi"] = xi.reshape(128, 256).astype(np.float32)
    c["zt"] = zt.reshape(128, 256).astype(np.float32)
    c["dec"] = dec.astype(np.float32)
    rm = np.zeros((128, 128), np.float32)
    for mm in range(128):
        if mm % 64 < 32:
            rm[mm + 32, mm] = -1.0
        else:
            rm[mm - 32, mm] = 1.0
    c["rm"] = rm
    half = 32
    inv = 10000.0 ** (-np.arange(half, dtype=np.float32) / half)
    p = np.arange(128)
    ang = np.arange(T, dtype=np.float32)[None, :] * inv[p % 32][:, None]
    c["cosT"] = np.cos(ang).astype(np.float32)
    c["sinT"] = np.sin(ang).astype(np.float32)
    return c


CONST_SHAPES = lambda T: {k: v.shape for k, v in make_consts(128 if False else T).items()}


def win_perm():
    aq0, ak0, av0, bx0, bb0, bc0, cq0, ckc0, cvc0, cks0, cvs0, ckw0, cvw0, cg0, dq0, dk0, dv0, dg0, gl0 = (
        0, 512, 640, 768, 1280, 1792, 2304, 2816, 2944, 3072, 3200, 3328, 3456, 3584, 3608, 3864, 4120, 4632, 5144)
    cols = []
    r64 = np.arange(64)
    for r in range(4):
        cols += list(aq0 + (0 * 4 + r) * 64 + r64) + list(aq0 + (4 + r) * 64 + r64)
    cols += list(ak0 + np.arange(128))
    for r in range(4):
        cols += list(cq0 + (0 * 4 + r) * 64 + r64) + list(cq0 + (4 + r) * 64 + r64)
    cols += list(cks0 + np.arange(128)) + list(ckw0 + np.arange(128))
    cols += list(ckc0 + np.arange(128)) + list(cvc0 + np.arange(128))
    cols += list(dq0 + np.arange(256)) + list(dk0 + np.arange(256)) + list(dg0 + np.arange(512))
    cols += list(bx0 + np.arange(512)) + list(bb0 + np.arange(512)) + list(bc0 + np.arange(512))
    cols += list(cg0 + np.arange(24))
    cols += list(gl0 + np.arange(4096))
    cols += list(av0 + np.arange(128)) + list(cvs0 + np.arange(128)) + list(cvw0 + np.arange(128))
    cols += list(dv0 + np.arange(512))
    assert len(cols) == IN_TOTAL and len(set(cols)) == IN_TOTAL
    return np.array(cols)


def prep_weights(inp, L):
    perm = win_perm()
    WA = np.concatenate([inp["ffn1_w_gate"][:L], inp["ffn1_w_up"][:L], inp["w_in"][:L][:, :, perm],
                         inp["w_out"][:L], inp["ffn2_w_gate"][:L], inp["ffn2_w_up"][:L]], axis=2)
    WD = np.concatenate([inp["ffn1_w_down"][:L], inp["ffn2_w_down"][:L]], axis=2)
    WB = inp["w_branch"][:L]
    att_rows = []
    r64 = np.arange(64)
    for r in range(4):
        att_rows += list((0 * 4 + r) * 64 + r64) + list((4 + r) * 64 + r64)
    att_rows = np.array(att_rows)
    WB = WB.copy()
    WB[:, 0] = WB[:, 0][:, att_rows, :]
    WB[:, 2] = WB[:, 2][:, att_rows, :]
    WB = WB.reshape(L, 2048, 1024)

    def w1l(w):
        a = w[:L].reshape(L, 32, 64, 256).transpose(0, 2, 1, 3).reshape(L, 64, 32 * 256)
        return np.concatenate([a, a], axis=1)
    W1 = np.stack([w1l(inp["cmp_wk1"]), w1l(inp["cmp_wv1"])], axis=1)
    w2k = inp["cmp_wk2"][:L]
    z = np.zeros_like(w2k)
    W2K = np.stack([np.concatenate([w2k, z], axis=2), np.concatenate([z, w2k], axis=2)], axis=1)
    W2V = inp["cmp_wv2"][:L]
    pek = np.stack([inp["cmp_pos_k"][:L], inp["cmp_pos_v"][:L]], axis=1)
    peT = pek.transpose(0, 1, 3, 2)
    peT = np.concatenate([peT, peT], axis=2)
    vec = np.zeros((L, 128, NVEC), np.float32)

    def fm(v, n):
        return v.reshape(L, n, 128).transpose(0, 2, 1)
    vec[:, :, V_N1:V_N1 + 8] = fm(inp["ffn1_norm"][:L], 8)
    vec[:, :, V_NM:V_NM + 8] = fm(inp["mix_norm"][:L], 8)
    vec[:, :, V_N2:V_N2 + 8] = fm(inp["ffn2_norm"][:L], 8)
    vec[:, :, V_GB:V_GB + 32] = fm(inp["merge_gate_bias"][:L], 32)
    vec[:, :, V_AQG] = np.tile(inp["swa_q_gain"][:L], (1, 2))
    vec[:, :, V_AKG] = np.tile(inp["swa_k_gain"][:L], (1, 2))
    vec[:, :, V_CQG] = np.tile(inp["nsa_q_gain"][:L], (1, 2))
    for k in range(3):
        vec[:, :, V_CKG + k] = np.tile(inp["nsa_k_gain"][:L, k], (1, 2))
    cw = inp["conv_w"][:L].reshape(L, 3, 512)
    for k in range(3):
        vec[:, :, V_CW + 4 * k:V_CW + 4 * k + 4] = fm(cw[:, k], 4)
    vec[:, :, V_RG:V_RG + 4] = fm(inp["ret_norm_gain"][:L], 4)
    sk = inp["swa_sinks"][:L].reshape(L, 2, 4)
    sinks = np.repeat(sk, 64, axis=1)
    f = lambda a: np.ascontiguousarray(a, dtype=np.float32)
    return dict(WA=f(WA), WD=f(WD), WB=f(WB), W1=f(W1), W2K=f(W2K), W2V=f(W2V), peT=f(peT), vec=f(vec), sinks=f(sinks))


def build(T, L, TB=256, stop=None):
    nc = bass.Bass("TRN2", target_bir_lowering=False)
    P = Prog(nc)
    NT = T // 128
    NB = T // TB
    TPB = TB // 128
    NCT = max(1, T // 2048)
    RING = 8
    consts_np_shapes = {k: v.shape for k, v in make_consts(T).items()}

    def din(name, shape):
        return nc.dram_tensor(name, list(shape), F32, kind="ExternalInput").ap()

    xT_in = din("xT", (1024, T))
    WA = din("WA", (L, 1024, WA_COLS))
    WD = din("WD", (L, 2816, 2048))
    WB = din("WB", (L, 2048, 1024))
    W1 = din("W1", (L, 2, 128, 8192))
    W2K = din("W2K", (L, 2, 256, 128))
    W2V = din("W2V", (L, 256, 64))
    peT = din("peT", (L, 2, 128, 32))
    vec_in = din("vec", (L, 128, NVEC))
    sinks_in = din("sinks", (L, 128, 4))
    cin = {k: din("c_" + k, s) for k, s in consts_np_shapes.items()}
    outT = nc.dram_tensor("outT", [1024, T], F32, kind="ExternalOutput").ap()

    WA_bf = [nc.dram_tensor("WAbf%d" % l, [128, 8, WA_COLS], BF16).ap() for l in range(L)]
    WD_bf = [nc.dram_tensor("WDbf%d" % l, [128, 22, 2048], BF16).ap() for l in range(L)]
    WB_bf = [nc.dram_tensor("WBbf%d" % l, [128, 16, 1024], BF16).ap() for l in range(L)]
    W1_bf = [nc.dram_tensor("W1bf%d" % l, [2, 128, 8192], BF16).ap() for l in range(L)]
    xs = [nc.dram_tensor("xs%d" % l, [1024, T], F32).ap() for l in range(max(L - 1, 1))]
    B_wscr = [Buf("wscr%d" % l) for l in range(L)]
    B_xs = [[Buf() for _ in range(NB)] for _ in range(max(L - 1, 1))]

    def sb(name, shape, dt=F32):
        return nc.alloc_sbuf_tensor("s_" + name, list(shape), dt)

    def mm(out, lhsT, rhs, start, stop_, reads, writes):
        P.op("pe", lambda e: e.matmul(out, lhsT, rhs, start=start, stop=stop_), reads, writes)

    def tr(out, in_, ident, reads, writes):
        P.op("pe", lambda e: e.matmul(out, in_, ident, start=True, stop=True), reads, writes)

    def act(out, in_, func, reads, writes, bias=None, scale=None):
        kw = {}
        if bias is not None:
            kw["bias"] = bias
        if scale is not None:
            kw["scale"] = scale
        P.op("act", lambda e: e.activation(out=out, in_=in_, func=func, **kw), reads, writes)

    def cp(eng, out, in_, reads, writes):
        if eng == "act":
            P.op("act", lambda e: e.activation(out=out, in_=in_, func=AF.Copy), reads, writes)
        else:
            P.op(eng, lambda e: e.tensor_copy(out=out, in_=in_), reads, writes)

    def tt(eng, out, in0, in1, op, reads, writes):
        P.op(eng, lambda e: e.tensor_tensor(out=out, in0=in0, in1=in1, op=op), reads, writes)

    def ts(eng, out, in0, s1, s2, op0, op1, reads, writes):
        if op1 is None:
            P.op(eng, lambda e: e.tensor_scalar(out=out, in0=in0, scalar1=s1, scalar2=None, op0=op0), reads, writes)
        else:
            P.op(eng, lambda e: e.tensor_scalar(out=out, in0=in0, scalar1=s1, scalar2=s2, op0=op0, op1=op1), reads, writes)

    def stt(eng, out, in0, scalar, in1, op0, op1, reads, writes):
        P.op(eng, lambda e: e.scalar_tensor_tensor(out=out, in0=in0, scalar=scalar, in1=in1, op0=op0, op1=op1),
             reads, writes)

    def memset(eng, ap, val, writes):
        P.op(eng, lambda e: e.memset(ap, val), (), writes)

    def rsqrt_eps(out, in_, reads, writes):
        P.op("act", lambda e: e.activation(out=out, in_=in_, func=AF.Sqrt, bias=eps_col[0:out.shape[0], :], scale=1.0), list(reads) + [B_epscol], writes)
        P.op("dve", lambda e: e.reciprocal(out=out, in_=out), writes, writes)

    def recip_add(out, in_, addend, reads, writes):
        P.op("dve", lambda e: e.tensor_scalar(out=out, in0=in_, scalar1=addend, scalar2=None, op0=ALU.add), reads, writes)
        P.op("dve", lambda e: e.reciprocal(out=out, in_=out), writes, writes)

    def dma(q, out, in_, reads, writes):
        P.op(q, lambda e: e.dma_start(out=out, in_=in_), reads, writes, dma=True)

    def load_const(name, dt, q="pool", src=None, shape=None):
        src = cin[name] if src is None else src
        shape = consts_np_shapes[name] if shape is None else shape
        t = sb("k_" + name, shape, dt)
        b = Buf(name)
        dma("pool" if dt == BF16 else "sp", t[:], src, (), [b])
        return t, b

    ident_bf, B_ident = load_const("ident", BF16)
    ones_bd, B_obd = load_const("ones_bd", F32)
    ones_d, B_od = load_const("ones_d", F32)
    ones_v, B_ov = load_const("ones_v", F32)
    E_sb, B_E = load_const("E", BF16)
    ovl_sb, B_ovl = load_const("ovl", BF16)
    fb_sb, B_fb = load_const("fb", F32)
    f0_sb, B_f0 = load_const("f0", F32)
    mcur_sb, B_mcur = load_const("mask_cur", BF16)
    mprev_sb, B_mprev = load_const("mask_prev", BF16)
    cm_sb, B_cm = load_const("cm", BF16)
    selg_sb, B_selg = load_const("selg", BF16)
    dmask_sb, B_dmask = load_const("dmask", F32)
    xi_sb, B_xi = load_const("xi", F32)
    zt_sb, B_zt = load_const("zt", F32)
    dec_sb, B_dec = load_const("dec", F32)
    rm_sb, B_rm = load_const("rm", F32)
    eps_col = sb("eps_col", [128, 1], F32)
    B_epscol = Buf()
    memset("dve", eps_col[:], EPS, [B_epscol])
    ones_col = sb("ones_col", [128, 1], BF16)
    B_onescol = Buf()
    memset("dve", ones_col[:], 1.0, [B_onescol])
    CONSTB = [B_ident, B_obd, B_od, B_ov, B_E, B_ovl, B_fb, B_f0, B_mcur, B_mprev, B_cm, B_selg, B_dmask, B_xi,
              B_zt, B_dec, B_rm, B_onescol]

    pb = [nc.alloc_psum_tensor("pb%d" % k, [128, 512], F32) for k in range(7)]
    B_pb = [Buf("pb%d" % k) for k in range(7)]
    pbt = nc.alloc_psum_tensor("pbt", [128, 512], F32)
    B_pbt = Buf("pbt")
    mm_rot = [0]

    def mmbank():
        k = mm_rot[0] % 2
        mm_rot[0] += 1
        return pb[k], B_pb[k]

    xT = sb("xT", [128, 8, TB], F32); B_x = [Buf() for _ in range(8)]
    xn = sb("xn", [128, 8, TB], BF16); B_xn = [Buf() for _ in range(8)]
    hT = sb("hT", [128, 22, TB], BF16); B_h = [Buf() for _ in range(22)]
    yT = sb("yT", [128, 4, 4, TB], BF16); B_y = [[Buf() for _ in range(4)] for _ in range(4)]
    mrg = sb("mrg", [128, 8, TB], BF16); B_mrg = [Buf() for _ in range(8)]
    aqT = sb("aqT", [128, 2, 4, TB], BF16); B_aq = Buf()
    cqT = sb("cqT", [128, 2, 4, TB], BF16); B_cq = Buf()
    memset("dve", aqT[:].rearrange("p a b c -> p (a b c)"), 0.0, [B_aq])
    memset("dve", cqT[:].rearrange("p a b c -> p (a b c)"), 0.0, [B_cq])
    dqT = sb("dqT", [128, 2, TB], BF16); B_dq = Buf()
    dkT = sb("dkT", [128, 2, TB], BF16); B_dk = Buf()
    qxi = sb("qxi", [128, 2, TB], BF16); B_qxi = Buf()
    sgd = sb("sgd", [128, 4, TB], F32); B_sgd = Buf()
    cgs = sb("cgs", [24, TB], BF16); B_cgs = Buf()
    cosb = sb("cosb", [128, TB], F32); sinb = sb("sinb", [128, TB], F32); B_cs = Buf()
    vec = sb("vec", [128, NVEC], F32); B_vec = Buf()
    esink = sb("esink", [128, 4], F32); B_esink = Buf()
    ksT = sb("ksT", [128, T], BF16); B_ks = [Buf() for _ in range(NT)]
    vsA = sb("vsA", [128, NT, 192], BF16); B_vs = [Buf() for _ in range(NT)]
    akR = sb("akR", [128, RING, 128], BF16); B_akR = [Buf() for _ in range(RING)]
    avR = sb("avR", [128, RING, 192], BF16); B_avR = [Buf() for _ in range(RING)]
    kwR = sb("kwR", [128, RING, 128], BF16); B_kwR = [Buf() for _ in range(RING)]
    vwR = sb("vwR", [128, RING, 192], BF16); B_vwR = [Buf() for _ in range(RING)]
    dvR = sb("dvR", [128, TPB, 512], BF16); B_dvR = [Buf() for _ in range(TPB)]
    kcmpT = sb("kcmpT", [128, NCT * 128], BF16); B_kcmp = Buf()
    vcmpA = sb("vcmpA", [128, NCT, 192], BF16); B_vcmp = Buf()
    kcC = sb("kcC", [128, 16 + TB], BF16); vcC = sb("vcC", [128, 16 + TB], BF16); B_kcC = Buf(); B_vcC = Buf()
    zC = sb("zC", [128, 4, 2 + TB], F32); B_z = [Buf() for _ in range(4)]
    Rst = sb("Rst", [128, 2, 256], F32); Rbf = sb("Rbf", [128, 2, 256], BF16); B_R = Buf(); B_Rbf = Buf()
    w2k_sb = sb("w2k", [128, 2, 2, 128], BF16); w2v_sb = sb("w2v", [128, 2, 64], BF16); B_w2 = Buf()
    peT_sb = sb("peT", [128, 2, 32], BF16); B_pe = Buf()
    cbias = sb("cbias", [128, 2, 2], F32); B_cbias = Buf()
    NTMP = 5
    tmpf = [sb("tmpf%d" % k, [128, 512], F32) for k in range(NTMP)]; B_tmpf = [Buf() for _ in range(NTMP)]
    tf_rot = [0]

    def tmp():
        k = tf_rot[0] % NTMP
        tf_rot[0] += 1
        return tmpf[k], B_tmpf[k]
    ptb = [sb("ptb%d" % k, [128, 512], BF16) for k in range(5)]; B_ptb = [Buf() for _ in range(5)]
    pt_rot = [0]

    def ptmp():
        k = pt_rot[0] % 5
        pt_rot[0] += 1
        return ptb[k], B_ptb[k]
    obr = [sb("obr%d" % k, [128, 512], F32) for k in range(3)]; B_obr = [Buf() for _ in range(3)]
    selbT = sb("selbT", [128, 2, 2, 128], BF16); B_selbT = [Buf(), Buf()]
    memset("dve", selbT[:].rearrange("p a b c -> p (a b c)"), 0.0, B_selbT)
    impb = sb("impb", [128, 128], F32); impw = sb("impw", [128, 128], F32); m8 = sb("m8", [128, 16], F32)
    selb = sb("selb", [128, 2, 128], BF16); rdt = sb("rdt", [128, 4], F32)
    B_imp = Buf(); B_impw = Buf(); B_m8 = Buf(); B_selb = [Buf(), Buf()]; B_rdt = Buf()
    kz = sb("kz", [128, 2, 128], BF16); B_kz = Buf()
    WSLOT = 4096
    NW = 3
    wring = [sb("wr%d" % k, [128, WSLOT], BF16) for k in range(NW)]; B_wr = [Buf() for _ in range(NW)]
    w_rot = [0]

    def wload(src_ap, n_in, n_col):
        k = w_rot[0] % NW
        w_rot[0] += 1
        assert n_in * n_col <= WSLOT
        v = wring[k][:, 0:n_in * n_col].rearrange("p (a b) -> p a b", a=n_in)
        dma("sp", v, src_ap, [B_wscr_cur[0]], [B_wr[k]])
        return v, B_wr[k]

    B_wscr_cur = [None]

    def convert_weights(l):
        b = B_wscr[l]
        for kc in range(8):
            dma("pool", WA_bf[l][:, kc, :], WA[l, kc * 128:(kc + 1) * 128, :], (), [b])
        for kc in range(22):
            dma("pool", WD_bf[l][:, kc, :], WD[l, kc * 128:(kc + 1) * 128, :], (), [b])
        for kc in range(16):
            dma("pool", WB_bf[l][:, kc, :], WB[l, kc * 128:(kc + 1) * 128, :], (), [b])
        for kv in range(2):
            dma("pool", W1_bf[l][kv], W1[l, kv], (), [b])

    def rmsnorm_block(vcol):
        ps, Bp = pb[2], B_pb[2]
        for c in range(8):
            t, Bt = tmp()
            tt("pool", t[:, 0:TB], xT[:, c, :], xT[:, c, :], ALU.mult, [B_x[c]], [Bt])
            mm(ps[:, 0:TB], ones_d[:], t[:, 0:TB], c == 0, c == 7, [Bt, B_od], [Bp])
        r, Br = tmp()
        rsqrt_eps(r[:, 0:TB], ps[:, 0:TB], [Bp], [Br])
        for c in range(8):
            stt("dve", xn[:, c, :], xT[:, c, :], vec[:, vcol + c:vcol + c + 1], r[:, 0:TB], ALU.mult, ALU.mult,
                [B_x[c], B_vec, Br], [B_xn[c]])

    def ffn_block(l, og, ou, od):
        for jg in range(0, 22, 4):
            nj = min(4, 22 - jg)
            wg, Bwg = wload(WA_bf[l][:, :, og + jg * 128: og + (jg + nj) * 128], 8, nj * 128)
            wu, Bwu = wload(WA_bf[l][:, :, ou + jg * 128: ou + (jg + nj) * 128], 8, nj * 128)
            for jj in range(nj):
                j = jg + jj
                pg, Bg = pb[0], B_pb[0]
                pu, Bu = pb[1], B_pb[1]
                for kc in range(8):
                    mm(pg[:, 0:TB], wg[:, kc, jj * 128:(jj + 1) * 128], xn[:, kc, :], kc == 0, kc == 7,
                       [Bwg, B_xn[kc]], [Bg])
                for kc in range(8):
                    mm(pu[:, 0:TB], wu[:, kc, jj * 128:(jj + 1) * 128], xn[:, kc, :], kc == 0, kc == 7,
                       [Bwu, B_xn[kc]], [Bu])
                s, Bs = tmp()
                act(s[:, 0:TB], pg[:, 0:TB], AF.Silu, [Bg], [Bs])
                tt("dve", hT[:, j, :], s[:, 0:TB], pu[:, 0:TB], ALU.mult, [Bs, Bu], [B_h[j]])
        for mg in range(0, 8, 1):
            wd, Bwd = wload(WD_bf[l][:, :, od + mg * 128: od + (mg + 1) * 128], 22, 128)
            for m2 in range(1):
                m = mg + m2
                po, Bo = mmbank()
                for j in range(22):
                    mm(po[:, 0:TB], wd[:, j, m2 * 128:(m2 + 1) * 128], hT[:, j, :], j == 0, j == 21,
                       [Bwd, B_h[j]], [Bo])
                stt("dve", xT[:, m, :], po[:, 0:TB], 0.5, xT[:, m, :], ALU.mult, ALU.add, [Bo, B_x[m]], [B_x[m]])

    def headnorm(ps, Bp, gcol, dest, Bdest):
        q, Bq = tmp()
        cp("act", q[:, 0:TB], ps[:, 0:TB], [Bp], [Bq])
        s, Bs = tmp()
        tt("pool", s[:, 0:TB], q[:, 0:TB], q[:, 0:TB], ALU.mult, [Bq], [Bs])
        p2, Bp2 = pb[2], B_pb[2]
        mm(p2[:, 0:TB], ones_bd[:], s[:, 0:TB], True, True, [Bs, B_obd], [Bp2])
        r, Br = tmp()
        rsqrt_eps(r[:, 0:TB], p2[:, 0:TB], [Bp2], [Br])
        stt("dve", dest, q[:, 0:TB], vec[:, gcol:gcol + 1], r[:, 0:TB], ALU.mult, ALU.mult, [Bq, B_vec, Br], Bdest)

    def rotary(ps, Bp, dest, Bdest):
        q, Bq = tmp()
        cp("act", q[:, 0:TB], ps[:, 0:TB], [Bp], [Bq])
        p2, Bp2 = pb[2], B_pb[2]
        mm(p2[:, 0:TB], rm_sb[:], q[:, 0:TB], True, True, [Bq, B_rm], [Bp2])
        a, Ba = tmp()
        tt("pool", a[:, 0:TB], q[:, 0:TB], cosb[:], ALU.mult, [Bq, B_cs], [Ba])
        b_, Bb = tmp()
        tt("dve", b_[:, 0:TB], p2[:, 0:TB], sinb[:], ALU.mult, [Bp2, B_cs], [Bb])
        tt("dve", dest, a[:, 0:TB], b_[:, 0:TB], ALU.add, [Ba, Bb], Bdest)

    def attend(g, qbuf, Bq, tcol, ktiles, onorm_dest_idx, sink=False, dest=None):
        rows = slice(64 * g, 64 * g + 64)
        drows = slice(64 * (1 - g), 64 * (1 - g) + 64)
        ob = (6, 0, 1)[po_rot[0] % 3]
        po_rot[0] += 1
        po, Bo = pb[ob], B_pb[ob]
        dbuf, Bdbuf = dest if dest is not None else (obr[onorm_dest_idx], B_obr[onorm_dest_idx])
        qv = qbuf[:, g, :, tcol:tcol + 128]
        n = len(ktiles)
        pend = None
        for idx, (kl, Bk, va, Bv, masks) in enumerate(ktiles):
            sp_, Bs = pb[4 + idx % 2], B_pb[4 + idx % 2]
            mm(sp_[:].rearrange("p (a b) -> p a b", a=4), kl, qv, True, len(masks) == 0, [Bk, Bq], [Bs])
            for mi, (ml, mr, mb) in enumerate(masks):
                mm(sp_[:].rearrange("p (a b) -> p a b", a=4), ml, mr, False, mi == len(masks) - 1, mb, [Bs])
            pt, Bpt = ptmp()
            act(pt[:], sp_[:], AF.Exp, [Bs], [Bpt], scale=0.125)
            if pend is not None:
                pidx, ppt, pBpt, pva, pBv = pend
                mm(po[:], pva, ppt[:], pidx == 0, False, [pBv, pBpt], [Bo])
                yield ppt, pBpt
            pend = (idx, pt, Bpt, va, Bv)
        pidx, ppt, pBpt, pva, pBv = pend
        mm(po[:], pva, ppt[:], pidx == 0, True, [pBv, pBpt], [Bo])
        yield ppt, pBpt
        rd, Brd = tmp()
        if sink:
            for r in range(4):
                recip_add(rd[rows, r * 128:(r + 1) * 128], po[drows, r * 128:(r + 1) * 128],
                          esink[rows, r:r + 1], [Bo, B_esink], [Brd])
        else:
            recip_add(rd[rows, :], po[drows, :], 1e-30, [Bo], [Brd])
        tt("dve", dbuf[rows, :], po[rows, :], rd[rows, :], ALU.mult, [Bo, Brd], [Bdbuf])

    po_rot = [0]

    def bc4(ap2d):
        return ap2d.unsqueeze(1).to_broadcast([ap2d.shape[0], 4, 128])

    bxs = sb("bxs", [128, 4, TB], F32); B_bx = [Buf() for _ in range(4)]
    bbs = sb("bbs", [128, 4, TB], F32); B_bb = [Buf() for _ in range(4)]
    hc = sb("hc", [128, 128], F32); B_hc = Buf()
    hc2 = sb("hc2", [128, 128], F32); B_hc2 = Buf()
    gk = sb("gk", [128, 2, 128], BF16); B_gk = Buf()
    gpad = sb("gpad", [128, 2, 2, 2, 128], BF16); B_gpad = Buf()
    osb = sb("osb", [128, 512], F32); B_osb = Buf()
    macc = sb("macc", [128, 4, TB], F32); B_macc = [Buf() for _ in range(4)]

    def headnorm_w(ps_ap, Bp, gcol, dest, Bdest, W):
        q, Bq = tmp()
        cp("act", q[:, 0:W], ps_ap, [Bp], [Bq])
        s_, Bs = tmp()
        tt("pool", s_[:, 0:W], q[:, 0:W], q[:, 0:W], ALU.mult, [Bq], [Bs])
        p2, Bp2 = pb[2], B_pb[2]
        mm(p2[:, 0:W], ones_bd[:], s_[:, 0:W], True, True, [Bs, B_obd], [Bp2])
        r, Br = tmp()
        rsqrt_eps(r[:, 0:W], p2[:, 0:W], [Bp2], [Br])
        if isinstance(dest, tuple):
            for g_ in range(2):
                rw = slice(64 * g_, 64 * g_ + 64)
                stt("dve", dest[g_][rw, :], q[rw, 0:W], vec[rw, gcol:gcol + 1], r[rw, 0:W], ALU.mult, ALU.mult,
                    [Bq, B_vec, Br], Bdest)
        else:
            stt("dve", dest, q[:, 0:W], vec[:, gcol:gcol + 1], r[:, 0:W], ALU.mult, ALU.mult, [Bq, B_vec, Br], Bdest)

    memset("dve", hc[:], 0.0, [B_hc])

    import os as _os2
    _katt = _os2.environ.get("KATT", "ABCDEF")

    _kret = int(_os2.environ.get("KRET", "9"))

    def _rl(n):
        return n <= _kret

    def _en(t):
        return t in _katt

    def run_att(gen):
        for _ in gen:
            pass

    for l in range(L):
        convert_weights(l)
    for l in range(L):
        B_wscr_cur[0] = B_wscr[l]
        src = xT_in if l == 0 else xs[l - 1]
        dst = outT if l == L - 1 else xs[l]
        srcv = src.rearrange("(c p) t -> p c t", p=128)
        dstv = dst.rearrange("(c p) t -> p c t", p=128)
        dma("sp", vec[:], vec_in[l], (), [B_vec])
        sk, Bsk = tmp()
        dma("sp", sk[:, 0:4], sinks_in[l], (), [Bsk])
        act(esink[:], sk[:, 0:4], AF.Exp, [Bsk], [B_esink])
        dma("pool", w2k_sb[:].rearrange("p g c m -> p (g c) m"),
            W2K[l].rearrange("g (c p) m -> p (g c) m", p=128), (), [B_w2])
        dma("pool", w2v_sb[:], W2V[l].rearrange("(c p) m -> p c m", p=128), (), [B_w2])
        dma("pool", peT_sb[:], peT[l].rearrange("k p n -> p k n"), (), [B_pe])
        memset("pool", kcmpT[:], 0.0, [B_kcmp])
        memset("pool", vcmpA[:], 0.0, [B_vcmp])
        memset("pool", vcmpA[:, :, 64:128], 1.0, [B_vcmp])
        memset("pool", kcC[:], 0.0, [B_kcC])
        memset("pool", vcC[:], 0.0, [B_vcC])
        memset("pool", zC[:], 0.0, B_z)
        memset("pool", Rst[:], 0.0, [B_R])
        memset("pool", Rbf[:], 0.0, [B_Rbf])
        memset("pool", vsA[:, :, 64:128], 1.0, B_vs)
        memset("pool", avR[:, :, 64:128], 1.0, B_avR)
        memset("pool", vwR[:, :, 64:128], 1.0, B_vwR)
        for kv in range(2):
            for ch in range(2):
                for half in range(2):
                    w1, Bw1 = wload(W1_bf[l][kv][:, half * 4096:(half + 1) * 4096]
                                    .rearrange("p (a b) -> p a b", a=16)[:, :, ch * 128:(ch + 1) * 128], 16, 128)
                    for li in range(16):
                        lg_ = half * 16 + li
                        col = kv * 2 + ch
                        mm(pb[3][:, col:col + 1], w1[0:64, li, :], peT_sb[0:64, kv, lg_:lg_ + 1],
                           lg_ == 0, lg_ == 31, [Bw1, B_pe], [B_pb[3]])
        cp("dve", cbias[:].rearrange("p a b -> p (a b)"), pb[3][:, 0:4], [B_pb[3]], [B_cbias])

        for blk in range(NB):
            t0 = blk * TB
            dma("sp", xT[:], srcv[:, :, t0:t0 + TB], [B_xs[l - 1][blk]] if l > 0 else [], B_x)
            dma("sp", cosb[:], cin["cosT"][:, t0:t0 + TB], (), [B_cs])
            dma("sp", sinb[:], cin["sinT"][:, t0:t0 + TB], (), [B_cs])
            rmsnorm_block(V_N1)
            ffn_block(l, OFF_F1G, OFF_F1U, 0)
            if stop == "ffn1":
                dma("sp", dstv[:, :, t0:t0 + TB], xT[:], B_x, [B_xs[min(l, len(B_xs) - 1)][blk]])
                continue
            rmsnorm_block(V_NM)

            def wgroup(c0, ncols):
                return wload(WA_bf[l][:, :, OFF_WIN + c0: OFF_WIN + c0 + ncols], 8, ncols)

            def proj(w, Bw, off, M=128):
                ps, Bp = mmbank()
                for kc in range(8):
                    mm(ps[0:M, 0:TB], w[:, kc, off:off + M], xn[:, kc, :], kc == 0, kc == 7, [Bw, B_xn[kc]], [Bp])
                return ps, Bp
            slots = [(TPB * blk + ti) % RING for ti in range(TPB)]
            handlers = []
            for r in range(4):
                handlers.append(lambda ps, Bp, r=r: headnorm_w(ps[:, 0:TB], Bp, V_AQG, (aqT[:, 0, r, :], aqT[:, 1, r, :]), [B_aq], TB))

            def h_ring(ps, Bp, gcol, ring, Bring):
                tk, Btk = ptmp()
                headnorm_w(ps[:, 0:TB], Bp, gcol, tk[:, 0:TB], [Btk], TB)
                for ti in range(TPB):
                    cp("pool", ring[:, slots[ti], :], tk[:, ti * 128:(ti + 1) * 128], [Btk], [Bring[slots[ti]]])
            handlers.append(lambda ps, Bp: h_ring(ps, Bp, V_AKG, akR, B_akR))
            for r in range(4):
                handlers.append(lambda ps, Bp, r=r: headnorm_w(ps[:, 0:TB], Bp, V_CQG, (cqT[:, 0, r, :], cqT[:, 1, r, :]), [B_cq], TB))
            handlers.append(lambda ps, Bp: headnorm_w(ps[:, 0:TB], Bp, V_CKG + 1, ksT[:, t0:t0 + TB],
                                                       B_ks[blk * TPB:(blk + 1) * TPB], TB))
            handlers.append(lambda ps, Bp: h_ring(ps, Bp, V_CKG + 2, kwR, B_kwR))
            handlers.append(lambda ps, Bp: cp("act", kcC[:, 16:16 + TB], ps[:, 0:TB], [Bp], [B_kcC]))
            handlers.append(lambda ps, Bp: cp("act", vcC[:, 16:16 + TB], ps[:, 0:TB], [Bp], [B_vcC]))
            for pp in range(2):
                handlers.append(lambda ps, Bp, pp=pp: rotary(ps, Bp, dqT[:, pp, :], [B_dq]))
            for pp in range(2):
                handlers.append(lambda ps, Bp, pp=pp: rotary(ps, Bp, dkT[:, pp, :], [B_dk]))
            for h in range(4):
                handlers.append(lambda ps, Bp, h=h: act(sgd[:, h, :], ps[:, 0:TB], AF.Silu, [Bp], [B_sgd]))
            for c in range(4):
                handlers.append(lambda ps, Bp, c=c: cp("act", bxs[:, c, :], ps[:, 0:TB], [Bp], [B_bx[c]]))
            for c in range(4):
                handlers.append(lambda ps, Bp, c=c: cp("act", bbs[:, c, :], ps[:, 0:TB], [Bp], [B_bb[c]]))
            for c in range(4):
                handlers.append(lambda ps, Bp, c=c: tt("dve", zC[:, c, 2:2 + TB], ps[:, 0:TB], bxs[:, c, :], ALU.mult,
                                                       [Bp, B_bx[c]], [B_z[c]]))
            assert len(handlers) == NFM
            dfr = [None]
            for g0 in range(0, NFM, 4):
                ng = min(4, NFM - g0)
                ncols = ng * 128 + (24 if g0 + ng == NFM else 0)
                w, Bw = wgroup(g0 * 128, ncols)
                for k in range(ng):
                    ps, Bp = proj(w, Bw, k * 128)
                    if dfr[0] is not None:
                        dfr[0]()
                    dfr[0] = (lambda hh=handlers[g0 + k], ps=ps, Bp=Bp: hh(ps, Bp))
                if g0 + ng == NFM:
                    ps, Bp = proj(w, Bw, ng * 128, M=24)
                    dfr[0]()
                    dfr[0] = None
                    act(cgs[:], ps[0:24, 0:TB], AF.Sigmoid, [Bp], [B_cgs])
            if stop == "projfm":
                dma("sp", dstv[:, :, t0:t0 + TB], xT[:], B_x, [B_xs[min(l, len(B_xs) - 1)][blk]])
                continue
            wtm1, Bwtm1 = wload(WA_bf[l][:, :, OFF_WIN + OFF_TM: OFF_WIN + OFF_TM + 384], 8, 384)
            wtm2, Bwtm2 = wload(WA_bf[l][:, :, OFF_WIN + OFF_TM + 384: OFF_WIN + OFF_TM + 896], 8, 512)
            for ti in range(TPB):
                tile_i = blk * TPB + ti
                ps, Bp = mmbank()
                for kc in range(8):
                    mm(ps[:, 0:384], xn[:, kc, ti * 128:(ti + 1) * 128], wtm1[:, kc, :], kc == 0, kc == 7,
                       [Bwtm1, B_xn[kc]], [Bp])

                def vput(dst3, off, ps=ps):
                    return (dst3.rearrange("p (a b) -> p a b", a=3)[:, 0:3:2, :],
                            ps[:, off:off + 128].rearrange("p (a b) -> p a b", a=2))
                for (dst3, off, Bd, e0, e1) in ((avR[:, slots[ti], :], 0, B_avR[slots[ti]], "act", "dve"),
                                                (vsA[:, tile_i, :], 128, B_vs[tile_i], "dve", "act"),
                                                (vwR[:, slots[ti], :], 256, B_vwR[slots[ti]], "act", "dve")):
                    cp(e0, dst3[:, 0:64], ps[:, off:off + 64], [Bp], [Bd])
                    cp(e1, dst3[:, 128:192], ps[:, off + 64:off + 128], [Bp], [Bd])
                ps, Bp = mmbank()
                for kc in range(8):
                    mm(ps[:, :], xn[:, kc, ti * 128:(ti + 1) * 128], wtm2[:, kc, :], kc == 0, kc == 7,
                       [Bwtm2, B_xn[kc]], [Bp])
                cp("act", dvR[:, ti, :], ps[:, :], [Bp], [B_dvR[ti]])

            if stop == "projtm":
                dma("sp", dstv[:, :, t0:t0 + TB], xT[:], B_x, [B_xs[min(l, len(B_xs) - 1)][blk]])
                continue
            for c in range(4):
                a, Ba = tmp()
                ts("pool", a[:, 0:TB], zC[:, c, 0:TB], vec[:, V_CW + c:V_CW + c + 1], None, ALU.mult, None,
                   [B_z[c], B_vec], [Ba])
                stt("dve", a[:, 0:TB], zC[:, c, 1:1 + TB], vec[:, V_CW + 4 + c:V_CW + 5 + c], a[:, 0:TB],
                    ALU.mult, ALU.add, [B_z[c], B_vec, Ba], [Ba])
                stt("dve", a[:, 0:TB], zC[:, c, 2:2 + TB], vec[:, V_CW + 8 + c:V_CW + 9 + c], a[:, 0:TB],
                    ALU.mult, ALU.add, [B_z[c], B_vec, Ba], [Ba])
                tt("pool", yT[:, 1, c, :], a[:, 0:TB], bbs[:, c, :], ALU.mult, [Ba, B_bb[c]], [B_y[1][c]])
                cp("pool", zC[:, c, 0:2], zC[:, c, TB:TB + 2], [B_z[c]], [B_z[c]])

            if stop == "conv":
                dma("sp", dstv[:, :, t0:t0 + TB], xT[:], B_x, [B_xs[min(l, len(B_xs) - 1)][blk]])
                continue
            import os as _os
            if _os.environ.get("KSKIP", "") != "cmp":
                per = TB // 16
                n_lo = 0 if blk == 0 else per * blk - 1
                n_hi = per * (blk + 1) - 2
                nn = n_hi - n_lo + 1
                cst = 16 if blk == 0 else 0
                pieces = []
                n_ = n_lo
                while n_ <= n_hi:
                    e_ = min(n_hi, (n_ // 128) * 128 + 127)
                    pieces.append((n_ // 128, n_, e_ - n_ + 1))
                    n_ = e_ + 1
                for kv, (car, Bcar) in enumerate(((kcC, B_kcC), (vcC, B_vcC))):
                    for ch in range(2):
                        for half in range(2):
                            w1, Bw1 = wload(W1_bf[l][kv][:, half * 4096:(half + 1) * 4096]
                                            .rearrange("p (a b) -> p a b", a=16)[:, :, ch * 128:(ch + 1) * 128], 16, 128)
                            for li in range(16):
                                lg_ = half * 16 + li
                                for g in range(2):
                                    rows = slice(64 * g, 64 * g + 64)
                                    rhs = car[rows, cst + lg_: cst + lg_ + 16 * (nn - 1) + 1: 16]
                                    bk = 3 if g == 0 else 2
                                    mm(pb[bk][:, 0:nn], w1[rows, li, :], rhs, lg_ == 0, lg_ == 31,
                                       [Bw1, Bcar], [B_pb[bk]])
                        ts("dve", hc[:, 0:nn], pb[3][:, 0:nn], cbias[:, kv, ch:ch + 1], None, ALU.add, None,
                           [B_pb[3], B_cbias], [B_hc])
                        ts("dve", hc[:, 64:64 + nn], pb[2][:, 0:nn], cbias[:, kv, ch:ch + 1], None, ALU.add, None,
                           [B_pb[2], B_cbias], [B_hc])
                        tt("pool", hc2[:], hc[:], hc[:], ALU.mult, [B_hc], [B_hc2])
                        ts("pool", hc2[:], hc2[:], 0.044715, 1.0, ALU.mult, ALU.add, [B_hc2], [B_hc2])
                        tt("pool", hc2[:], hc2[:], hc[:], ALU.mult, [B_hc2, B_hc], [B_hc2])
                        act(hc2[:], hc2[:], AF.Tanh, [B_hc2], [B_hc2], scale=0.7978845608028654)
                        ts("pool", hc2[:], hc2[:], 1.0, 0.5, ALU.add, ALU.mult, [B_hc2], [B_hc2])
                        if kv == 0:
                            tt("pool", gk[:, ch, :], hc2[:], hc[:], ALU.mult, [B_hc2, B_hc], [B_gk])
                        else:
                            if ch == 0:
                                memset("pool", gpad[:].rearrange("p a b c d -> p (a b c d)"), 0.0, [B_gpad])
                            for pi, (ptile, pn, pcnt) in enumerate(pieces):
                                for g in range(2):
                                    o0 = g * 64 + (pn - n_lo)
                                    tt("pool", gpad[:, pi, g, ch, pn % 128: pn % 128 + pcnt],
                                       hc2[:, o0:o0 + pcnt], hc[:, o0:o0 + pcnt], ALU.mult, [B_hc2, B_hc], [B_gpad])
                cp("pool", kcC[:, 0:16], kcC[:, TB:TB + 16], [B_kcC], [B_kcC])
                cp("pool", vcC[:, 0:16], vcC[:, TB:TB + 16], [B_vcC], [B_vcC])
                first = True
                for ch in range(2):
                    for g in range(2):
                        mm(pb[3][:, 0:nn], w2k_sb[:, g, ch, :], gk[:, ch, g * 64: g * 64 + nn], first, ch == 1 and g == 1,
                           [B_w2, B_gk], [B_pb[3]])
                        first = False
                headnorm_w(pb[3][:, 0:nn], B_pb[3], V_CKG + 0, kcmpT[:, n_lo:n_lo + nn], [B_kcmp], nn)
                for pi, (ptile, pn, pcnt) in enumerate(pieces):
                    for g in range(2):
                        for ch in range(2):
                            mm(pb[3][:, 256 + g * 64: 256 + g * 64 + 64], gpad[:, pi, g, ch, :], w2v_sb[:, ch, :],
                               ch == 0, ch == 1, [B_gpad, B_w2], [B_pb[3]])
                    tt("dve", vcmpA[:, ptile, 0:64], vcmpA[:, ptile, 0:64], pb[3][:, 256:320], ALU.add,
                       [B_pb[3], B_vcmp], [B_vcmp])
                    tt("dve", vcmpA[:, ptile, 128:192], vcmpA[:, ptile, 128:192], pb[3][:, 320:384], ALU.add,
                       [B_pb[3], B_vcmp], [B_vcmp])


            if stop == "cmp":
                dma("sp", dstv[:, :, t0:t0 + TB], xT[:], B_x, [B_xs[min(l, len(B_xs) - 1)][blk]])
                continue
            for ti in range(TPB):
                i = blk * TPB + ti
                tc = ti * 128
                if _en('C'):
                    n_ct = min(NCT, i // 16 + 1)
                    for g in range(2):
                        rows = slice(64 * g, 64 * g + 64)
                        kts = []
                        for c in range(n_ct):
                            dl = i - 16 * c
                            masks = []
                            if dl <= 16:
                                masks = [(ident_bf[:], bc4(cm_sb[:, dl * 128:(dl + 1) * 128]), [B_ident, B_cm])]
                            kts.append((kcmpT[:, c * 128:(c + 1) * 128], B_kcmp, vcmpA[:, c, 64 * g: 64 * g + 128],
                                        B_vcmp, masks))
                        ip, Bip = pb[3], B_pb[3]
                        dp, Bdp = pb[2], B_pb[2]
                        pts = list(attend(g, cqT, B_cq, tc, kts, 0))
                        for r in range(4):
                            for c, (pt, Bpt) in enumerate(pts):
                                mm(ip[:, r * 128:(r + 1) * 128], pt[:, r * 128:(r + 1) * 128], ovl_sb[:, c * 128:(c + 1) * 128],
                                   c == 0, c == n_ct - 1, [Bpt, B_ovl], [Bip])
                        for r in range(4):
                            for c, (pt, Bpt) in enumerate(pts):
                                mm(dp[:, r:r + 1], pt[:, r * 128:(r + 1) * 128], ones_col[:], c == 0, c == n_ct - 1,
                                   [Bpt, B_onescol], [Bdp])
                        recip_add(rdt[:], dp[:, 0:4], 1e-30, [Bdp], [B_rdt])
                        ts("dve", impw[:], ip[:, 0:128], rdt[:, 0:1], None, ALU.mult, None, [Bip, B_rdt], [B_impw])
                        for r in range(1, 4):
                            stt("dve", impw[:], ip[:, r * 128:(r + 1) * 128], rdt[:, r:r + 1], impw[:], ALU.mult, ALU.add,
                                [Bip, B_rdt, B_impw], [B_impw])
                        tt("dve", impw[:], impw[:], fb_sb[:, 128 - 2 * i: 256 - 2 * i], ALU.add, [B_impw, B_fb], [B_impw])
                        tt("dve", impb[:], impw[:], f0_sb[:], ALU.add, [B_impw, B_f0], [B_imp])
                        P.op("dve", lambda e: e.max(out=m8[:, 0:8], in_=impb[:]), [B_imp], [B_m8])
                        P.op("dve", lambda e: e.match_replace(out=impw[:], in_to_replace=m8[:, 0:8], in_values=impb[:],
                                                              imm_value=-3.0e4), [B_imp, B_m8], [B_impw])
                        P.op("dve", lambda e: e.max(out=m8[:, 8:16], in_=impw[:]), [B_impw], [B_m8])
                        ts("dve", selb[:, g, :], impb[:], m8[:, 15:16], NEG, ALU.is_lt, ALU.mult, [B_imp, B_m8], [B_selb[g]])
                if _en('A'):
                    for g in range(2):
                        rows = slice(64 * g, 64 * g + 64)
                        kts = []
                        for kt in ([i - 1, i] if i >= 1 else [i]):
                            s_ = kt % RING
                            msk = mcur_sb if kt == i else mprev_sb
                            kts.append((akR[:, s_, :], B_akR[s_], avR[:, s_, 64 * g: 64 * g + 128], B_avR[s_],
                                        [(ident_bf[:], bc4(msk[:]), [B_ident, B_mcur, B_mprev])]))
                        run_att(attend(g, aqT, B_aq, tc, kts, None, sink=True, dest=(osb, B_osb)))
                    for r in range(4):
                        cp("act", yT[:, 0, r, tc:tc + 128], osb[:, r * 128:(r + 1) * 128], [B_osb], [B_y[0][r]])
                if _en('B'):
                    for g in range(2):
                        rows = slice(64 * g, 64 * g + 64)
                        kts = []
                        for kt in range(max(0, i - 4), i + 1):
                            s_ = kt % RING
                            masks = []
                            if kt == i:
                                masks = [(ident_bf[:], bc4(mcur_sb[:]), [B_ident, B_mcur])]
                            elif kt == i - 4:
                                masks = [(ident_bf[:], bc4(mprev_sb[:]), [B_ident, B_mprev])]
                            kts.append((kwR[:, s_, :], B_kwR[s_], vwR[:, s_, 64 * g: 64 * g + 128], B_vwR[s_], masks))
                        run_att(attend(g, cqT, B_cq, tc, kts, 2))
                if _en('F'):
                    ap_, Bap = pb[4], B_pb[4]
                    for h in range(4):
                        pp, hh = h // 2, h % 2
                        rows = slice(64 * hh, 64 * hh + 64)
                        mm(pb[4 + hh][:, pp * 128:(pp + 1) * 128], dkT[rows, pp, tc:tc + 128], dqT[rows, pp, tc:tc + 128],
                           True, True, [B_dk, B_dq], [B_pb[4 + hh]])
                    if _rl(1):
                        am, Bam = ptmp()
                        for h in range(4):
                            pp, hh = h // 2, h % 2
                            tt("dve", am[:, h * 128:(h + 1) * 128], pb[4 + hh][:, pp * 128:(pp + 1) * 128],
                               dmask_sb[:, h * 128:(h + 1) * 128], ALU.mult, [B_pb[4 + hh], B_dmask], [Bam])

                    if _rl(2):
                        for pp in range(2):
                            tt("dve", qxi[:, pp, tc:tc + 128], dqT[:, pp, tc:tc + 128], xi_sb[:, pp * 128:(pp + 1) * 128],
                               ALU.mult, [B_dq, B_xi], [B_qxi])

                    if _rl(3):
                        op_, Bop = pb[6], B_pb[6]
                        for h in range(4):
                            pp, hh = h // 2, h % 2
                            rows = slice(64 * hh, 64 * hh + 64)
                            mm(op_[:, h * 128:(h + 1) * 128], dvR[:, ti, h * 128:(h + 1) * 128], am[:, h * 128:(h + 1) * 128],
                               True, False, [B_dvR[ti], Bam], [Bop])
                            mm(op_[:, h * 128:(h + 1) * 128], Rbf[rows, pp, hh * 128:(hh + 1) * 128], qxi[rows, pp, tc:tc + 128],
                               False, True, [B_Rbf, B_qxi], [Bop])

                    if _rl(4):
                        for pp in range(2):
                            tr(pbt[:, 256 + pp * 128: 256 + (pp + 1) * 128], dkT[:, pp, tc:tc + 128], ident_bf[:],
                               [B_dk, B_ident], [B_pbt])
                        tt("dve", kz[:].rearrange("p a b -> p (a b)"), pbt[:, 256:512], zt_sb[:], ALU.mult, [B_pbt, B_zt], [B_kz])

                    if _rl(5):
                        sp2, Bsp2 = pb[5], B_pb[5]
                        for pp in range(2):
                            mm(sp2[:, pp * 256:(pp + 1) * 256], kz[:, pp, :], dvR[:, ti, pp * 256:(pp + 1) * 256], True, True,
                               [B_kz, B_dvR[ti]], [Bsp2])
                        for pp in range(2):
                            stt("dve", Rst[:, pp, :], Rst[:, pp, :], dec_sb[:, pp:pp + 1], sp2[:, pp * 256:(pp + 1) * 256],
                                ALU.mult, ALU.add, [B_R, B_dec, Bsp2], [B_R])

                    if _rl(6):
                        cp("act", Rbf[:].rearrange("p a b -> p (a b)"), Rst[:].rearrange("p a b -> p (a b)"), [B_R], [B_Rbf])

                    if _rl(7):
                        o2, Bo2 = tmp()
                        cp("act", o2[:], op_[:], [Bop], [Bo2])
                        mp, Bmp = pb[2], B_pb[2]
                        mm(mp[:], ones_v[:], o2[:], True, True, [Bo2, B_ov], [Bmp])
                        c2, Bc2 = tmp()
                        tt("dve", c2[:], o2[:], mp[:], ALU.subtract, [Bo2, Bmp], [Bc2])

                    if _rl(8):
                        s2, Bs2 = tmp()
                        tt("pool", s2[:], c2[:], c2[:], ALU.mult, [Bc2], [Bs2])
                        mm(mp[:], ones_v[:], s2[:], True, True, [Bs2, B_ov], [Bmp])
                        r2, Br2 = tmp()
                        rsqrt_eps(r2[:], mp[:], [Bmp], [Br2])
                        tt("pool", c2[:], c2[:], r2[:], ALU.mult, [Bc2, Br2], [Bc2])
                        for h in range(4):
                            stt("dve", yT[:, 3, h, tc:tc + 128], c2[:, h * 128:(h + 1) * 128], vec[:, V_RG + h:V_RG + h + 1],
                                sgd[:, h, tc:tc + 128], ALU.mult, ALU.mult, [Bc2, B_vec, B_sgd], [B_y[3][h]])


                if _en('D'):
                    for g in range(2):
                        tr(pbt[:, g * 128:(g + 1) * 128], selb[:, g, :], ident_bf[:], [B_selb[g], B_ident], [B_pbt])
                        cp("act", selbT[0:64, g, 0, :], pbt[0:64, g * 128:(g + 1) * 128], [B_pbt], [B_selbT[g]])
                        cp("act", selbT[64:128, g, 1, :], pbt[64:128, g * 128:(g + 1) * 128], [B_pbt], [B_selbT[g]])
                    for g in range(2):
                        rows = slice(64 * g, 64 * g + 64)
                        kts = []
                        for kt in range(0, i + 1):
                            hf = 0 if kt < 32 else 1
                            masks = [(E_sb[:, (kt % 32) * 128:(kt % 32 + 1) * 128],
                                      bc4(selbT[:, g, hf, :]), [B_E, B_selbT[g]])]
                            if kt == i:
                                masks.append((ident_bf[:], bc4(mcur_sb[:]), [B_ident, B_mcur]))
                            kts.append((ksT[:, kt * 128:(kt + 1) * 128], B_ks[kt], vsA[:, kt, 64 * g: 64 * g + 128],
                                        B_vs[kt], masks))
                        run_att(attend(g, cqT, B_cq, tc, kts, 1))
                if _en('E'):
                    for br, srcb, Bsrc in ((0, obr[0], B_obr[0]), (1, obr[1], B_obr[1]), (2, obr[2], B_obr[2])):
                        gp, Bgp = (pb[3], B_pb[3]) if br != 1 else (pb[2], B_pb[2])
                        for r in range(4):
                            mm(gp[:, r * 128:(r + 1) * 128], selg_sb[:, (br * 4 + r) * 128:(br * 4 + r + 1) * 128],
                               cgs[:, tc:tc + 128], True, True, [B_selg, B_cgs], [Bgp])
                        if br == 0:
                            tt("dve", osb[:], gp[:], srcb[:], ALU.mult, [Bgp, Bsrc], [B_osb])
                        else:
                            t_, Bt_ = tmp()
                            tt("dve", t_[:], gp[:], srcb[:], ALU.mult, [Bgp, Bsrc], [Bt_])
                            tt("pool", osb[:], osb[:], t_[:], ALU.add, [B_osb, Bt_], [B_osb])
                    for r in range(4):
                        cp("act", yT[:, 2, r, tc:tc + 128], osb[:, r * 128:(r + 1) * 128], [B_osb], [B_y[2][r]])

                if stop == "att":
                    dma("sp", dstv[:, :, t0:t0 + TB], xT[:], B_x, [B_xs[min(l, len(B_xs) - 1)][blk]])
                    continue
            for m0 in range(0, 8, 4):
                for bi in range(4):
                    wbv, Bwb = wload(WB_bf[l][:, bi * 4:(bi + 1) * 4, m0 * 128:(m0 + 4) * 128], 4, 512)
                    gvw, Bgv = wload(WA_bf[l][:, :, OFF_WIN + OFF_GATES + bi * 1024 + m0 * 128:
                                              OFF_WIN + OFF_GATES + bi * 1024 + (m0 + 4) * 128], 8, 512)
                    for m in range(m0, m0 + 4):
                        mo = (m - m0) * 128
                        pg, Bg = pb[0], B_pb[0]
                        for kc in range(8):
                            mm(pg[:, 0:TB], gvw[:, kc, mo:mo + 128], xn[:, kc, :], kc == 0, kc == 7, [Bgv, B_xn[kc]], [Bg])
                        gs, Bgs = tmp()
                        act(gs[:, 0:TB], pg[:, 0:TB], AF.Sigmoid, [Bg, B_vec], [Bgs],
                            bias=vec[:, V_GB + bi * 8 + m: V_GB + bi * 8 + m + 1])
                        pbk, Bbk = pb[1], B_pb[1]
                        for c in range(4):
                            mm(pbk[:, 0:TB], wbv[:, c, mo:mo + 128], yT[:, bi, c, :], c == 0, c == 3,
                               [Bwb, B_y[bi][c]], [Bbk])
                        if bi == 0:
                            tt("dve", macc[:, m - m0, :], pbk[:, 0:TB], gs[:, 0:TB], ALU.mult, [Bbk, Bgs], [B_macc[m - m0]])
                        else:
                            t_, Bt_ = tmp()
                            tt("dve", t_[:, 0:TB], pbk[:, 0:TB], gs[:, 0:TB], ALU.mult, [Bbk, Bgs], [Bt_])
                            if bi < 3:
                                tt("pool", macc[:, m - m0, :], macc[:, m - m0, :], t_[:, 0:TB], ALU.add,
                                   [B_macc[m - m0], Bt_], [B_macc[m - m0]])
                            else:
                                tt("pool", mrg[:, m, :], macc[:, m - m0, :], t_[:, 0:TB], ALU.add,
                                   [B_macc[m - m0], Bt_], [B_mrg[m]])
            for m0 in range(0, 8, 4):
                wo, Bwo = wload(WA_bf[l][:, :, OFF_WO + m0 * 128: OFF_WO + (m0 + 4) * 128], 8, 512)
                for m in range(m0, m0 + 4):
                    po, Bo = mmbank()
                    for kc in range(8):
                        mm(po[:, 0:TB], wo[:, kc, (m - m0) * 128:(m - m0 + 1) * 128], mrg[:, kc, :], kc == 0, kc == 7,
                           [Bwo, B_mrg[kc]], [Bo])
                    tt("dve", xT[:, m, :], xT[:, m, :], po[:, 0:TB], ALU.add, [B_x[m], Bo], [B_x[m]])
            if stop != "mix":
                rmsnorm_block(V_N2)
                ffn_block(l, OFF_F2G, OFF_F2U, 1024)
            dma("sp", dstv[:, :, t0:t0 + TB], xT[:], B_x, [B_xs[min(l, len(B_xs) - 1)][blk]])

    P.emit()
    return nc, P


_CACHE = {}


def run_cores(x, inp, L, T, n_cores, TB=256, stop=None):
    key = (T, L, TB, stop)
    if key not in _CACHE:
        _CACHE[key] = build(T, L, TB, stop)
    nc, P = _CACHE[key]
    w = prep_weights(inp, L)
    cs = make_consts(T)
    in_maps = []
    for c in range(n_cores):
        m = {"xT": np.ascontiguousarray(x[c].T.astype(np.float32))}
        m.update(w)
        for k, v in cs.items():
            m["c_" + k] = np.ascontiguousarray(v)
        in_maps.append(m)
    res = run_bass_kernel_spmd(nc, in_maps, core_ids=list(range(n_cores)))
    return np.stack([np.ascontiguousarray(r["outT"].T) for r in res.results], axis=0)


def kernel(**inputs):
    x = np.asarray(inputs["x"], dtype=np.float32)
    B, T, _ = x.shape
    inp = {k: np.asarray(v, dtype=np.float32) for k, v in inputs.items() if k != "x"}
    out = run_cores(x, inp, L_FULL, T, B)
    return out.astype(np.float32)
```

```python
import contextlib
import math
import numpy as np
import concourse.bass as bass
import concourse.mybir as mybir
from concourse.bass_utils import run_bass_kernel_spmd

F32 = mybir.dt.float32
BF16 = mybir.dt.bfloat16
AF = mybir.ActivationFunctionType
ALU = mybir.AluOpType

D = 1024
DFF = 2816
L_FULL = 2
EPS = 1e-6
NEG = -32768.0
BIG = 1.0e4
IN_TOTAL = 9240
OFF_F1G, OFF_F1U, OFF_WIN = 0, 2816, 5632
OFF_WO = OFF_WIN + IN_TOTAL
OFF_F2G = OFF_WO + 1024
OFF_F2U = OFF_F2G + 2816
WA_COLS = OFF_F2U + 2816
NFM = 33
OFF_CG = NFM * 128
OFF_GATES = OFF_CG + 24
OFF_TM = OFF_GATES + 4096
(C_AQ, C_AK, C_CQ, C_CKS, C_CKW, C_CKC, C_CVC, C_DQ, C_DK, C_DG, C_BX, C_BB, C_BC) = (
    0, 4, 5, 9, 10, 11, 12, 13, 15, 17, 21, 25, 29)
V_N1, V_NM, V_N2, V_GB, V_AQG, V_AKG, V_CQG, V_CKG, V_CW, V_RG = 0, 8, 16, 24, 56, 57, 58, 59, 62, 74
NVEC = 78


class Buf:
    __slots__ = ("name", "last_w", "readers")

    def __init__(self, name=""):
        self.name = name
        self.last_w = None
        self.readers = []


class Ins:
    __slots__ = ("eng", "fn", "deps", "raw", "signal", "rank", "dma_slot", "dma_val", "is_dma", "pre_waits")

    def __init__(self, eng, fn, is_dma):
        self.eng = eng
        self.fn = fn
        self.deps = set()
        self.raw = set()
        self.signal = False
        self.rank = None
        self.is_dma = is_dma
        self.dma_slot = None
        self.dma_val = None
        self.pre_waits = []


ENGS = ("pe", "act", "dve", "pool", "sp")
N_HW_SEM = 24
N_SW_SEM = 8
N_DMA_SEM = N_HW_SEM + N_SW_SEM


class Prog:
    def __init__(self, nc):
        self.nc = nc
        self.ins = []
        self.eng_obj = {"pe": nc.tensor, "act": nc.scalar, "dve": nc.vector,
                        "pool": nc.gpsimd, "sp": nc.sync}
        self.dma_count = 0
        self.hw_count = 0
        self.sw_count = 0
        self.dma_slot_last = [None] * N_DMA_SEM

    def op(self, eng, fn, reads=(), writes=(), dma=False):
        i = Ins(eng, fn, dma)
        iid = len(self.ins)
        for b in reads:
            if b.last_w is not None:
                i.deps.add(b.last_w)
                i.raw.add(b.last_w)
        for b in writes:
            if b.last_w is not None:
                i.deps.add(b.last_w)
            for r in b.readers:
                i.deps.add(r)
        for b in reads:
            b.readers.append(iid)
        for b in writes:
            b.last_w = iid
            b.readers = []
        if dma:
            if eng == "pool":
                k = self.sw_count
                slot = N_HW_SEM + k % N_SW_SEM
                val = 16 * (k // N_SW_SEM + 1)
                self.sw_count += 1
            else:
                k = self.hw_count
                slot = k % N_HW_SEM
                val = 16 * (k // N_HW_SEM + 1)
                self.hw_count += 1
            prev = self.dma_slot_last[slot]
            if prev is not None:
                i.deps.add(prev)
            self.dma_slot_last[slot] = iid
            i.dma_slot = slot
            i.dma_val = val
            self.dma_count += 1
        i.deps.discard(iid)
        self.ins.append(i)
        return iid

    def emit(self, final_wait_eng="sp"):
        nc = self.nc
        ins = self.ins
        waited_eng = {e: {s: -1 for s in ENGS} for e in ENGS}
        waited_dma = {e: {} for e in ENGS}
        for iid, i in enumerate(ins):
            e = i.eng
            need_eng = {}
            need_dma = {}
            for d in i.deps:
                di = ins[d]
                if di.is_dma:
                    if need_dma.get(di.dma_slot, 0) < di.dma_val:
                        need_dma[di.dma_slot] = di.dma_val
                else:
                    if di.eng == e and not i.is_dma:
                        if e == "pe":
                            continue
                    if need_eng.get(di.eng, -1) < d:
                        need_eng[di.eng] = d
            for s, d in need_eng.items():
                if waited_eng[e][s] >= d:
                    continue
                waited_eng[e][s] = d
                ins[d].signal = True
                i.pre_waits.append(("eng", s, d))
            for slot, val in need_dma.items():
                if waited_dma[e].get(slot, 0) >= val:
                    continue
                waited_dma[e][slot] = val
                i.pre_waits.append(("dma", slot, val))
        rk = {e: 0 for e in ENGS}
        counts = {e: 0 for e in ENGS}
        for i in ins:
            counts[i.eng] += 1
            if i.signal and not i.is_dma:
                rk[i.eng] += 1
                i.rank = rk[i.eng]
        self.stats = dict(counts=counts, signals=dict(rk), n=len(ins), dmas=self.dma_count)
        with contextlib.ExitStack() as st:
            esem = {e: st.enter_context(nc.semaphore("s_" + e)) for e in ENGS}
            dsem = [st.enter_context(nc.semaphore("d_%d" % k)) for k in range(N_DMA_SEM)]
            for i in ins:
                eo = self.eng_obj[i.eng]
                for w in i.pre_waits:
                    if w[0] == "eng":
                        eo.wait_ge(esem[w[1]], ins[w[2]].rank)
                    else:
                        eo.wait_ge(dsem[w[1]], w[2])
                r = i.fn(eo)
                if i.is_dma:
                    r.then_inc(dsem[i.dma_slot], 16)
                elif i.signal:
                    r.then_inc(esem[i.eng], 1)
            eo = self.eng_obj[final_wait_eng]
            for slot in range(N_DMA_SEM):
                d = self.dma_slot_last[slot]
                if d is not None:
                    eo.wait_ge(dsem[slot], ins[d].dma_val)


def make_consts(T):
    NT = T // 128
    c = {}
    c["ident"] = np.eye(128, dtype=np.float32)
    bd = np.zeros((128, 128), np.float32)
    bd[:64, :64] = 1.0 / 64
    bd[64:, 64:] = 1.0 / 64
    c["ones_bd"] = bd
    c["ones_d"] = np.full((128, 128), 1.0 / 1024, np.float32)
    c["ones_v"] = np.full((128, 128), 1.0 / 128, np.float32)
    E = np.zeros((128, 32, 128), np.float32)
    for q in range(32):
        for m in range(128):
            s = 2 * q + m // 64
            E[s, q, m] = 1.0
            E[64 + s, q, m] = 1.0
    c["E"] = E.reshape(128, 32 * 128)
    n = np.arange(512)
    s = np.arange(128)
    cs = n * 16
    ce = cs + 31
    ss = s * 64
    ov = ((cs[:, None] < ss[None, :] + 64) & (ce[:, None] >= ss[None, :])).astype(np.float32)
    c["ovl"] = ov.reshape(4, 128, 128).transpose(1, 0, 2).reshape(128, 4 * 128)
    j = np.arange(128)
    e = np.arange(256)
    rel = (e[None, :] - 128) - (j[:, None] // 64)
    fb = np.zeros((128, 256), np.float32)
    fb[(rel == 0) | (rel == -1)] = BIG
    fb[rel > 0] = -BIG
    c["fb"] = fb
    f0 = np.zeros((128, 128), np.float32)
    f0[:, 0] = BIG
    c["f0"] = f0
    m = np.arange(128)
    c["mask_cur"] = np.where(m[:, None] <= j[None, :], 0.0, NEG).astype(np.float32)
    c["mask_prev"] = np.where(m[:, None] > j[None, :], 0.0, NEG).astype(np.float32)
    cm = np.zeros((128, 17, 128), np.float32)
    for dl in range(17):
        ok = (16 * m[:, None] + 31 - j[None, :]) <= 128 * dl
        cm[:, dl, :] = np.where(ok, 0.0, NEG)
    c["cm"] = cm.reshape(128, 17 * 128)
    sg = np.zeros((24, 12, 128), np.float32)
    for br in range(3):
        for r in range(4):
            for mm in range(128):
                h = 4 * (mm // 64) + r
                sg[h * 3 + br, br * 4 + r, mm] = 1.0
    c["selg"] = sg.reshape(24, 12 * 128)
    gam = 1.0 - 2.0 ** (-5.0 - np.arange(4, dtype=np.float64))
    lg = np.log(gam)
    diff = j[None, :].astype(np.float64) - m[:, None]
    dm = np.zeros((128, 4, 128), np.float64)
    for h in range(4):
        dm[:, h, :] = np.where(diff >= 0, np.exp(diff * lg[h]), 0.0) * 0.125
    c["dmask"] = dm.reshape(128, 512).astype(np.float32)
    xi = np.zeros((128, 2, 128), np.float64)
    zt = np.zeros((128, 2, 128), np.float64)
    dec = np.zeros((128, 2), np.float64)
    for pp in range(2):
        for hh in range(2):
            h = 2 * pp + hh
            xi[64 * hh:64 * hh + 64, pp, :] = np.exp((j[None, :] + 1.0) * lg[h])
            zt[:, pp, 64 * hh:64 * hh + 64] = (np.exp((127.0 - m) * lg[h]) * 0.125)[:, None]
            dec[64 * hh:64 * hh + 64, pp] = np.exp(128.0 * lg[h])
    c["xi"] = xi.reshape(128, 256).astype(np.float32)
    c["zt"] = zt.reshape(128, 256).astype(np.float32)
    c["dec"] = dec.astype(np.float32)
    rm = np.zeros((128, 128), np.float32)
    for mm in range(128):
        if mm % 64 < 32:
            rm[mm + 32, mm] = -1.0
        else:
            rm[mm - 32, mm] = 1.0
    c["rm"] = rm
    half = 32
    inv = 10000.0 ** (-np.arange(half, dtype=np.float32) / half)
    p = np.arange(128)
    ang = np.arange(T, dtype=np.float32)[None, :] * inv[p % 32][:, None]
    c["cosT"] = np.cos(ang).astype(np.float32)
    c["sinT"] = np.sin(ang).astype(np.float32)
    return c


CONST_SHAPES = lambda T: {k: v.shape for k, v in make_consts(128 if False else T).items()}


def win_perm():
    aq0, ak0, av0, bx0, bb0, bc0, cq0, ckc0, cvc0, cks0, cvs0, ckw0, cvw0, cg0, dq0, dk0, dv0, dg0, gl0 = (
        0, 512, 640, 768, 1280, 1792, 2304, 2816, 2944, 3072, 3200, 3328, 3456, 3584, 3608, 3864, 4120, 4632, 5144)
    cols = []
    r64 = np.arange(64)
    for r in range(4):
        cols += list(aq0 + (0 * 4 + r) * 64 + r64) + list(aq0 + (4 + r) * 64 + r64)
    cols += list(ak0 + np.arange(128))
    for r in range(4):
        cols += list(cq0 + (0 * 4 + r) * 64 + r64) + list(cq0 + (4 + r) * 64 + r64)
    cols += list(cks0 + np.arange(128)) + list(ckw0 + np.arange(128))
    cols += list(ckc0 + np.arange(128)) + list(cvc0 + np.arange(128))
    cols += list(dq0 + np.arange(256)) + list(dk0 + np.arange(256)) + list(dg0 + np.arange(512))
    cols += list(bx0 + np.arange(512)) + list(bb0 + np.arange(512)) + list(bc0 + np.arange(512))
    cols += list(cg0 + np.arange(24))
    cols += list(gl0 + np.arange(4096))
    cols += list(av0 + np.arange(128)) + list(cvs0 + np.arange(128)) + list(cvw0 + np.arange(128))
    cols += list(dv0 + np.arange(512))
    assert len(cols) == IN_TOTAL and len(set(cols)) == IN_TOTAL
    return np.array(cols)


def prep_weights(inp, L):
    perm = win_perm()
    WA = np.concatenate([inp["ffn1_w_gate"][:L], inp["ffn1_w_up"][:L], inp["w_in"][:L][:, :, perm],
                         inp["w_out"][:L], inp["ffn2_w_gate"][:L], inp["ffn2_w_up"][:L]], axis=2)
    WD = np.concatenate([inp["ffn1_w_down"][:L], inp["ffn2_w_down"][:L]], axis=2)
    WB = inp["w_branch"][:L]
    att_rows = []
    r64 = np.arange(64)
    for r in range(4):
        att_rows += list((0 * 4 + r) * 64 + r64) + list((4 + r) * 64 + r64)
    att_rows = np.array(att_rows)
    WB = WB.copy()
    WB[:, 0] = WB[:, 0][:, att_rows, :]
    WB[:, 2] = WB[:, 2][:, att_rows, :]
    WB = WB.reshape(L, 2048, 1024)

    def w1l(w):
        a = w[:L].reshape(L, 32, 64, 256).transpose(0, 2, 1, 3).reshape(L, 64, 32 * 256)
        return np.concatenate([a, a], axis=1)
    W1 = np.stack([w1l(inp["cmp_wk1"]), w1l(inp["cmp_wv1"])], axis=1)
    w2k = inp["cmp_wk2"][:L]
    z = np.zeros_like(w2k)
    W2K = np.stack([np.concatenate([w2k, z], axis=2), np.concatenate([z, w2k], axis=2)], axis=1)
    W2V = inp["cmp_wv2"][:L]
    pek = np.stack([inp["cmp_pos_k"][:L], inp["cmp_pos_v"][:L]], axis=1)
    peT = pek.transpose(0, 1, 3, 2)
    peT = np.concatenate([peT, peT], axis=2)
    vec = np.zeros((L, 128, NVEC), np.float32)

    def fm(v, n):
        return v.reshape(L, n, 128).transpose(0, 2, 1)
    vec[:, :, V_N1:V_N1 + 8] = fm(inp["ffn1_norm"][:L], 8)
    vec[:, :, V_NM:V_NM + 8] = fm(inp["mix_norm"][:L], 8)
    vec[:, :, V_N2:V_N2 + 8] = fm(inp["ffn2_norm"][:L], 8)
    vec[:, :, V_GB:V_GB + 32] = fm(inp["merge_gate_bias"][:L], 32)
    vec[:, :, V_AQG] = np.tile(inp["swa_q_gain"][:L], (1, 2))
    vec[:, :, V_AKG] = np.tile(inp["swa_k_gain"][:L], (1, 2))
    vec[:, :, V_CQG] = np.tile(inp["nsa_q_gain"][:L], (1, 2))
    for k in range(3):
        vec[:, :, V_CKG + k] = np.tile(inp["nsa_k_gain"][:L, k], (1, 2))
    cw = inp["conv_w"][:L].reshape(L, 3, 512)
    for k in range(3):
        vec[:, :, V_CW + 4 * k:V_CW + 4 * k + 4] = fm(cw[:, k], 4)
    vec[:, :, V_RG:V_RG + 4] = fm(inp["ret_norm_gain"][:L], 4)
    sk = inp["swa_sinks"][:L].reshape(L, 2, 4)
    sinks = np.repeat(sk, 64, axis=1)
    f = lambda a: np.ascontiguousarray(a, dtype=np.float32)
    return dict(WA=f(WA), WD=f(WD), WB=f(WB), W1=f(W1), W2K=f(W2K), W2V=f(W2V), peT=f(peT), vec=f(vec), sinks=f(sinks))


def build(T, L, TB=256, stop=None):
    nc = bass.Bass("TRN2", target_bir_lowering=False)
    P = Prog(nc)
    NT = T // 128
    NB = T // TB
    TPB = TB // 128
    NCT = max(1, T // 2048)
    RING = 8
    consts_np_shapes = {k: v.shape for k, v in make_consts(T).items()}

    def din(name, shape):
        return nc.dram_tensor(name, list(shape), F32, kind="ExternalInput").ap()

    xT_in = din("xT", (1024, T))
    WA = din("WA", (L, 1024, WA_COLS))
    WD = din("WD", (L, 2816, 2048))
    WB = din("WB", (L, 2048, 1024))
    W1 = din("W1", (L, 2, 128, 8192))
    W2K = din("W2K", (L, 2, 256, 128))
    W2V = din("W2V", (L, 256, 64))
    peT = din("peT", (L, 2, 128, 32))
    vec_in = din("vec", (L, 128, NVEC))
    sinks_in = din("sinks", (L, 128, 4))
    cin = {k: din("c_" + k, s) for k, s in consts_np_shapes.items()}
    outT = nc.dram_tensor("outT", [1024, T], F32, kind="ExternalOutput").ap()

    WA_bf = [nc.dram_tensor("WAbf%d" % l, [128, 8, WA_COLS], BF16).ap() for l in range(L)]
    WD_bf = [nc.dram_tensor("WDbf%d" % l, [128, 22, 2048], BF16).ap() for l in range(L)]
    WB_bf = [nc.dram_tensor("WBbf%d" % l, [128, 16, 1024], BF16).ap() for l in range(L)]
    W1_bf = [nc.dram_tensor("W1bf%d" % l, [2, 128, 8192], BF16).ap() for l in range(L)]
    xs = [nc.dram_tensor("xs%d" % l, [1024, T], F32).ap() for l in range(max(L - 1, 1))]
    B_wscr = [Buf("wscr%d" % l) for l in range(L)]
    B_xs = [[Buf() for _ in range(NB)] for _ in range(max(L - 1, 1))]

    def sb(name, shape, dt=F32):
        return nc.alloc_sbuf_tensor("s_" + name, list(shape), dt)

    def mm(out, lhsT, rhs, start, stop_, reads, writes):
        P.op("pe", lambda e: e.matmul(out, lhsT, rhs, start=start, stop=stop_), reads, writes)

    def tr(out, in_, ident, reads, writes):
        P.op("pe", lambda e: e.matmul(out, in_, ident, start=True, stop=True), reads, writes)

    def act(out, in_, func, reads, writes, bias=None, scale=None):
        kw = {}
        if bias is not None:
            kw["bias"] = bias
        if scale is not None:
            kw["scale"] = scale
        P.op("act", lambda e: e.activation(out=out, in_=in_, func=func, **kw), reads, writes)

    def cp(eng, out, in_, reads, writes):
        if eng == "act":
            P.op("act", lambda e: e.activation(out=out, in_=in_, func=AF.Copy), reads, writes)
        else:
            P.op(eng, lambda e: e.tensor_copy(out=out, in_=in_), reads, writes)

    def tt(eng, out, in0, in1, op, reads, writes):
        P.op(eng, lambda e: e.tensor_tensor(out=out, in0=in0, in1=in1, op=op), reads, writes)

    def ts(eng, out, in0, s1, s2, op0, op1, reads, writes):
        if op1 is None:
            P.op(eng, lambda e: e.tensor_scalar(out=out, in0=in0, scalar1=s1, scalar2=None, op0=op0), reads, writes)
        else:
            P.op(eng, lambda e: e.tensor_scalar(out=out, in0=in0, scalar1=s1, scalar2=s2, op0=op0, op1=op1), reads, writes)

    def stt(eng, out, in0, scalar, in1, op0, op1, reads, writes):
        P.op(eng, lambda e: e.scalar_tensor_tensor(out=out, in0=in0, scalar=scalar, in1=in1, op0=op0, op1=op1),
             reads, writes)

    def memset(eng, ap, val, writes):
        P.op(eng, lambda e: e.memset(ap, val), (), writes)

    def rsqrt_eps(out, in_, reads, writes):
        P.op("act", lambda e: e.activation(out=out, in_=in_, func=AF.Sqrt, bias=eps_col[0:out.shape[0], :], scale=1.0), list(reads) + [B_epscol], writes)
        P.op("dve", lambda e: e.reciprocal(out=out, in_=out), writes, writes)

    def recip_add(out, in_, addend, reads, writes):
        P.op("dve", lambda e: e.tensor_scalar(out=out, in0=in_, scalar1=addend, scalar2=None, op0=ALU.add), reads, writes)
        P.op("dve", lambda e: e.reciprocal(out=out, in_=out), writes, writes)

    def dma(q, out, in_, reads, writes):
        P.op(q, lambda e: e.dma_start(out=out, in_=in_), reads, writes, dma=True)

    def load_const(name, dt, q="pool", src=None, shape=None):
        src = cin[name] if src is None else src
        shape = consts_np_shapes[name] if shape is None else shape
        t = sb("k_" + name, shape, dt)
        b = Buf(name)
        dma("pool" if dt == BF16 else "sp", t[:], src, (), [b])
        return t, b

    ident_bf, B_ident = load_const("ident", BF16)
    ones_bd, B_obd = load_const("ones_bd", F32)
    ones_d, B_od = load_const("ones_d", F32)
    ones_v, B_ov = load_const("ones_v", F32)
    E_sb, B_E = load_const("E", BF16)
    ovl_sb, B_ovl = load_const("ovl", BF16)
    fb_sb, B_fb = load_const("fb", F32)
    f0_sb, B_f0 = load_const("f0", F32)
    mcur_sb, B_mcur = load_const("mask_cur", BF16)
    mprev_sb, B_mprev = load_const("mask_prev", BF16)
    cm_sb, B_cm = load_const("cm", BF16)
    selg_sb, B_selg = load_const("selg", BF16)
    dmask_sb, B_dmask = load_const("dmask", F32)
    xi_sb, B_xi = load_const("xi", F32)
    zt_sb, B_zt = load_const("zt", F32)
    dec_sb, B_dec = load_const("dec", F32)
    rm_sb, B_rm = load_const("rm", F32)
    eps_col = sb("eps_col", [128, 1], F32)
    B_epscol = Buf()
    memset("dve", eps_col[:], EPS, [B_epscol])
    ones_col = sb("ones_col", [128, 1], BF16)
    B_onescol = Buf()
    memset("dve", ones_col[:], 1.0, [B_onescol])
    CONSTB = [B_ident, B_obd, B_od, B_ov, B_E, B_ovl, B_fb, B_f0, B_mcur, B_mprev, B_cm, B_selg, B_dmask, B_xi,
              B_zt, B_dec, B_rm, B_onescol]

    pb = [nc.alloc_psum_tensor("pb%d" % k, [128, 512], F32) for k in range(7)]
    B_pb = [Buf("pb%d" % k) for k in range(7)]
    pbt = nc.alloc_psum_tensor("pbt", [128, 512], F32)
    B_pbt = Buf("pbt")
    mm_rot = [0]

    def mmbank():
        k = mm_rot[0] % 2
        mm_rot[0] += 1
        return pb[k], B_pb[k]

    xT = sb("xT", [128, 8, TB], F32); B_x = [Buf() for _ in range(8)]
    xn = sb("xn", [128, 8, TB], BF16); B_xn = [Buf() for _ in range(8)]
    hT = sb("hT", [128, 22, TB], BF16); B_h = [Buf() for _ in range(22)]
    yT = sb("yT", [128, 4, 4, TB], BF16); B_y = [[Buf() for _ in range(4)] for _ in range(4)]
    mrg = sb("mrg", [128, 8, TB], BF16); B_mrg = [Buf() for _ in range(8)]
    aqT = sb("aqT", [128, 2, 4, TB], BF16); B_aq = Buf()
    cqT = sb("cqT", [128, 2, 4, TB], BF16); B_cq = Buf()
    memset("dve", aqT[:].rearrange("p a b c -> p (a b c)"), 0.0, [B_aq])
    memset("dve", cqT[:].rearrange("p a b c -> p (a b c)"), 0.0, [B_cq])
    dqT = sb("dqT", [128, 2, TB], BF16); B_dq = Buf()
    dkT = sb("dkT", [128, 2, TB], BF16); B_dk = Buf()
    qxi = sb("qxi", [128, 2, TB], BF16); B_qxi = Buf()
    sgd = sb("sgd", [128, 4, TB], F32); B_sgd = Buf()
    cgs = sb("cgs", [24, TB], BF16); B_cgs = Buf()
    cosb = sb("cosb", [128, TB], F32); sinb = sb("sinb", [128, TB], F32); B_cs = Buf()
    vec = sb("vec", [128, NVEC], F32); B_vec = Buf()
    esink = sb("esink", [128, 4], F32); B_esink = Buf()
    ksT = sb("ksT", [128, T], BF16); B_ks = [Buf() for _ in range(NT)]
    vsA = sb("vsA", [128, NT, 192], BF16); B_vs = [Buf() for _ in range(NT)]
    akR = sb("akR", [128, RING, 128], BF16); B_akR = [Buf() for _ in range(RING)]
    avR = sb("avR", [128, RING, 192], BF16); B_avR = [Buf() for _ in range(RING)]
    kwR = sb("kwR", [128, RING, 128], BF16); B_kwR = [Buf() for _ in range(RING)]
    vwR = sb("vwR", [128, RING, 192], BF16); B_vwR = [Buf() for _ in range(RING)]
    dvR = sb("dvR", [128, TPB, 512], BF16); B_dvR = [Buf() for _ in range(TPB)]
    kcmpT = sb("kcmpT", [128, NCT * 128], BF16); B_kcmp = Buf()
    vcmpA = sb("vcmpA", [128, NCT, 192], BF16); B_vcmp = Buf()
    kcC = sb("kcC", [128, 16 + TB], BF16); vcC = sb("vcC", [128, 16 + TB], BF16); B_kcC = Buf(); B_vcC = Buf()
    zC = sb("zC", [128, 4, 2 + TB], F32); B_z = [Buf() for _ in range(4)]
    Rst = sb("Rst", [128, 2, 256], F32); Rbf = sb("Rbf", [128, 2, 256], BF16); B_R = Buf(); B_Rbf = Buf()
    w2k_sb = sb("w2k", [128, 2, 2, 128], BF16); w2v_sb = sb("w2v", [128, 2, 64], BF16); B_w2 = Buf()
    peT_sb = sb("peT", [128, 2, 32], BF16); B_pe = Buf()
    cbias = sb("cbias", [128, 2, 2], F32); B_cbias = Buf()
    NTMP = 5
    tmpf = [sb("tmpf%d" % k, [128, 512], F32) for k in range(NTMP)]; B_tmpf = [Buf() for _ in range(NTMP)]
    tf_rot = [0]

    def tmp():
        k = tf_rot[0] % NTMP
        tf_rot[0] += 1
        return tmpf[k], B_tmpf[k]
    ptb = [sb("ptb%d" % k, [128, 512], BF16) for k in range(5)]; B_ptb = [Buf() for _ in range(5)]
    pt_rot = [0]

    def ptmp():
        k = pt_rot[0] % 5
        pt_rot[0] += 1
        return ptb[k], B_ptb[k]
    obr = [sb("obr%d" % k, [128, 512], F32) for k in range(3)]; B_obr = [Buf() for _ in range(3)]
    selbT = sb("selbT", [128, 2, 2, 128], BF16); B_selbT = [Buf(), Buf()]
    memset("dve", selbT[:].rearrange("p a b c -> p (a b c)"), 0.0, B_selbT)
    impb = sb("impb", [128, 128], F32); impw = sb("impw", [128, 128], F32); m8 = sb("m8", [128, 16], F32)
    selb = sb("selb", [128, 2, 128], BF16); rdt = sb("rdt", [128, 4], F32)
    B_imp = Buf(); B_impw = Buf(); B_m8 = Buf(); B_selb = [Buf(), Buf()]; B_rdt = Buf()
    kz = sb("kz", [128, 2, 128], BF16); B_kz = Buf()
    WSLOT = 3072
    NW = 4
    wring = [sb("wr%d" % k, [128, WSLOT], BF16) for k in range(NW)]; B_wr = [Buf() for _ in range(NW)]
    w_rot = [0]

    def wload(src_ap, n_in, n_col):
        k = w_rot[0] % NW
        w_rot[0] += 1
        assert n_in * n_col <= WSLOT
        v = wring[k][:, 0:n_in * n_col].rearrange("p (a b) -> p a b", a=n_in)
        dma("sp", v, src_ap, [B_wscr_cur[0]], [B_wr[k]])
        return v, B_wr[k]

    B_wscr_cur = [None]

    def convert_weights(l):
        b = B_wscr[l]
        for kc in range(8):
            dma("pool", WA_bf[l][:, kc, :], WA[l, kc * 128:(kc + 1) * 128, :], (), [b])
        for kc in range(22):
            dma("pool", WD_bf[l][:, kc, :], WD[l, kc * 128:(kc + 1) * 128, :], (), [b])
        for kc in range(16):
            dma("pool", WB_bf[l][:, kc, :], WB[l, kc * 128:(kc + 1) * 128, :], (), [b])
        for kv in range(2):
            dma("pool", W1_bf[l][kv], W1[l, kv], (), [b])

    def rmsnorm_block(vcol):
        ps, Bp = pb[2], B_pb[2]
        for c in range(8):
            t, Bt = tmp()
            tt("pool", t[:, 0:TB], xT[:, c, :], xT[:, c, :], ALU.mult, [B_x[c]], [Bt])
            mm(ps[:, 0:TB], ones_d[:], t[:, 0:TB], c == 0, c == 7, [Bt, B_od], [Bp])
        r, Br = tmp()
        rsqrt_eps(r[:, 0:TB], ps[:, 0:TB], [Bp], [Br])
        for c in range(8):
            stt("dve", xn[:, c, :], xT[:, c, :], vec[:, vcol + c:vcol + c + 1], r[:, 0:TB], ALU.mult, ALU.mult,
                [B_x[c], B_vec, Br], [B_xn[c]])

    def ffn_block(l, og, ou, od):
        for jg in range(0, 22, 3):
            nj = min(3, 22 - jg)
            wg, Bwg = wload(WA_bf[l][:, :, og + jg * 128: og + (jg + nj) * 128], 8, nj * 128)
            wu, Bwu = wload(WA_bf[l][:, :, ou + jg * 128: ou + (jg + nj) * 128], 8, nj * 128)
            for jj in range(nj):
                j = jg + jj
                pg, Bg = pb[0], B_pb[0]
                pu, Bu = pb[1], B_pb[1]
                for kc in range(8):
                    mm(pg[:, 0:TB], wg[:, kc, jj * 128:(jj + 1) * 128], xn[:, kc, :], kc == 0, kc == 7,
                       [Bwg, B_xn[kc]], [Bg])
                for kc in range(8):
                    mm(pu[:, 0:TB], wu[:, kc, jj * 128:(jj + 1) * 128], xn[:, kc, :], kc == 0, kc == 7,
                       [Bwu, B_xn[kc]], [Bu])
                s, Bs = tmp()
                act(s[:, 0:TB], pg[:, 0:TB], AF.Silu, [Bg], [Bs])
                tt("dve", hT[:, j, :], s[:, 0:TB], pu[:, 0:TB], ALU.mult, [Bs, Bu], [B_h[j]])
        for mg in range(0, 8, 1):
            wd, Bwd = wload(WD_bf[l][:, :, od + mg * 128: od + (mg + 1) * 128], 22, 128)
            for m2 in range(1):
                m = mg + m2
                po, Bo = mmbank()
                for j in range(22):
                    mm(po[:, 0:TB], wd[:, j, m2 * 128:(m2 + 1) * 128], hT[:, j, :], j == 0, j == 21,
                       [Bwd, B_h[j]], [Bo])
                stt("dve", xT[:, m, :], po[:, 0:TB], 0.5, xT[:, m, :], ALU.mult, ALU.add, [Bo, B_x[m]], [B_x[m]])

    def headnorm(ps, Bp, gcol, dest, Bdest):
        q, Bq = tmp()
        cp("act", q[:, 0:TB], ps[:, 0:TB], [Bp], [Bq])
        s, Bs = tmp()
        tt("pool", s[:, 0:TB], q[:, 0:TB], q[:, 0:TB], ALU.mult, [Bq], [Bs])
        p2, Bp2 = pb[2], B_pb[2]
        mm(p2[:, 0:TB], ones_bd[:], s[:, 0:TB], True, True, [Bs, B_obd], [Bp2])
        r, Br = tmp()
        rsqrt_eps(r[:, 0:TB], p2[:, 0:TB], [Bp2], [Br])
        stt("dve", dest, q[:, 0:TB], vec[:, gcol:gcol + 1], r[:, 0:TB], ALU.mult, ALU.mult, [Bq, B_vec, Br], Bdest)

    def rotary(ps, Bp, dest, Bdest):
        q, Bq = tmp()
        cp("act", q[:, 0:TB], ps[:, 0:TB], [Bp], [Bq])
        p2, Bp2 = pb[2], B_pb[2]
        mm(p2[:, 0:TB], rm_sb[:], q[:, 0:TB], True, True, [Bq, B_rm], [Bp2])
        a, Ba = tmp()
        tt("pool", a[:, 0:TB], q[:, 0:TB], cosb[:], ALU.mult, [Bq, B_cs], [Ba])
        b_, Bb = tmp()
        tt("dve", b_[:, 0:TB], p2[:, 0:TB], sinb[:], ALU.mult, [Bp2, B_cs], [Bb])
        tt("dve", dest, a[:, 0:TB], b_[:, 0:TB], ALU.add, [Ba, Bb], Bdest)

    def attend(g, qbuf, Bq, tcol, ktiles, onorm_dest_idx, sink=False, dest=None):
        rows = slice(64 * g, 64 * g + 64)
        drows = slice(64 * (1 - g), 64 * (1 - g) + 64)
        ob = (6, 0, 1)[po_rot[0] % 3]
        po_rot[0] += 1
        po, Bo = pb[ob], B_pb[ob]
        dbuf, Bdbuf = dest if dest is not None else (obr[onorm_dest_idx], B_obr[onorm_dest_idx])
        qv = qbuf[:, g, :, tcol:tcol + 128]
        n = len(ktiles)
        pend = None
        for idx, (kl, Bk, va, Bv, masks) in enumerate(ktiles):
            sp_, Bs = pb[4 + idx % 2], B_pb[4 + idx % 2]
            mm(sp_[:].rearrange("p (a b) -> p a b", a=4), kl, qv, True, len(masks) == 0, [Bk, Bq], [Bs])
            for mi, (ml, mr, mb) in enumerate(masks):
                mm(sp_[:].rearrange("p (a b) -> p a b", a=4), ml, mr, False, mi == len(masks) - 1, mb, [Bs])
            pt, Bpt = ptmp()
            act(pt[:], sp_[:], AF.Exp, [Bs], [Bpt], scale=0.125)
            if pend is not None:
                pidx, ppt, pBpt, pva, pBv = pend
                mm(po[:], pva, ppt[:], pidx == 0, False, [pBv, pBpt], [Bo])
                yield ppt, pBpt
            pend = (idx, pt, Bpt, va, Bv)
        pidx, ppt, pBpt, pva, pBv = pend
        mm(po[:], pva, ppt[:], pidx == 0, True, [pBv, pBpt], [Bo])
        yield ppt, pBpt
        rd, Brd = tmp()
        if sink:
            for r in range(4):
                recip_add(rd[rows, r * 128:(r + 1) * 128], po[drows, r * 128:(r + 1) * 128],
                          esink[rows, r:r + 1], [Bo, B_esink], [Brd])
        else:
            recip_add(rd[rows, :], po[drows, :], 1e-30, [Bo], [Brd])
        tt("dve", dbuf[rows, :], po[rows, :], rd[rows, :], ALU.mult, [Bo, Brd], [Bdbuf])

    po_rot = [0]

    def bc4(ap2d):
        return ap2d.unsqueeze(1).to_broadcast([ap2d.shape[0], 4, 128])

    bxs = sb("bxs", [128, 4, TB], F32); B_bx = [Buf() for _ in range(4)]
    bbs = sb("bbs", [128, 4, TB], F32); B_bb = [Buf() for _ in range(4)]
    hc = sb("hc", [128, 128], F32); B_hc = Buf()
    hc2 = sb("hc2", [128, 128], F32); B_hc2 = Buf()
    gk = sb("gk", [128, 2, 128], BF16); B_gk = Buf()
    gpad = sb("gpad", [128, 2, 2, 2, 128], BF16); B_gpad = Buf()
    osb = sb("osb", [128, 512], F32); B_osb = Buf()
    macc = sb("macc", [128, 4, TB], F32); B_macc = [Buf() for _ in range(4)]

    def headnorm_w(ps_ap, Bp, gcol, dest, Bdest, W):
        q, Bq = tmp()
        cp("act", q[:, 0:W], ps_ap, [Bp], [Bq])
        s_, Bs = tmp()
        tt("pool", s_[:, 0:W], q[:, 0:W], q[:, 0:W], ALU.mult, [Bq], [Bs])
        p2, Bp2 = pb[2], B_pb[2]
        mm(p2[:, 0:W], ones_bd[:], s_[:, 0:W], True, True, [Bs, B_obd], [Bp2])
        r, Br = tmp()
        rsqrt_eps(r[:, 0:W], p2[:, 0:W], [Bp2], [Br])
        if isinstance(dest, tuple):
            for g_ in range(2):
                rw = slice(64 * g_, 64 * g_ + 64)
                stt("dve", dest[g_][rw, :], q[rw, 0:W], vec[rw, gcol:gcol + 1], r[rw, 0:W], ALU.mult, ALU.mult,
                    [Bq, B_vec, Br], Bdest)
        else:
            stt("dve", dest, q[:, 0:W], vec[:, gcol:gcol + 1], r[:, 0:W], ALU.mult, ALU.mult, [Bq, B_vec, Br], Bdest)

    memset("dve", hc[:], 0.0, [B_hc])

    import os as _os2
    _katt = _os2.environ.get("KATT", "ABCDEF")

    _kret = int(_os2.environ.get("KRET", "9"))

    def _rl(n):
        return n <= _kret

    def _en(t):
        return t in _katt

    def run_att(gen):
        for _ in gen:
            pass

    for l in range(L):
        convert_weights(l)
    for l in range(L):
        B_wscr_cur[0] = B_wscr[l]
        src = xT_in if l == 0 else xs[l - 1]
        dst = outT if l == L - 1 else xs[l]
        srcv = src.rearrange("(c p) t -> p c t", p=128)
        dstv = dst.rearrange("(c p) t -> p c t", p=128)
        dma("sp", vec[:], vec_in[l], (), [B_vec])
        sk, Bsk = tmp()
        dma("sp", sk[:, 0:4], sinks_in[l], (), [Bsk])
        act(esink[:], sk[:, 0:4], AF.Exp, [Bsk], [B_esink])
        dma("pool", w2k_sb[:].rearrange("p g c m -> p (g c) m"),
            W2K[l].rearrange("g (c p) m -> p (g c) m", p=128), (), [B_w2])
        dma("pool", w2v_sb[:], W2V[l].rearrange("(c p) m -> p c m", p=128), (), [B_w2])
        dma("pool", peT_sb[:], peT[l].rearrange("k p n -> p k n"), (), [B_pe])
        memset("pool", kcmpT[:], 0.0, [B_kcmp])
        memset("pool", vcmpA[:], 0.0, [B_vcmp])
        memset("pool", vcmpA[:, :, 64:128], 1.0, [B_vcmp])
        memset("pool", kcC[:], 0.0, [B_kcC])
        memset("pool", vcC[:], 0.0, [B_vcC])
        memset("pool", zC[:], 0.0, B_z)
        memset("pool", Rst[:], 0.0, [B_R])
        memset("pool", Rbf[:], 0.0, [B_Rbf])
        memset("pool", vsA[:, :, 64:128], 1.0, B_vs)
        memset("pool", avR[:, :, 64:128], 1.0, B_avR)
        memset("pool", vwR[:, :, 64:128], 1.0, B_vwR)
        for kv in range(2):
            for ch in range(2):
                for half in range(2):
                    w1, Bw1 = wload(W1_bf[l][kv][:, half * 4096:(half + 1) * 4096]
                                    .rearrange("p (a b) -> p a b", a=16)[:, :, ch * 128:(ch + 1) * 128], 16, 128)
                    for li in range(16):
                        lg_ = half * 16 + li
                        col = kv * 2 + ch
                        mm(pb[3][:, col:col + 1], w1[0:64, li, :], peT_sb[0:64, kv, lg_:lg_ + 1],
                           lg_ == 0, lg_ == 31, [Bw1, B_pe], [B_pb[3]])
        cp("dve", cbias[:].rearrange("p a b -> p (a b)"), pb[3][:, 0:4], [B_pb[3]], [B_cbias])

        for blk in range(NB):
            t0 = blk * TB
            dma("sp", xT[:], srcv[:, :, t0:t0 + TB], [B_xs[l - 1][blk]] if l > 0 else [], B_x)
            dma("sp", cosb[:], cin["cosT"][:, t0:t0 + TB], (), [B_cs])
            dma("sp", sinb[:], cin["sinT"][:, t0:t0 + TB], (), [B_cs])
            rmsnorm_block(V_N1)
            ffn_block(l, OFF_F1G, OFF_F1U, 0)
            if stop == "ffn1":
                dma("sp", dstv[:, :, t0:t0 + TB], xT[:], B_x, [B_xs[min(l, len(B_xs) - 1)][blk]])
                continue
            rmsnorm_block(V_NM)

            def wgroup(c0, ncols):
                return wload(WA_bf[l][:, :, OFF_WIN + c0: OFF_WIN + c0 + ncols], 8, ncols)

            def proj(w, Bw, off, M=128):
                ps, Bp = mmbank()
                for kc in range(8):
                    mm(ps[0:M, 0:TB], w[:, kc, off:off + M], xn[:, kc, :], kc == 0, kc == 7, [Bw, B_xn[kc]], [Bp])
                return ps, Bp
            slots = [(TPB * blk + ti) % RING for ti in range(TPB)]
            handlers = []
            for r in range(4):
                handlers.append(lambda ps, Bp, r=r: headnorm_w(ps[:, 0:TB], Bp, V_AQG, (aqT[:, 0, r, :], aqT[:, 1, r, :]), [B_aq], TB))

            def h_ring(ps, Bp, gcol, ring, Bring):
                tk, Btk = ptmp()
                headnorm_w(ps[:, 0:TB], Bp, gcol, tk[:, 0:TB], [Btk], TB)
                for ti in range(TPB):
                    cp("pool", ring[:, slots[ti], :], tk[:, ti * 128:(ti + 1) * 128], [Btk], [Bring[slots[ti]]])
            handlers.append(lambda ps, Bp: h_ring(ps, Bp, V_AKG, akR, B_akR))
            for r in range(4):
                handlers.append(lambda ps, Bp, r=r: headnorm_w(ps[:, 0:TB], Bp, V_CQG, (cqT[:, 0, r, :], cqT[:, 1, r, :]), [B_cq], TB))
            handlers.append(lambda ps, Bp: headnorm_w(ps[:, 0:TB], Bp, V_CKG + 1, ksT[:, t0:t0 + TB],
                                                       B_ks[blk * TPB:(blk + 1) * TPB], TB))
            handlers.append(lambda ps, Bp: h_ring(ps, Bp, V_CKG + 2, kwR, B_kwR))
            handlers.append(lambda ps, Bp: cp("act", kcC[:, 16:16 + TB], ps[:, 0:TB], [Bp], [B_kcC]))
            handlers.append(lambda ps, Bp: cp("act", vcC[:, 16:16 + TB], ps[:, 0:TB], [Bp], [B_vcC]))
            for pp in range(2):
                handlers.append(lambda ps, Bp, pp=pp: rotary(ps, Bp, dqT[:, pp, :], [B_dq]))
            for pp in range(2):
                handlers.append(lambda ps, Bp, pp=pp: rotary(ps, Bp, dkT[:, pp, :], [B_dk]))
            for h in range(4):
                handlers.append(lambda ps, Bp, h=h: act(sgd[:, h, :], ps[:, 0:TB], AF.Silu, [Bp], [B_sgd]))
            for c in range(4):
                handlers.append(lambda ps, Bp, c=c: cp("act", bxs[:, c, :], ps[:, 0:TB], [Bp], [B_bx[c]]))
            for c in range(4):
                handlers.append(lambda ps, Bp, c=c: cp("act", bbs[:, c, :], ps[:, 0:TB], [Bp], [B_bb[c]]))
            for c in range(4):
                handlers.append(lambda ps, Bp, c=c: tt("dve", zC[:, c, 2:2 + TB], ps[:, 0:TB], bxs[:, c, :], ALU.mult,
                                                       [Bp, B_bx[c]], [B_z[c]]))
            assert len(handlers) == NFM
            dfr = [None]
            for g0 in range(0, NFM, 3):
                ng = min(3, NFM - g0)
                ncols = ng * 128
                w, Bw = wgroup(g0 * 128, ncols)
                for k in range(ng):
                    ps, Bp = proj(w, Bw, k * 128)
                    if dfr[0] is not None:
                        dfr[0]()
                    dfr[0] = (lambda hh=handlers[g0 + k], ps=ps, Bp=Bp: hh(ps, Bp))
                if g0 + ng == NFM:
                    w, Bw = wgroup(OFF_CG, 24)
                    ps, Bp = proj(w, Bw, 0, M=24)
                    dfr[0]()
                    dfr[0] = None
                    act(cgs[:], ps[0:24, 0:TB], AF.Sigmoid, [Bp], [B_cgs])
            if stop == "projfm":
                dma("sp", dstv[:, :, t0:t0 + TB], xT[:], B_x, [B_xs[min(l, len(B_xs) - 1)][blk]])
                continue
            wtm1, Bwtm1 = wload(WA_bf[l][:, :, OFF_WIN + OFF_TM: OFF_WIN + OFF_TM + 384], 8, 384)
            wtm2a, Bwtm2a = wload(WA_bf[l][:, :, OFF_WIN + OFF_TM + 384: OFF_WIN + OFF_TM + 640], 8, 256)
            wtm2b, Bwtm2b = wload(WA_bf[l][:, :, OFF_WIN + OFF_TM + 640: OFF_WIN + OFF_TM + 896], 8, 256)
            for ti in range(TPB):
                tile_i = blk * TPB + ti
                ps, Bp = mmbank()
                for kc in range(8):
                    mm(ps[:, 0:384], xn[:, kc, ti * 128:(ti + 1) * 128], wtm1[:, kc, :], kc == 0, kc == 7,
                       [Bwtm1, B_xn[kc]], [Bp])

                def vput(dst3, off, ps=ps):
                    return (dst3.rearrange("p (a b) -> p a b", a=3)[:, 0:3:2, :],
                            ps[:, off:off + 128].rearrange("p (a b) -> p a b", a=2))
                for (dst3, off, Bd, e0, e1) in ((avR[:, slots[ti], :], 0, B_avR[slots[ti]], "act", "dve"),
                                                (vsA[:, tile_i, :], 128, B_vs[tile_i], "dve", "act"),
                                                (vwR[:, slots[ti], :], 256, B_vwR[slots[ti]], "act", "dve")):
                    cp(e0, dst3[:, 0:64], ps[:, off:off + 64], [Bp], [Bd])
                    cp(e1, dst3[:, 128:192], ps[:, off + 64:off + 128], [Bp], [Bd])
                ps, Bp = mmbank()
                for (wv_, Bwv_, c0_) in ((wtm2a, Bwtm2a, 0), (wtm2b, Bwtm2b, 256)):
                    for kc in range(8):
                        mm(ps[:, c0_:c0_ + 256], xn[:, kc, ti * 128:(ti + 1) * 128], wv_[:, kc, :], kc == 0, kc == 7,
                           [Bwv_, B_xn[kc]], [Bp])
                cp("act", dvR[:, ti, :], ps[:, :], [Bp], [B_dvR[ti]])

            if stop == "projtm":
                dma("sp", dstv[:, :, t0:t0 + TB], xT[:], B_x, [B_xs[min(l, len(B_xs) - 1)][blk]])
                continue
            for c in range(4):
                a, Ba = tmp()
                ts("pool", a[:, 0:TB], zC[:, c, 0:TB], vec[:, V_CW + c:V_CW + c + 1], None, ALU.mult, None,
                   [B_z[c], B_vec], [Ba])
                stt("dve", a[:, 0:TB], zC[:, c, 1:1 + TB], vec[:, V_CW + 4 + c:V_CW + 5 + c], a[:, 0:TB],
                    ALU.mult, ALU.add, [B_z[c], B_vec, Ba], [Ba])
                stt("dve", a[:, 0:TB], zC[:, c, 2:2 + TB], vec[:, V_CW + 8 + c:V_CW + 9 + c], a[:, 0:TB],
                    ALU.mult, ALU.add, [B_z[c], B_vec, Ba], [Ba])
                tt("pool", yT[:, 1, c, :], a[:, 0:TB], bbs[:, c, :], ALU.mult, [Ba, B_bb[c]], [B_y[1][c]])
                cp("pool", zC[:, c, 0:2], zC[:, c, TB:TB + 2], [B_z[c]], [B_z[c]])

            if stop == "conv":
                dma("sp", dstv[:, :, t0:t0 + TB], xT[:], B_x, [B_xs[min(l, len(B_xs) - 1)][blk]])
                continue
            import os as _os
            if _os.environ.get("KSKIP", "") != "cmp":
                per = TB // 16
                n_lo = 0 if blk == 0 else per * blk - 1
                n_hi = per * (blk + 1) - 2
                nn = n_hi - n_lo + 1
                cst = 16 if blk == 0 else 0
                pieces = []
                n_ = n_lo
                while n_ <= n_hi:
                    e_ = min(n_hi, (n_ // 128) * 128 + 127)
                    pieces.append((n_ // 128, n_, e_ - n_ + 1))
                    n_ = e_ + 1
                for kv, (car, Bcar) in enumerate(((kcC, B_kcC), (vcC, B_vcC))):
                    for ch in range(2):
                        for half in range(2):
                            w1, Bw1 = wload(W1_bf[l][kv][:, half * 4096:(half + 1) * 4096]
                                            .rearrange("p (a b) -> p a b", a=16)[:, :, ch * 128:(ch + 1) * 128], 16, 128)
                            for li in range(16):
                                lg_ = half * 16 + li
                                for g in range(2):
                                    rows = slice(64 * g, 64 * g + 64)
                                    rhs = car[rows, cst + lg_: cst + lg_ + 16 * (nn - 1) + 1: 16]
                                    bk = 3 if g == 0 else 2
                                    mm(pb[bk][:, 0:nn], w1[rows, li, :], rhs, lg_ == 0, lg_ == 31,
                                       [Bw1, Bcar], [B_pb[bk]])
                        ts("dve", hc[:, 0:nn], pb[3][:, 0:nn], cbias[:, kv, ch:ch + 1], None, ALU.add, None,
                           [B_pb[3], B_cbias], [B_hc])
                        ts("dve", hc[:, 64:64 + nn], pb[2][:, 0:nn], cbias[:, kv, ch:ch + 1], None, ALU.add, None,
                           [B_pb[2], B_cbias], [B_hc])
                        tt("pool", hc2[:], hc[:], hc[:], ALU.mult, [B_hc], [B_hc2])
                        ts("pool", hc2[:], hc2[:], 0.044715, 1.0, ALU.mult, ALU.add, [B_hc2], [B_hc2])
                        tt("pool", hc2[:], hc2[:], hc[:], ALU.mult, [B_hc2, B_hc], [B_hc2])
                        act(hc2[:], hc2[:], AF.Tanh, [B_hc2], [B_hc2], scale=0.7978845608028654)
                        ts("pool", hc2[:], hc2[:], 1.0, 0.5, ALU.add, ALU.mult, [B_hc2], [B_hc2])
                        if kv == 0:
                            tt("pool", gk[:, ch, :], hc2[:], hc[:], ALU.mult, [B_hc2, B_hc], [B_gk])
                        else:
                            if ch == 0:
                                memset("pool", gpad[:].rearrange("p a b c d -> p (a b c d)"), 0.0, [B_gpad])
                            for pi, (ptile, pn, pcnt) in enumerate(pieces):
                                for g in range(2):
                                    o0 = g * 64 + (pn - n_lo)
                                    tt("pool", gpad[:, pi, g, ch, pn % 128: pn % 128 + pcnt],
                                       hc2[:, o0:o0 + pcnt], hc[:, o0:o0 + pcnt], ALU.mult, [B_hc2, B_hc], [B_gpad])
                cp("pool", kcC[:, 0:16], kcC[:, TB:TB + 16], [B_kcC], [B_kcC])
                cp("pool", vcC[:, 0:16], vcC[:, TB:TB + 16], [B_vcC], [B_vcC])
                first = True
                for ch in range(2):
                    for g in range(2):
                        mm(pb[3][:, 0:nn], w2k_sb[:, g, ch, :], gk[:, ch, g * 64: g * 64 + nn], first, ch == 1 and g == 1,
                           [B_w2, B_gk], [B_pb[3]])
                        first = False
                headnorm_w(pb[3][:, 0:nn], B_pb[3], V_CKG + 0, kcmpT[:, n_lo:n_lo + nn], [B_kcmp], nn)
                for pi, (ptile, pn, pcnt) in enumerate(pieces):
                    for g in range(2):
                        for ch in range(2):
                            mm(pb[3][:, 256 + g * 64: 256 + g * 64 + 64], gpad[:, pi, g, ch, :], w2v_sb[:, ch, :],
                               ch == 0, ch == 1, [B_gpad, B_w2], [B_pb[3]])
                    tt("dve", vcmpA[:, ptile, 0:64], vcmpA[:, ptile, 0:64], pb[3][:, 256:320], ALU.add,
                       [B_pb[3], B_vcmp], [B_vcmp])
                    tt("dve", vcmpA[:, ptile, 128:192], vcmpA[:, ptile, 128:192], pb[3][:, 320:384], ALU.add,
                       [B_pb[3], B_vcmp], [B_vcmp])


            if stop == "cmp":
                dma("sp", dstv[:, :, t0:t0 + TB], xT[:], B_x, [B_xs[min(l, len(B_xs) - 1)][blk]])
                continue
            for ti in range(TPB):
                i = blk * TPB + ti
                tc = ti * 128
                if _en('C'):
                    n_ct = min(NCT, i // 16 + 1)
                    for g in range(2):
                        rows = slice(64 * g, 64 * g + 64)
                        kts = []
                        for c in range(n_ct):
                            dl = i - 16 * c
                            masks = []
                            if dl <= 16:
                                masks = [(ident_bf[:], bc4(cm_sb[:, dl * 128:(dl + 1) * 128]), [B_ident, B_cm])]
                            kts.append((kcmpT[:, c * 128:(c + 1) * 128], B_kcmp, vcmpA[:, c, 64 * g: 64 * g + 128],
                                        B_vcmp, masks))
                        ip, Bip = pb[3], B_pb[3]
                        dp, Bdp = pb[2], B_pb[2]
                        pts = list(attend(g, cqT, B_cq, tc, kts, 0))
                        for r in range(4):
                            for c, (pt, Bpt) in enumerate(pts):
                                mm(ip[:, r * 128:(r + 1) * 128], pt[:, r * 128:(r + 1) * 128], ovl_sb[:, c * 128:(c + 1) * 128],
                                   c == 0, c == n_ct - 1, [Bpt, B_ovl], [Bip])
                        for r in range(4):
                            for c, (pt, Bpt) in enumerate(pts):
                                mm(dp[:, r:r + 1], pt[:, r * 128:(r + 1) * 128], ones_col[:], c == 0, c == n_ct - 1,
                                   [Bpt, B_onescol], [Bdp])
                        recip_add(rdt[:], dp[:, 0:4], 1e-30, [Bdp], [B_rdt])
                        ts("dve", impw[:], ip[:, 0:128], rdt[:, 0:1], None, ALU.mult, None, [Bip, B_rdt], [B_impw])
                        for r in range(1, 4):
                            stt("dve", impw[:], ip[:, r * 128:(r + 1) * 128], rdt[:, r:r + 1], impw[:], ALU.mult, ALU.add,
                                [Bip, B_rdt, B_impw], [B_impw])
                        tt("dve", impw[:], impw[:], fb_sb[:, 128 - 2 * i: 256 - 2 * i], ALU.add, [B_impw, B_fb], [B_impw])
                        tt("dve", impb[:], impw[:], f0_sb[:], ALU.add, [B_impw, B_f0], [B_imp])
                        P.op("dve", lambda e: e.max(out=m8[:, 0:8], in_=impb[:]), [B_imp], [B_m8])
                        P.op("dve", lambda e: e.match_replace(out=impw[:], in_to_replace=m8[:, 0:8], in_values=impb[:],
                                                              imm_value=-3.0e4), [B_imp, B_m8], [B_impw])
                        P.op("dve", lambda e: e.max(out=m8[:, 8:16], in_=impw[:]), [B_impw], [B_m8])
                        ts("dve", selb[:, g, :], impb[:], m8[:, 15:16], NEG, ALU.is_lt, ALU.mult, [B_imp, B_m8], [B_selb[g]])
                if _en('A'):
                    for g in range(2):
                        rows = slice(64 * g, 64 * g + 64)
                        kts = []
                        for kt in ([i - 1, i] if i >= 1 else [i]):
                            s_ = kt % RING
                            msk = mcur_sb if kt == i else mprev_sb
                            kts.append((akR[:, s_, :], B_akR[s_], avR[:, s_, 64 * g: 64 * g + 128], B_avR[s_],
                                        [(ident_bf[:], bc4(msk[:]), [B_ident, B_mcur, B_mprev])]))
                        run_att(attend(g, aqT, B_aq, tc, kts, None, sink=True, dest=(osb, B_osb)))
                    for r in range(4):
                        cp("act", yT[:, 0, r, tc:tc + 128], osb[:, r * 128:(r + 1) * 128], [B_osb], [B_y[0][r]])
                if _en('B'):
                    for g in range(2):
                        rows = slice(64 * g, 64 * g + 64)
                        kts = []
                        for kt in range(max(0, i - 4), i + 1):
                            s_ = kt % RING
                            masks = []
                            if kt == i:
                                masks = [(ident_bf[:], bc4(mcur_sb[:]), [B_ident, B_mcur])]
                            elif kt == i - 4:
                                masks = [(ident_bf[:], bc4(mprev_sb[:]), [B_ident, B_mprev])]
                            kts.append((kwR[:, s_, :], B_kwR[s_], vwR[:, s_, 64 * g: 64 * g + 128], B_vwR[s_], masks))
                        run_att(attend(g, cqT, B_cq, tc, kts, 2))
                if _en('F'):
                    ap_, Bap = pb[4], B_pb[4]
                    for h in range(4):
                        pp, hh = h // 2, h % 2
                        rows = slice(64 * hh, 64 * hh + 64)
                        mm(pb[4 + hh][:, pp * 128:(pp + 1) * 128], dkT[rows, pp, tc:tc + 128], dqT[rows, pp, tc:tc + 128],
                           True, True, [B_dk, B_dq], [B_pb[4 + hh]])
                    if _rl(1):
                        am, Bam = ptmp()
                        for h in range(4):
                            pp, hh = h // 2, h % 2
                            tt("dve", am[:, h * 128:(h + 1) * 128], pb[4 + hh][:, pp * 128:(pp + 1) * 128],
                               dmask_sb[:, h * 128:(h + 1) * 128], ALU.mult, [B_pb[4 + hh], B_dmask], [Bam])

                    if _rl(2):
                        for pp in range(2):
                            tt("dve", qxi[:, pp, tc:tc + 128], dqT[:, pp, tc:tc + 128], xi_sb[:, pp * 128:(pp + 1) * 128],
                               ALU.mult, [B_dq, B_xi], [B_qxi])

                    if _rl(3):
                        op_, Bop = pb[6], B_pb[6]
                        for h in range(4):
                            pp, hh = h // 2, h % 2
                            rows = slice(64 * hh, 64 * hh + 64)
                            mm(op_[:, h * 128:(h + 1) * 128], dvR[:, ti, h * 128:(h + 1) * 128], am[:, h * 128:(h + 1) * 128],
                               True, False, [B_dvR[ti], Bam], [Bop])
                            mm(op_[:, h * 128:(h + 1) * 128], Rbf[rows, pp, hh * 128:(hh + 1) * 128], qxi[rows, pp, tc:tc + 128],
                               False, True, [B_Rbf, B_qxi], [Bop])

                    if _rl(4):
                        for pp in range(2):
                            tr(pbt[:, 256 + pp * 128: 256 + (pp + 1) * 128], dkT[:, pp, tc:tc + 128], ident_bf[:],
                               [B_dk, B_ident], [B_pbt])
                        tt("dve", kz[:].rearrange("p a b -> p (a b)"), pbt[:, 256:512], zt_sb[:], ALU.mult, [B_pbt, B_zt], [B_kz])

                    if _rl(5):
                        sp2, Bsp2 = pb[5], B_pb[5]
                        for pp in range(2):
                            mm(sp2[:, pp * 256:(pp + 1) * 256], kz[:, pp, :], dvR[:, ti, pp * 256:(pp + 1) * 256], True, True,
                               [B_kz, B_dvR[ti]], [Bsp2])
                        for pp in range(2):
                            stt("dve", Rst[:, pp, :], Rst[:, pp, :], dec_sb[:, pp:pp + 1], sp2[:, pp * 256:(pp + 1) * 256],
                                ALU.mult, ALU.add, [B_R, B_dec, Bsp2], [B_R])

                    if _rl(6):
                        cp("act", Rbf[:].rearrange("p a b -> p (a b)"), Rst[:].rearrange("p a b -> p (a b)"), [B_R], [B_Rbf])

                    if _rl(7):
                        o2, Bo2 = tmp()
                        cp("act", o2[:], op_[:], [Bop], [Bo2])
                        mp, Bmp = pb[2], B_pb[2]
                        mm(mp[:], ones_v[:], o2[:], True, True, [Bo2, B_ov], [Bmp])
                        c2, Bc2 = tmp()
                        tt("dve", c2[:], o2[:], mp[:], ALU.subtract, [Bo2, Bmp], [Bc2])

                    if _rl(8):
                        s2, Bs2 = tmp()
                        tt("pool", s2[:], c2[:], c2[:], ALU.mult, [Bc2], [Bs2])
                        mm(mp[:], ones_v[:], s2[:], True, True, [Bs2, B_ov], [Bmp])
                        r2, Br2 = tmp()
                        rsqrt_eps(r2[:], mp[:], [Bmp], [Br2])
                        tt("pool", c2[:], c2[:], r2[:], ALU.mult, [Bc2, Br2], [Bc2])
                        for h in range(4):
                            stt("dve", yT[:, 3, h, tc:tc + 128], c2[:, h * 128:(h + 1) * 128], vec[:, V_RG + h:V_RG + h + 1],
                                sgd[:, h, tc:tc + 128], ALU.mult, ALU.mult, [Bc2, B_vec, B_sgd], [B_y[3][h]])


                if _en('D'):
                    for g in range(2):
                        tr(pbt[:, g * 128:(g + 1) * 128], selb[:, g, :], ident_bf[:], [B_selb[g], B_ident], [B_pbt])
                        cp("act", selbT[0:64, g, 0, :], pbt[0:64, g * 128:(g + 1) * 128], [B_pbt], [B_selbT[g]])
                        cp("act", selbT[64:128, g, 1, :], pbt[64:128, g * 128:(g + 1) * 128], [B_pbt], [B_selbT[g]])
                    for g in range(2):
                        rows = slice(64 * g, 64 * g + 64)
                        kts = []
                        for kt in range(0, i + 1):
                            hf = 0 if kt < 32 else 1
                            masks = [(E_sb[:, (kt % 32) * 128:(kt % 32 + 1) * 128],
                                      bc4(selbT[:, g, hf, :]), [B_E, B_selbT[g]])]
                            if kt == i:
                                masks.append((ident_bf[:], bc4(mcur_sb[:]), [B_ident, B_mcur]))
                            kts.append((ksT[:, kt * 128:(kt + 1) * 128], B_ks[kt], vsA[:, kt, 64 * g: 64 * g + 128],
                                        B_vs[kt], masks))
                        run_att(attend(g, cqT, B_cq, tc, kts, 1))
                if _en('E'):
                    for br, srcb, Bsrc in ((0, obr[0], B_obr[0]), (1, obr[1], B_obr[1]), (2, obr[2], B_obr[2])):
                        gp, Bgp = (pb[3], B_pb[3]) if br != 1 else (pb[2], B_pb[2])
                        for r in range(4):
                            mm(gp[:, r * 128:(r + 1) * 128], selg_sb[:, (br * 4 + r) * 128:(br * 4 + r + 1) * 128],
                               cgs[:, tc:tc + 128], True, True, [B_selg, B_cgs], [Bgp])
                        if br == 0:
                            tt("dve", osb[:], gp[:], srcb[:], ALU.mult, [Bgp, Bsrc], [B_osb])
                        else:
                            t_, Bt_ = tmp()
                            tt("dve", t_[:], gp[:], srcb[:], ALU.mult, [Bgp, Bsrc], [Bt_])
                            tt("pool", osb[:], osb[:], t_[:], ALU.add, [B_osb, Bt_], [B_osb])
                    for r in range(4):
                        cp("act", yT[:, 2, r, tc:tc + 128], osb[:, r * 128:(r + 1) * 128], [B_osb], [B_y[2][r]])

                if stop == "att":
                    dma("sp", dstv[:, :, t0:t0 + TB], xT[:], B_x, [B_xs[min(l, len(B_xs) - 1)][blk]])
                    continue
            for m0 in range(0, 8, 2):
                for bi in range(4):
                    wbv, Bwb = wload(WB_bf[l][:, bi * 4:(bi + 1) * 4, m0 * 128:(m0 + 2) * 128], 4, 256)
                    gvw, Bgv = wload(WA_bf[l][:, :, OFF_WIN + OFF_GATES + bi * 1024 + m0 * 128:
                                              OFF_WIN + OFF_GATES + bi * 1024 + (m0 + 2) * 128], 8, 256)
                    for m in range(m0, m0 + 2):
                        mo = (m - m0) * 128
                        pg, Bg = pb[0], B_pb[0]
                        for kc in range(8):
                            mm(pg[:, 0:TB], gvw[:, kc, mo:mo + 128], xn[:, kc, :], kc == 0, kc == 7, [Bgv, B_xn[kc]], [Bg])
                        gs, Bgs = tmp()
                        act(gs[:, 0:TB], pg[:, 0:TB], AF.Sigmoid, [Bg, B_vec], [Bgs],
                            bias=vec[:, V_GB + bi * 8 + m: V_GB + bi * 8 + m + 1])
                        pbk, Bbk = pb[1], B_pb[1]
                        for c in range(4):
                            mm(pbk[:, 0:TB], wbv[:, c, mo:mo + 128], yT[:, bi, c, :], c == 0, c == 3,
                               [Bwb, B_y[bi][c]], [Bbk])
                        if bi == 0:
                            tt("dve", macc[:, m - m0, :], pbk[:, 0:TB], gs[:, 0:TB], ALU.mult, [Bbk, Bgs], [B_macc[m - m0]])
                        else:
                            t_, Bt_ = tmp()
                            tt("dve", t_[:, 0:TB], pbk[:, 0:TB], gs[:, 0:TB], ALU.mult, [Bbk, Bgs], [Bt_])
                            if bi < 3:
                                tt("pool", macc[:, m - m0, :], macc[:, m - m0, :], t_[:, 0:TB], ALU.add,
                                   [B_macc[m - m0], Bt_], [B_macc[m - m0]])
                            else:
                                tt("pool", mrg[:, m, :], macc[:, m - m0, :], t_[:, 0:TB], ALU.add,
                                   [B_macc[m - m0], Bt_], [B_mrg[m]])
            for m0 in range(0, 8, 2):
                wo, Bwo = wload(WA_bf[l][:, :, OFF_WO + m0 * 128: OFF_WO + (m0 + 2) * 128], 8, 256)
                for m in range(m0, m0 + 2):
                    po, Bo = mmbank()
                    for kc in range(8):
                        mm(po[:, 0:TB], wo[:, kc, (m - m0) * 128:(m - m0 + 1) * 128], mrg[:, kc, :], kc == 0, kc == 7,
                           [Bwo, B_mrg[kc]], [Bo])
                    tt("dve", xT[:, m, :], xT[:, m, :], po[:, 0:TB], ALU.add, [B_x[m], Bo], [B_x[m]])
            if stop != "mix":
                rmsnorm_block(V_N2)
                ffn_block(l, OFF_F2G, OFF_F2U, 1024)
            dma("sp", dstv[:, :, t0:t0 + TB], xT[:], B_x, [B_xs[min(l, len(B_xs) - 1)][blk]])

    P.emit()
    return nc, P


_CACHE = {}


def run_cores(x, inp, L, T, n_cores, TB=256, stop=None):
    key = (T, L, TB, stop)
    if key not in _CACHE:
        _CACHE[key] = build(T, L, TB, stop)
    nc, P = _CACHE[key]
    w = prep_weights(inp, L)
    cs = make_consts(T)
    in_maps = []
    for c in range(n_cores):
        m = {"xT": np.ascontiguousarray(x[c].T.astype(np.float32))}
        m.update(w)
        for k, v in cs.items():
            m["c_" + k] = np.ascontiguousarray(v)
        in_maps.append(m)
    res = run_bass_kernel_spmd(nc, in_maps, core_ids=list(range(n_cores)))
    return np.stack([np.ascontiguousarray(r["outT"].T) for r in res.results], axis=0)


def kernel(**inputs):
    x = np.asarray(inputs["x"], dtype=np.float32)
    B, T, _ = x.shape
    inp = {k: np.asarray(v, dtype=np.float32) for k, v in inputs.items() if k != "x"}
    out = run_cores(x, inp, L_FULL, T, B)
    return out.astype(np.float32)
```
